# Optimizing a Trainium2 kernel written in Bass

```python
import math
import jax, jax.numpy as jnp
from jax import lax
import numpy as np

D_MODEL = 2048
BATCH = 2
SEQ = 8192
DEPTH = 4

GRID_W = 64
CTX_LEN = 256
EPS = 1e-6

S5_WIDTH = D_MODEL // 4
S5_GROUP = 16
S5_GROUPS = S5_WIDTH // S5_GROUP
S5_STATE = 64

HY_WIDTH = D_MODEL // 4
HY_ORDER = 2
HY_POS_FREQS = 16
HY_POS_DIM = 1 + 2 * HY_POS_FREQS
HY_FILTER_HIDDEN = 64
HY_DECAY_TARGET = 1e-2
HY_SHORT_DECAY_PCT = 0.3
HY_LONG_DECAY_PCT = 1.5

GLA_HEADS = 4
GLA_DK = D_MODEL // 16
GLA_DV = D_MODEL // 8
GLA_KEY = GLA_HEADS * GLA_DK
GLA_VAL = GLA_HEADS * GLA_DV
GLA_GATE_RANK = 16
GLA_GATE_TEMP = 16.0
GLA_CHUNK = 64

MIX_WIDTH = S5_WIDTH + HY_WIDTH + GLA_VAL
N_BRANCH = 3
FF_HIDDEN = 11 * D_MODEL // 4

C_S5 = 0
C_GK = C_S5 + S5_WIDTH
C_GV = C_GK + GLA_KEY
C_GG = C_GV + GLA_VAL
C_GQ = C_GG + 2 * GLA_GATE_RANK
STATE_COLS = C_GQ
C_GR = C_GQ + GLA_KEY
C_HY = C_GR + GLA_VAL
C_MG = C_HY + (HY_ORDER + 1) * HY_WIDTH
IN_WIDTH = C_MG + N_BRANCH * D_MODEL

kernel_name = "hybrid_s5_hyena_gla_dit_block"

F32 = jnp.float32


def rmsnorm(x, g):
    xf = x.astype(F32)
    y = xf * lax.rsqrt(jnp.mean(xf * xf, axis=-1, keepdims=True) + EPS)
    return (y * g.astype(F32)).astype(x.dtype)


def dwconv3(x, w, b):
    xp = jnp.pad(x, ((0, 0), (1, 1), (0, 0)))
    return xp[:, :-2] * w[0] + xp[:, 1:-1] * w[1] + xp[:, 2:] * w[2] + b


def to_col_major(t):
    bsz, n = t.shape[:2]
    rows = n // GRID_W
    return t.reshape(bsz, rows, GRID_W, *t.shape[2:]).swapaxes(1, 2).reshape(bsz, n, *t.shape[2:])


def from_col_major(t):
    bsz, n = t.shape[:2]
    rows = n // GRID_W
    return t.reshape(bsz, GRID_W, rows, *t.shape[2:]).swapaxes(1, 2).reshape(bsz, n, *t.shape[2:])


def flip_seq(t):
    return None if t is None else jnp.flip(t, axis=1)


def _linear_op(e1, e2):
    a1, b1 = e1
    a2, b2 = e2
    return a1 * a2, a2 * b1 + b2


def s5_scan(bu, a_bar, reverse):
    a = jnp.broadcast_to(a_bar, bu.shape)
    _, xs = lax.associative_scan(_linear_op, (a, bu), reverse=reverse, axis=1)
    return xs


def s5_carry(lam_dt, s0, n, reverse):
    steps = (jnp.arange(n, 0, -1) if reverse else jnp.arange(1, n + 1)).astype(F32)
    decay = jnp.exp(lam_dt[None] * steps[:, None, None])
    return decay[None] * s0[:, None]


def s5_drive(u, b_bar):
    ug = u.astype(F32).reshape(u.shape[0], u.shape[1], S5_GROUPS, S5_GROUP).astype(jnp.complex64)
    return jnp.einsum('blgh,gnh->blgn', ug, b_bar)


def s5_readout(c_mat, xs):
    y = jnp.real(jnp.einsum('ghn,blgn->blgh', c_mat, xs))
    return y.reshape(y.shape[0], y.shape[1], S5_WIDTH)


def s5_mixer(ux, uc, p, ctx_out):
    yx, yc = [], []
    for d in range(2):
        rev = d == 1
        lam = lax.complex(p['s5_a_re'][d].astype(F32), p['s5_a_im'][d].astype(F32))
        lam_dt = lam * jnp.exp(p['s5_log_step'][d].astype(F32))[:, None]
        a_bar = jnp.exp(lam_dt)
        b_mat = lax.complex(p['s5_b_re'][d].astype(F32), p['s5_b_im'][d].astype(F32))
        b_bar = ((a_bar - 1.0) / lam)[..., None] * b_mat
        c_mat = lax.complex(p['s5_c_re'][d].astype(F32), p['s5_c_im'][d].astype(F32))
        xs_c = s5_scan(s5_drive(uc, b_bar), a_bar, rev)
        s0 = xs_c[:, 0] if rev else xs_c[:, -1]
        xs_x = s5_scan(s5_drive(ux, b_bar), a_bar, rev) + s5_carry(lam_dt, s0, ux.shape[1], rev)
        yx.append(s5_readout(c_mat, xs_x))
        if ctx_out:
            yc.append(s5_readout(c_mat, xs_c))

    def finish(ys, u):
        y = ys[0] + ys[1] + p['s5_d'].astype(F32) * u.astype(F32)
        y = jax.nn.gelu(y)
        y = y * jax.nn.sigmoid(y @ p['s5_glu_w'].astype(F32) + p['s5_glu_b'].astype(F32))
        return y.astype(u.dtype)

    return finish(yx, ux), (finish(yc, uc) if ctx_out else None)


def hyena_filters(n, p):
    pos = jnp.arange(n, dtype=F32)
    t = pos / max(n - 1, 1)
    freqs = jnp.linspace(1e-4, HY_POS_FREQS - 1, HY_POS_FREQS, dtype=F32)
    ang = (2.0 * math.pi / n) * pos[:, None] * freqs[None]
    feats = jnp.concatenate([t[:, None], jnp.cos(ang), -jnp.sin(ang)], axis=-1)
    h = jnp.sin(p['hy_f_freq1'].astype(F32) * (feats @ p['hy_f_w1'].astype(F32) + p['hy_f_b1'].astype(F32)))
    h = jnp.sin(p['hy_f_freq2'].astype(F32) * (h @ p['hy_f_w2'].astype(F32) + p['hy_f_b2'].astype(F32)))
    h = (h @ p['hy_f_w3'].astype(F32) + p['hy_f_b3'].astype(F32)).reshape(n, HY_ORDER, 2, HY_WIDTH)
    rates = jnp.abs(jnp.linspace(math.log(HY_DECAY_TARGET) / HY_LONG_DECAY_PCT,
                                 math.log(HY_DECAY_TARGET) / HY_SHORT_DECAY_PCT, HY_WIDTH, dtype=F32))
    h = h * jnp.exp(-t[:, None] * rates)[:, None, None, :]
    fwd, bwd = h[:, :, 0], h[:, :, 1]
    filt = jnp.concatenate([fwd, jnp.zeros_like(fwd[:1]), jnp.flip(bwd[1:], axis=0)], axis=0)
    filt = filt / (jnp.sum(jnp.abs(filt), axis=0, keepdims=True) + EPS)
    return jnp.fft.rfft(filt, axis=0)


def hyena_seq(z, p):
    n = z.shape[1]
    zc = dwconv3(z, p['hy_conv_w'], p['hy_conv_b']).astype(F32)
    v, x1, x2 = jnp.split(zc, 3, axis=-1)
    filt_f = hyena_filters(n, p)
    y = v
    for o, gate in enumerate((x1, x2)):
        yf = jnp.fft.rfft(y, n=2 * n, axis=1)
        conv = jnp.fft.irfft(yf * filt_f[None, :, o], n=2 * n, axis=1)[:, :n]
        y = gate * (conv + y * p['hy_bias'][o].astype(F32))
    return y.astype(z.dtype)


def gla_chunked(q, k, v, g, s0):
    bsz, n = k.shape[:2]
    nc = n // GLA_CHUNK
    chunk = lambda t: t.reshape(bsz, nc, GLA_CHUNK, *t.shape[2:])
    k, v, g = chunk(k), chunk(v), chunk(g)
    b = jnp.cumsum(g, axis=2)
    b_last = b[:, :, -1]
    kv = jnp.einsum('bnjhd,bnjhe->bnhde', k * jnp.exp(b_last[:, :, None] - b), v)
    step = lambda s, inp: (jnp.exp(inp[0])[..., None] * s + inp[1], s)
    s_fin, s_prev = lax.scan(step, s0, (jnp.moveaxis(b_last, 1, 0), jnp.moveaxis(kv, 1, 0)))
    if q is None:
        return None, s_fin
    q_in = chunk(q) * jnp.exp(b)
    scores = jnp.einsum('bnihd,bnjhd->bnhij', q_in, k * jnp.exp(-b))
    lower = jnp.tril(jnp.ones((GLA_CHUNK, GLA_CHUNK), dtype=bool))
    scores = jnp.where(lower, scores, 0.0)
    o = (jnp.einsum('bnhij,bnjhe->bnihe', scores, v)
         + jnp.einsum('bnihd,bnhde->bnihe', q_in, jnp.moveaxis(s_prev, 0, 1)))
    return o.reshape(bsz, n, *o.shape[3:]), s_fin


def gla_mixer(px, pc, p, ctx_out):
    heads = lambda t, dh: t.astype(F32).reshape(t.shape[0], t.shape[1], GLA_HEADS, dh)
    scale = GLA_DK ** -0.5

    def gate(lr, d):
        pre = lr[..., d * GLA_GATE_RANK:(d + 1) * GLA_GATE_RANK] @ p['gla_wg'][d] + p['gla_bg'][d]
        return heads(jax.nn.log_sigmoid(pre.astype(F32)) / GLA_GATE_TEMP, GLA_DK)

    qx = heads(to_col_major(px[..., C_GQ:C_GR]), GLA_DK) * scale
    kx = heads(to_col_major(px[..., C_GK:C_GV]), GLA_DK)
    vx = heads(to_col_major(px[..., C_GV:C_GG]), GLA_DV)
    lx = to_col_major(px[..., C_GG:C_GQ])
    qc = heads(pc[..., C_GQ:C_GR], GLA_DK) * scale if ctx_out else None
    kc = heads(pc[..., C_GK:C_GV], GLA_DK)
    vc = heads(pc[..., C_GV:C_GG], GLA_DV)
    lc = pc[..., C_GG:C_GQ]
    zero = jnp.zeros((kc.shape[0], GLA_HEADS, GLA_DK, GLA_DV), F32)
    ox, oc = [], []
    for d in range(2):
        f = flip_seq if d == 1 else (lambda t: t)
        o_c, s_c = gla_chunked(f(qc), f(kc), f(vc), f(gate(lc, d)), zero)
        o_x, _ = gla_chunked(f(qx), f(kx), f(vx), f(gate(lx, d)), s_c)
        ox.append(f(o_x))
        if ctx_out:
            oc.append(f(o_c))

    def finish(o, r):
        o = rmsnorm(o, p['gla_norm_g']).reshape(o.shape[0], o.shape[1], GLA_VAL)
        return (o * jax.nn.silu(r.astype(F32))).astype(r.dtype)

    yx = finish(from_col_major(ox[0] + ox[1]), px[..., C_GR:C_HY])
    yc = finish(oc[0] + oc[1], pc[..., C_GR:C_HY]) if ctx_out else None
    return yx, yc


def merge(ya, yb, yc, gate_logits, p):
    wb = p['w_branch']
    g = jax.nn.sigmoid(gate_logits.astype(F32)).astype(ya.dtype)
    m = (g[..., :D_MODEL] * (ya @ wb[:S5_WIDTH])
         + g[..., D_MODEL:2 * D_MODEL] * (yb @ wb[S5_WIDTH:S5_WIDTH + HY_WIDTH])
         + g[..., 2 * D_MODEL:] * (yc @ wb[S5_WIDTH + HY_WIDTH:]))
    return m @ p['w_out']


def token_mixer(hx, hc, p, ctx_out):
    px = hx @ p['w_in']
    pc = hc @ (p['w_in'] if ctx_out else p['w_in'][:, :STATE_COLS])
    ya_x, ya_c = s5_mixer(px[..., C_S5:C_GK], pc[..., C_S5:C_GK], p, ctx_out)
    yc_x, yc_c = gla_mixer(px, pc, p, ctx_out)
    yb_x = hyena_seq(px[..., C_HY:C_MG], p)
    out_x = merge(ya_x, yb_x, yc_x, px[..., C_MG:], p)
    if not ctx_out:
        return out_x, None
    yb_c = hyena_seq(pc[..., C_HY:C_MG], p)
    out_c = merge(ya_c, yb_c, yc_c, pc[..., C_MG:], p)
    return out_x, out_c


def conv_ffn(h, w_up, conv_w, conv_b, w_down):
    a, b = jnp.split(h @ w_up, 2, axis=-1)
    a = dwconv3(a, conv_w, conv_b)
    return (jax.nn.silu(a) * b) @ w_down


def setup_inputs(seed: int = 0) -> dict:
    key = jax.random.key(seed)
    ks = iter(jax.random.split(key, 64))

    def nrm(shape, scale=1.0):
        return scale * jax.random.normal(next(ks), shape, F32)

    def gain(shape):
        return 1.0 + nrm(shape, 0.02)

    G, N, H = S5_GROUPS, S5_STATE, S5_GROUP
    return {
        "x": nrm((BATCH, SEQ, D_MODEL)),
        "c": nrm((BATCH, D_MODEL)),
        "ctx": nrm((BATCH, CTX_LEN, D_MODEL)),
        "c_ctx": nrm((D_MODEL,)),
        "w_mod": nrm((DEPTH, D_MODEL, 6 * D_MODEL), 0.5 * D_MODEL ** -0.5),
        "b_mod": nrm((DEPTH, 6 * D_MODEL), 0.01),
        "norm1_g": gain((DEPTH, D_MODEL)),
        "norm2_g": gain((DEPTH, D_MODEL)),
        "w_in": nrm((DEPTH, D_MODEL, IN_WIDTH), D_MODEL ** -0.5),
        "s5_a_re": -0.5 + nrm((DEPTH, 2, G, N), 0.01),
        "s5_a_im": math.pi * jnp.arange(N, dtype=F32) + nrm((DEPTH, 2, G, N), 0.01),
        "s5_log_step": jax.random.uniform(next(ks), (DEPTH, 2, G), F32, math.log(1e-3), math.log(1e-1)),
        "s5_b_re": nrm((DEPTH, 2, G, N, H), (2 * H) ** -0.5),
        "s5_b_im": nrm((DEPTH, 2, G, N, H), (2 * H) ** -0.5),
        "s5_c_re": nrm((DEPTH, 2, G, H, N), N ** -0.5),
        "s5_c_im": nrm((DEPTH, 2, G, H, N), N ** -0.5),
        "s5_d": nrm((DEPTH, S5_WIDTH)),
        "s5_glu_w": nrm((DEPTH, S5_WIDTH, S5_WIDTH), S5_WIDTH ** -0.5),
        "s5_glu_b": nrm((DEPTH, S5_WIDTH), 0.01),
        "hy_conv_w": nrm((DEPTH, 3, (HY_ORDER + 1) * HY_WIDTH), 0.5),
        "hy_conv_b": nrm((DEPTH, (HY_ORDER + 1) * HY_WIDTH), 0.01),
        "hy_f_w1": nrm((DEPTH, HY_POS_DIM, HY_FILTER_HIDDEN), HY_POS_DIM ** -0.5),
        "hy_f_b1": nrm((DEPTH, HY_FILTER_HIDDEN), 0.1),
        "hy_f_freq1": 1.0 + nrm((DEPTH, HY_FILTER_HIDDEN), 0.1),
        "hy_f_w2": nrm((DEPTH, HY_FILTER_HIDDEN, HY_FILTER_HIDDEN), HY_FILTER_HIDDEN ** -0.5),
        "hy_f_b2": nrm((DEPTH, HY_FILTER_HIDDEN), 0.1),
        "hy_f_freq2": 1.0 + nrm((DEPTH, HY_FILTER_HIDDEN), 0.1),
        "hy_f_w3": nrm((DEPTH, HY_FILTER_HIDDEN, HY_ORDER * 2 * HY_WIDTH), HY_FILTER_HIDDEN ** -0.5),
        "hy_f_b3": nrm((DEPTH, HY_ORDER * 2 * HY_WIDTH), 0.01),
        "hy_bias": nrm((DEPTH, HY_ORDER, HY_WIDTH)),
        "gla_wg": nrm((DEPTH, 2, GLA_GATE_RANK, GLA_KEY), GLA_GATE_RANK ** -0.5),
        "gla_bg": nrm((DEPTH, 2, GLA_KEY), 0.01),
        "gla_norm_g": gain((DEPTH, GLA_DV)),
        "w_branch": jnp.concatenate([nrm((DEPTH, S5_WIDTH, D_MODEL), S5_WIDTH ** -0.5),
                                     nrm((DEPTH, HY_WIDTH, D_MODEL), HY_WIDTH ** -0.5),
                                     nrm((DEPTH, GLA_VAL, D_MODEL), GLA_VAL ** -0.5)], axis=1),
        "w_out": nrm((DEPTH, D_MODEL, D_MODEL), D_MODEL ** -0.5),
        "ff_w_up": nrm((DEPTH, D_MODEL, 2 * FF_HIDDEN), D_MODEL ** -0.5),
        "ff_conv_w": nrm((DEPTH, 3, FF_HIDDEN), 0.5),
        "ff_conv_b": nrm((DEPTH, FF_HIDDEN), 0.01),
        "ff_w_down": nrm((DEPTH, FF_HIDDEN, D_MODEL), FF_HIDDEN ** -0.5),
        "final_norm_g": gain((D_MODEL,)),
    }


def reference(x, c, ctx, c_ctx, w_mod, b_mod, norm1_g, norm2_g, w_in,
              s5_a_re, s5_a_im, s5_log_step, s5_b_re, s5_b_im, s5_c_re, s5_c_im,
              s5_d, s5_glu_w, s5_glu_b,
              hy_conv_w, hy_conv_b, hy_f_w1, hy_f_b1, hy_f_freq1, hy_f_w2, hy_f_b2,
              hy_f_freq2, hy_f_w3, hy_f_b3, hy_bias,
              gla_wg, gla_bg, gla_norm_g, w_branch, w_out,
              ff_w_up, ff_conv_w, ff_conv_b, ff_w_down, final_norm_g):
    dt = x.dtype
    sc = jax.nn.silu(c.astype(F32)).astype(dt)
    scc = jax.nn.silu(c_ctx.astype(F32)).astype(dt)
    for l in range(DEPTH):
        ctx_out = l < DEPTH - 1
        mod_x = (sc @ w_mod[l] + b_mod[l])[:, None, :]
        mod_c = scc @ w_mod[l] + b_mod[l]
        sh1, s1, g1, sh2, s2, g2 = jnp.split(mod_x, 6, axis=-1)
        csh1, cs1, cg1, csh2, cs2, cg2 = jnp.split(mod_c, 6, axis=-1)
        p = dict(w_in=w_in[l], s5_a_re=s5_a_re[l], s5_a_im=s5_a_im[l], s5_log_step=s5_log_step[l],
                 s5_b_re=s5_b_re[l], s5_b_im=s5_b_im[l], s5_c_re=s5_c_re[l], s5_c_im=s5_c_im[l],
                 s5_d=s5_d[l], s5_glu_w=s5_glu_w[l], s5_glu_b=s5_glu_b[l],
                 hy_conv_w=hy_conv_w[l], hy_conv_b=hy_conv_b[l], hy_f_w1=hy_f_w1[l], hy_f_b1=hy_f_b1[l],
                 hy_f_freq1=hy_f_freq1[l], hy_f_w2=hy_f_w2[l], hy_f_b2=hy_f_b2[l],
                 hy_f_freq2=hy_f_freq2[l], hy_f_w3=hy_f_w3[l], hy_f_b3=hy_f_b3[l], hy_bias=hy_bias[l],
                 gla_wg=gla_wg[l], gla_bg=gla_bg[l], gla_norm_g=gla_norm_g[l],
                 w_branch=w_branch[l], w_out=w_out[l])
        hx = rmsnorm(x, norm1_g[l]) * (1 + s1) + sh1
        hc = rmsnorm(ctx, norm1_g[l]) * (1 + cs1) + csh1
        ox, oc = token_mixer(hx, hc, p, ctx_out)
        x = x + g1 * ox
        hx = rmsnorm(x, norm2_g[l]) * (1 + s2) + sh2
        x = x + g2 * conv_ffn(hx, ff_w_up[l], ff_conv_w[l], ff_conv_b[l], ff_w_down[l])
        if ctx_out:
            ctx = ctx + cg1 * oc
            hc = rmsnorm(ctx, norm2_g[l]) * (1 + cs2) + csh2
            ctx = ctx + cg2 * conv_ffn(hc, ff_w_up[l], ff_conv_w[l], ff_conv_b[l], ff_w_down[l])
    return rmsnorm(x, final_norm_g)
```

```python
import contextlib
import numpy as np
import ml_dtypes
import concourse.bass as bass
import concourse.mybir as mybir
from concourse.bass_utils import run_bass_kernel_spmd

F32 = mybir.dt.float32
BF16 = mybir.dt.bfloat16
AF = mybir.ActivationFunctionType
ALU = mybir.AluOpType

D = 2048
DEPTH = 4
SEQ = 8192
CTX = 256
IN_W = 11296
IN_WP = 11392
FF = 5632
C_MG = IN_W - 3 * D
GATE0 = 5248
PSPLIT = 5632
EPS = 1e-6
ENGS = ("pe", "act", "dve", "pool", "sp")


class Prog:
    def __init__(self, nc, n_dma_slots=6):
        self.nc = nc
        self.ops = {e: [] for e in ENGS}
        self.count = {e: 0 for e in ENGS}
        self.waited = {e: {} for e in ENGS}
        self.last_writer = {}
        self.readers = {}
        self.n_dma_slots = n_dma_slots
        self.dma_uses = {e: [0] * n_dma_slots for e in ENGS}
        self.dma_next = {e: 0 for e in ENGS}

    def _deps(self, reads, writes):
        deps = {}

        def add(tok):
            if tok:
                for s, v in tok.items():
                    if deps.get(s, 0) < v:
                        deps[s] = v

        for r in reads:
            add(self.last_writer.get(r))
        for w in writes:
            add(self.last_writer.get(w))
            add(self.readers.get(w))
        return deps

    def _record(self, tok, reads, writes):
        for w in writes:
            self.last_writer[w] = dict(tok)
            self.readers[w] = {}
        for r in reads:
            d = self.readers.setdefault(r, {})
            for s, v in tok.items():
                if d.get(s, 0) < v:
                    d[s] = v

    def _waits(self, eng, deps):
        out = []
        wd = self.waited[eng]
        for s, v in deps.items():
            if s == ("e", "pe") and eng == "pe":
                continue
            if wd.get(s, 0) < v:
                wd[s] = v
                out.append((s, v))
        return out

    def op(self, eng, fn, reads=(), writes=()):
        deps = self._deps(reads, writes)
        waits = self._waits(eng, deps)
        self.count[eng] += 1
        tok = {("e", eng): self.count[eng]}
        self.ops[eng].append((waits, fn, (("e", eng), 1)))
        self._record(tok, reads, writes)

    def dma(self, eng, out, in_, reads=(), writes=(), **kw):
        deps = self._deps(reads, writes)
        slot = self.dma_next[eng]
        self.dma_next[eng] = (slot + 1) % self.n_dma_slots
        s = ("d", eng, slot)
        prev = self.dma_uses[eng][slot]
        if prev > 0 and deps.get(s, 0) < 16 * prev:
            deps[s] = 16 * prev
        waits = self._waits(eng, deps)
        self.dma_uses[eng][slot] = prev + 1
        tok = {s: 16 * (prev + 1)}
        self.ops[eng].append((waits, lambda e: e.dma_start(out=out, in_=in_, **kw), (s, 16)))
        self._record(tok, reads, writes)

    def barrier(self):
        deps = {}
        for e in ENGS:
            if self.count[e] > 0:
                deps[("e", e)] = self.count[e]
            for slot, u in enumerate(self.dma_uses[e]):
                if u > 0:
                    deps[("d", e, slot)] = 16 * u
        for e in ENGS:
            waits = []
            wd = self.waited[e]
            for s, v in deps.items():
                if wd.get(s, 0) < v:
                    wd[s] = v
                    waits.append((s, v))
            self.ops[e].append((waits, None, None))
        self.last_writer = {}
        self.readers = {}

    def emit(self):
        nc = self.nc
        names = set()
        for e in ENGS:
            for waits, fn, inc in self.ops[e]:
                for s, v in waits:
                    names.add(s)
                if inc:
                    names.add(inc[0])
        names = sorted(names, key=str)
        with contextlib.ExitStack() as st:
            sem = {}
            for n in names:
                sem[n] = st.enter_context(nc.semaphore("s_" + "_".join(str(x) for x in n)))
            block = st.enter_context(nc.Block())

            def run(e):
                def f(eng):
                    for waits, fn, inc in self.ops[e]:
                        for s, v in waits:
                            eng.wait_ge(sem[s], v)
                        if fn is not None:
                            fn(eng).then_inc(sem[inc[0]], inc[1])
                return f

            block.tensor(run("pe"))
            block.scalar(run("act"))
            block.vector(run("dve"))
            block.gpsimd(run("pool"))
            block.sync(run("sp"))


def token_blocks(nx, nctx):
    blks = []
    for t0 in range(0, nx, 512):
        blks.append((t0, min(512, nx - t0), 0, t0 == 0, t0 + 512 >= nx))
    for t0 in range(0, nctx, 512):
        blks.append((nx + t0, min(512, nctx - t0), 1, t0 == 0, t0 + 512 >= nctx))
    return blks


def build(depth=DEPTH, nx=SEQ, nctx=CTX, debug=False, only=None):
    T = nx + nctx
    KC = D // 128
    MC_IN = IN_WP // 128
    MC_UP = 2 * FF // 128
    KC_FF = FF // 128
    blks = token_blocks(nx, nctx)
    nc = bass.Bass("TRN2", target_bir_lowering=False)
    dt_in = lambda name, shape, dt=F32: nc.dram_tensor(name, shape, dt, kind="ExternalInput").ap()
    xT_in = dt_in("xT", [D, T])
    cT = dt_in("cT", [128, KC, 2])
    w_mod = dt_in("w_mod", [depth, 6 * KC, 128, KC, 128])
    b_mod = dt_in("b_mod", [depth, 128, 6 * KC])
    n1g = dt_in("n1g", [depth, 128, KC])
    n2g = dt_in("n2g", [depth, 128, KC])
    fng = dt_in("fng", [128, KC])
    w_in = dt_in("w_in", [depth, MC_IN, 128, KC, 128])
    w_br = dt_in("w_br", [depth, KC, 128, KC, 128])
    w_out = dt_in("w_out", [depth, KC, 128, KC, 128])
    w_up = dt_in("w_up", [depth, MC_UP, 128, KC, 128])
    w_dn = dt_in("w_dn", [depth, KC, 128, KC_FF, 128])
    fcw = dt_in("fcw", [depth, 128, KC_FF, 3])
    fcb = dt_in("fcb", [depth, 128, KC_FF])
    s5_are = dt_in("s5_are", [depth, 128, 2, 16])
    s5_aim = dt_in("s5_aim", [depth, 128, 2, 16])
    s5_lst = dt_in("s5_lst", [depth, 128, 2, 16])
    s5_bre = dt_in("s5_bre", [depth, 2, 4, 128, 512])
    s5_bim = dt_in("s5_bim", [depth, 2, 4, 128, 512])
    s5_cre = dt_in("s5_cre", [depth, 2, 4, 128, 4, 128])
    s5_cim = dt_in("s5_cim", [depth, 2, 4, 128, 4, 128])
    s5_dd = dt_in("s5_dd", [depth, 128, 4])
    s5_gw = dt_in("s5_gw", [depth, 128, 4, 4, 128])
    s5_gb = dt_in("s5_gb", [depth, 128, 4])
    cst128 = dt_in("cst128", [128, 4, 128])
    cstk = dt_in("cstk", [128, 4])
    hy_cw = dt_in("hy_cw", [depth, 128, 12, 3])
    hy_cb = dt_in("hy_cb", [depth, 128, 12])
    hy_w1 = dt_in("hy_w1", [depth, 33, 64])
    hy_w2 = dt_in("hy_w2", [depth, 64, 64])
    hy_w3 = dt_in("hy_w3", [depth, 65, 2, 4, 256])
    hy_fb = dt_in("hy_fb", [depth, 64, 6])
    hy_bias = dt_in("hy_bias", [depth, 128, 2, 4])
    HSEQ = [("x", 0, nx), ("c", nx, nctx)]
    hyc = {}
    for nm, _, n_ in HSEQ:
        N2_ = n_ // 64
        A_ = n_ // 128
        hyc[nm] = dict(N2=N2_, A=A_,
                       feats=dt_in(f"hy_feats_{nm}", [33, 2 * n_]),
                       dec=dt_in(f"hy_dec_{nm}", [4, N2_, 128, 128]),
                       F2f=dt_in(f"hy_F2f_{nm}", [N2_, 2, N2_]),
                       TW=dt_in(f"hy_TW_{nm}", [128, 2, N2_]),
                       F2i=dt_in(f"hy_F2i_{nm}", [N2_, 2, A_]),
                       TWT=dt_in(f"hy_TWT_{nm}", [N2_, 2, 128]),
                       msk=dt_in(f"hy_msk_{nm}", [N2_, 2]),
                       fspec=nc.dram_tensor(f"hy_fspec_{nm}", [2, 4, 128, 2, 128, N2_], F32).ap())
    hy_F1 = dt_in("hy_F1", [128, 3, 128])
    gla_wg = dt_in("gla_wg", [depth, 2, 17, 512])
    gla_ng = dt_in("gla_ng", [depth, 128, 2])
    cst64 = dt_in("cst64", [64, 6, 64])
    identb_in = dt_in("identb", [128, 128], BF16)
    outT = nc.dram_tensor("outT", [D, nx], F32, kind="ExternalOutput").ap()
    if only:
        dbg_out = nc.dram_tensor("dbg_out", [nx // 64, 4, 128], F32, kind="ExternalOutput").ap()
        px_dbg = dt_in("px_dbg", [PSPLIT, T])
        y_out = nc.dram_tensor("y_out", [len(blks), 128, KC, 512], BF16, kind="ExternalOutput").ap()
    if debug:
        y_dbg = dt_in("y_dbg", [len(blks), 128, KC, 512], BF16)
    xT = nc.dram_tensor("xTs", [D, T], F32).ap()
    hT = nc.dram_tensor("hTs", [len(blks), 128, KC, 512], BF16).ap()
    pxA = nc.dram_tensor("pxAs", [PSPLIT, T], F32).ap()
    pxB = nc.dram_tensor("pxBs", [IN_WP - PSPLIT, T], F32).ap()
    yT = nc.dram_tensor("yTs", [len(blks), 128, KC, 512], BF16).ap()
    aT = nc.dram_tensor("aTs", [FF, T], F32).ap()
    bT = nc.dram_tensor("bTs", [FF, T], F32).ap()
    uT = nc.dram_tensor("uTs", [len(blks), 128, KC_FF, 512], BF16).ap()
    wb_in = nc.dram_tensor("wb_in", [MC_IN, 128, KC, 128], BF16).ap()
    wb_br = nc.dram_tensor("wb_br", [KC, 128, KC, 128], BF16).ap()
    wb_out = nc.dram_tensor("wb_out", [KC, 128, KC, 128], BF16).ap()
    wb_up = nc.dram_tensor("wb_up", [MC_UP, 128, KC, 128], BF16).ap()
    wb_dn = nc.dram_tensor("wb_dn", [KC, 128, KC_FF, 128], BF16).ap()
    s5rows = nc.dram_tensor("s5rows", [2, 4, 2048], F32).ap()
    ygT = nc.dram_tensor("ygT", [len(blks), 128, 4, 512], BF16).ap()
    hvT = nc.dram_tensor("hvT", [3, 512, T], F32).ap()
    hc1T = nc.dram_tensor("hc1T", [512, T], F32).ap()
    hy1T = nc.dram_tensor("hy1T", [512, T], F32).ap()

    P = Prog(nc)
    root = contextlib.ExitStack()
    with root:
        uid = [0]

        def SB(st, name, shape, dt):
            uid[0] += 1
            return st.enter_context(nc.sbuf_tensor(f"{name}_{uid[0]}", shape, dt))
        PS = lambda st, name: st.enter_context(nc.psum_tensor(name, [128, 512], F32))
        ones = SB(root, "ones", [128, 128], F32)
        sc = SB(root, "sc", [128, KC, 2], F32)
        modv = SB(root, "modv", [128, 6 * KC, 2], F32)
        A1 = SB(root, "A1", [128, KC, 2], F32)
        A2 = SB(root, "A2", [128, KC, 2], F32)
        bm = SB(root, "bm", [128, 6 * KC], F32)
        g1t = SB(root, "g1t", [128, KC], F32)
        g2t = SB(root, "g2t", [128, KC], F32)
        gft = SB(root, "gft", [128, KC], F32)
        cwt = SB(root, "cwt", [128, KC_FF, 3], F32)
        cbt = SB(root, "cbt", [128, KC_FF], F32)
        ps = [PS(root, f"ps{i}") for i in range(8)]

        P.op("pool", lambda e: e.memset(ones[:], 1.0), writes=["ones"])
        epst = SB(root, "epst", [128, 1], F32)
        P.op("pool", lambda e: e.memset(epst[:], EPS), writes=["epst"])
        P.dma("sp", sc[:], cT, writes=["sc"])
        P.op("act", lambda e: e.activation(out=sc[:], in_=sc[:], func=AF.Silu), reads=["sc"], writes=["sc"])
        P.dma("sp", gft[:], fng, writes=["gft"])
        for r0 in range(0, D, 256):
            P.dma("pool", xT[r0:r0 + 256], xT_in[r0:r0 + 256], writes=["xT"])
        P.barrier()

        def castw(src, dst, n, kc, tag):
            with contextlib.ExitStack() as st:
                f = [SB(st, f"cw_f{i}", [128, kc, 128], F32) for i in range(3)]
                b = [SB(st, f"cw_b{i}", [128, kc, 128], BF16) for i in range(3)]
                for i in range(n):
                    j = i % 3
                    P.dma("sp", f[j][:], src[i], writes=[f"cwf{j}"])
                    eng = ("dve", "pool", "act")[i % 3]
                    if eng == "act":
                        P.op("act", lambda e, j=j: e.activation(out=b[j][:], in_=f[j][:], func=AF.Copy),
                             reads=[f"cwf{j}"], writes=[f"cwb{j}"])
                    else:
                        P.op(eng, lambda e, j=j: e.tensor_copy(out=b[j][:], in_=f[j][:]),
                             reads=[f"cwf{j}"], writes=[f"cwb{j}"])
                    P.dma("pool", dst[i], b[j][:], reads=[f"cwb{j}"], writes=[tag])
                P.barrier()

        def mod_stage(l):
            with contextlib.ExitStack() as st:
                wt = [SB(st, f"md_w{i}", [128, KC, 128], F32) for i in range(3)]
                P.dma("sp", bm[:], b_mod[l], writes=["bm"])
                for mi in range(6 * KC):
                    j = mi % 3
                    P.dma("sp", wt[j][:], w_mod[l, mi], writes=[f"mdw{j}"])
                    pb = ps[mi % 2]
                    for kc in range(KC):
                        P.op("pe", lambda e, j=j, kc=kc, pb=pb: e.matmul(pb[:, 0:2], wt[j][:, kc, :], sc[:, kc, :],
                                                                         start=(kc == 0), stop=(kc == KC - 1)),
                             reads=[f"mdw{j}", "sc"], writes=[f"ps{mi % 2}"])
                    P.op("act", lambda e, mi=mi, pb=pb: e.activation(out=modv[:, mi, :], in_=pb[:, 0:2], func=AF.Identity,
                                                                     bias=bm[:, mi:mi + 1]),
                         reads=[f"ps{mi % 2}", "bm"], writes=["modv"])
                P.dma("sp", g1t[:], n1g[l], writes=["g1t"])
                P.dma("sp", g2t[:], n2g[l], writes=["g2t"])
                P.dma("sp", cwt[:], fcw[l], writes=["cwt"])
                P.dma("sp", cbt[:], fcb[l], writes=["cbt"])
                for col in range(2):
                    P.op("dve", lambda e, col=col: e.scalar_tensor_tensor(
                        out=A1[:, :, col], in0=modv[:, KC:2 * KC, col], scalar=1.0, in1=g1t[:], op0=ALU.add, op1=ALU.mult),
                        reads=["modv", "g1t"], writes=["A1"])
                    P.op("dve", lambda e, col=col: e.scalar_tensor_tensor(
                        out=A2[:, :, col], in0=modv[:, 4 * KC:5 * KC, col], scalar=1.0, in1=g2t[:], op0=ALU.add, op1=ALU.mult),
                        reads=["modv", "g2t"], writes=["A2"])
                P.barrier()

        def norm_stage(Asc, shift_base, final=False):
            with contextlib.ExitStack() as st:
                xb = [SB(st, f"nm_x{i}", [128, KC, 512], F32) for i in range(2)]
                sq = SB(st, "nm_sq", [128, KC, 512], F32)
                rs = SB(st, "nm_rs", [128, 512], F32)
                hb = [SB(st, f"nm_h{i}", [128, KC, 512], F32 if final else BF16) for i in range(2)]
                for bi, (t0, nb, col, _, _) in enumerate(blks):
                    if final and col == 1:
                        continue
                    j = bi % 2
                    P.dma("sp", xb[j][:, :, 0:nb], xT[:, t0:t0 + nb].rearrange("(c p) n -> p c n", p=128),
                          reads=["xT"], writes=[f"nmx{j}"])
                    P.op("act", lambda e, j=j, nb=nb: e.activation(out=sq[:, :, 0:nb], in_=xb[j][:, :, 0:nb], func=AF.Square),
                         reads=[f"nmx{j}"], writes=["nmsq"])
                    for kc in range(KC):
                        P.op("pe", lambda e, kc=kc, nb=nb: e.matmul(ps[0][:, 0:nb], ones[:], sq[:, kc, 0:nb],
                                                                    start=(kc == 0), stop=(kc == KC - 1)),
                             reads=["ones", "nmsq"], writes=["ps0"])
                    P.op("act", lambda e, nb=nb: e.activation(out=rs[:, 0:nb], in_=ps[0][:, 0:nb], func=AF.Sqrt,
                                                              bias=epst[:, 0:1], scale=1.0 / D),
                         reads=["ps0", "epst"], writes=["nmrs"])
                    P.op("dve", lambda e, nb=nb: e.reciprocal(out=rs[:, 0:nb], in_=rs[:, 0:nb]),
                         reads=["nmrs"], writes=["nmrs"])
                    for kc in range(KC):
                        eng = "dve" if kc % 2 == 0 else "pool"
                        P.op(eng, lambda e, j=j, kc=kc, nb=nb: e.tensor_tensor(out=xb[j][:, kc, 0:nb], in0=xb[j][:, kc, 0:nb],
                                                                              in1=rs[:, 0:nb], op=ALU.mult),
                             reads=[f"nmx{j}", "nmrs"], writes=[f"nmx{j}"])
                    for kc in range(KC):
                        if final:
                            P.op("act", lambda e, j=j, kc=kc, nb=nb: e.activation(
                                out=hb[j][:, kc, 0:nb], in_=xb[j][:, kc, 0:nb], func=AF.Copy, scale=gft[:, kc:kc + 1]),
                                reads=[f"nmx{j}", "gft"], writes=[f"nmh{j}"])
                        else:
                            P.op("act", lambda e, j=j, kc=kc, nb=nb, col=col: e.activation(
                                out=hb[j][:, kc, 0:nb], in_=xb[j][:, kc, 0:nb], func=AF.Identity,
                                scale=Asc[:, kc, col:col + 1], bias=modv[:, shift_base + kc, col:col + 1]),
                                reads=[f"nmx{j}", "A1", "A2", "modv"], writes=[f"nmh{j}"])
                    if final:
                        P.dma("pool", outT[:, t0:t0 + nb].rearrange("(c p) n -> p c n", p=128), hb[j][:, :, 0:nb],
                              reads=[f"nmh{j}"], writes=["outT"])
                    else:
                        P.dma("pool", hT[bi], hb[j][:], reads=[f"nmh{j}"], writes=["hT"])
                P.barrier()

        def gemm_stage(src_blocks, src_key, kc_n, wsrc, mc_n, epilogue, extra=None, G=4, nib=2, pair=False):
            with contextlib.ExitStack() as st:
                if pair:
                    G, nib = 2, 4
                ib = [SB(st, f"gm_i{i}", [128, kc_n, 512], BF16) for i in range(nib)]
                wt = [SB(st, f"gm_w{i}", [128, G, kc_n, 128], BF16) for i in range(2)]
                ctx = extra(st) if extra else None
                wi = 0
                step = 2 if pair else 1
                for b0 in range(0, len(blks), step):
                    grp = list(range(b0, min(b0 + step, len(blks))))
                    for bi in grp:
                        P.dma("sp", ib[bi % nib][:], src_blocks[bi], reads=[src_key], writes=[f"gmi{bi % nib}"])
                    for m0 in range(0, mc_n, G):
                        g = min(G, mc_n - m0)
                        wj = wi % 2
                        if pair:
                            pgrp = 4 * (wi % 2)
                        else:
                            pgrp = 4 * (wi % 2) if G == 4 else 2 * (wi % 4) if G == 2 else (wi % 8)
                        wi += 1
                        P.dma("sp", wt[wj][:, 0:g], wsrc[m0:m0 + g].rearrange("g p k m -> p g k m"),
                              reads=["wsrc"], writes=[f"gmw{wj}"])
                        for k_, bi in enumerate(grp):
                            nb = blks[bi][1]
                            j = bi % nib
                            pbase = pgrp + 2 * k_ if pair else pgrp
                            for gi in range(g):
                                pi = pbase + gi
                                for kc in range(kc_n):
                                    P.op("pe", lambda e, wj=wj, j=j, kc=kc, pi=pi, nb=nb, gi=gi: e.matmul(
                                        ps[pi][:, 0:nb], wt[wj][:, gi, kc, :], ib[j][:, kc, 0:nb], start=(kc == 0), stop=(kc == kc_n - 1)),
                                        reads=[f"gmw{wj}", f"gmi{j}"], writes=[f"psg{pbase}"])
                        for k_, bi in enumerate(grp):
                            pbase = pgrp + 2 * k_ if pair else pgrp
                            if pair:
                                epilogue(ctx, bi, blks[bi], m0, g, pbase, 2 * k_, (pgrp // 4) % 2)
                            else:
                                epilogue(ctx, bi, blks[bi], m0, g, pbase)
                P.barrier()

        PI = float(np.pi)

        def s5_stage(l):
            with contextlib.ExitStack() as st:
                c128 = SB(st, "s5_c128", [128, 4, 128], F32)
                ck = SB(st, "s5_ck", [128, 4], F32)
                negpi = SB(st, "s5_negpi", [128, 1], F32)
                are = SB(st, "s5_are", [128, 2, 16], F32)
                aim = SB(st, "s5_aim", [128, 2, 16], F32)
                stp = SB(st, "s5_stp", [128, 2, 16], F32)
                lre = SB(st, "s5_lre", [128, 2, 16], F32)
                lim = SB(st, "s5_lim", [128, 2, 16], F32)
                abr = SB(st, "s5_abr", [128, 2, 16], F32)
                abi = SB(st, "s5_abi", [128, 2, 16], F32)
                cfr = SB(st, "s5_cfr", [128, 2, 16], F32)
                cfi = SB(st, "s5_cfi", [128, 2, 16], F32)
                w1 = SB(st, "s5_w1", [128, 2, 16], F32)
                w2 = SB(st, "s5_w2", [128, 2, 16], F32)
                w3 = SB(st, "s5_w3", [128, 2, 16], F32)
                ddt = SB(st, "s5_dd", [128, 4], F32)
                uT = SB(st, "s5_u", [128, T], F32)
                acc = SB(st, "s5_acc", [128, T], F32)
                rows = SB(st, "s5_rows", [128, 4, 512], F32)
                Pm = SB(st, "s5_Pm", [128, 2, 512], F32)
                Pp = SB(st, "s5_Pp", [128, 2, 4, 128], F32)
                Bb = SB(st, "s5_Bb", [128, 2, 512], F32)
                Braw = SB(st, "s5_Braw", [128, 2, 512], F32)
                Cb = SB(st, "s5_Cb", [128, 2, 4, 128], F32)
                ph = SB(st, "s5_ph", [128, 512], F32)
                ph2 = SB(st, "s5_ph2", [128, 512], F32)
                mg = SB(st, "s5_mg", [128, 512], F32)
                tt = [SB(st, f"s5_t{i}", [128, 512], F32) for i in range(4)]
                Wt = [SB(st, f"s5_W{i}", [128, 2, 512], F32) for i in range(2)]
                tmp = SB(st, "s5_tmp", [128, 2, 4, 128], F32)
                Xt = SB(st, "s5_X", [128, 2, 4, 128], F32)
                xx = [SB(st, f"s5_x{i}", [128, 4, 128], F32) for i in range(4)]
                aS = SB(st, "s5_aS", [128, 2, 4], F32)
                sw = [SB(st, f"s5_sw{i}", [128, 4], F32) for i in range(4)]
                P.dma("sp", c128[:], cst128, writes=["c128"])
                P.dma("sp", ck[:], cstk, writes=["ck"])
                P.op("pool", lambda e: e.memset(negpi[:], -PI), writes=["negpi"])
                P.dma("sp", are[:], s5_are[l], writes=["are"])
                P.dma("sp", aim[:], s5_aim[l], writes=["aim"])
                P.dma("sp", stp[:], s5_lst[l], writes=["stp"])
                P.dma("sp", ddt[:], s5_dd[l], writes=["ddt"])
                V = lambda fn, r, w: P.op("dve", fn, reads=r, writes=w)

                I32 = mybir.dt.int32
                sc_i = SB(st, "s5_sci", [128, 512], I32)
                sc_r = SB(st, "s5_scr", [128, 512], F32)
                sc_f = SB(st, "s5_scf", [128, 512], F32)
                sc_m = SB(st, "s5_scm", [128, 512], F32)

                def sincos(theta_ap, sin_out, cos_out, scratch, rkeys, skey, wkeys):
                    shp = list(theta_ap.shape)
                    n = 1
                    for v in shp[1:]:
                        n *= v
                    flat = lambda ap: ap if len(shp) == 2 else ap.rearrange("p a b -> p (a b)")
                    th = flat(theta_ap)
                    ri_, rr_, rf_, rm_ = sc_i[:, 0:n], sc_r[:, 0:n], sc_f[:, 0:n], sc_m[:, 0:n]
                    for out_ap, off, wk in ((sin_out, 0.0, wkeys[0]), (cos_out, 0.25, wkeys[1])):
                        if out_ap is None:
                            continue
                        V(lambda e, off=off: e.tensor_scalar(out=rr_, in0=th, scalar1=1.0 / (2.0 * PI), scalar2=off, op0=ALU.mult, op1=ALU.add),
                          rkeys, ["scr"])
                        V(lambda e: e.tensor_copy(out=ri_, in_=rr_), ["scr"], ["sci"])
                        V(lambda e: e.tensor_copy(out=rf_, in_=ri_), ["sci"], ["scf"])
                        V(lambda e: e.tensor_tensor(out=rr_, in0=rr_, in1=rf_, op=ALU.subtract), ["scr", "scf"], ["scr"])
                        V(lambda e: e.tensor_scalar(out=rm_, in0=rr_, scalar1=0.5, scalar2=None, op0=ALU.is_gt), ["scr"], ["scm"])
                        V(lambda e: e.tensor_tensor(out=rr_, in0=rr_, in1=rm_, op=ALU.subtract), ["scr", "scm"], ["scr"])
                        V(lambda e: e.tensor_scalar(out=rm_, in0=rr_, scalar1=-0.5, scalar2=None, op0=ALU.is_lt), ["scr"], ["scm"])
                        V(lambda e: e.tensor_tensor(out=rr_, in0=rr_, in1=rm_, op=ALU.add), ["scr", "scm"], ["scr"])
                        P.op("act", lambda e, out_ap=out_ap: e.activation(out=flat(out_ap), in_=rr_, func=AF.Sin, scale=2.0 * PI),
                             reads=["scr"], writes=[wk])

                P.op("act", lambda e: e.activation(out=stp[:], in_=stp[:], func=AF.Exp), reads=["stp"], writes=["stp"])
                V(lambda e: e.tensor_tensor(out=lre[:], in0=are[:], in1=stp[:], op=ALU.mult), ["are", "stp"], ["lre"])
                V(lambda e: e.tensor_tensor(out=lim[:], in0=aim[:], in1=stp[:], op=ALU.mult), ["aim", "stp"], ["lim"])
                P.op("act", lambda e: e.activation(out=w1[:], in_=lre[:], func=AF.Exp), reads=["lre"], writes=["w1"])
                sincos(lim[:], w2[:], w3[:], cfr[:], ["lim"], "cfr", ["w2", "w3"])
                V(lambda e: e.tensor_tensor(out=abr[:], in0=w1[:], in1=w3[:], op=ALU.mult), ["w1", "w3"], ["abr"])
                V(lambda e: e.tensor_tensor(out=abi[:], in0=w1[:], in1=w2[:], op=ALU.mult), ["w1", "w2"], ["abi"])
                V(lambda e: e.tensor_scalar(out=w1[:], in0=abr[:], scalar1=-1.0, scalar2=None, op0=ALU.add), ["abr"], ["w1"])
                V(lambda e: e.tensor_tensor(out=w2[:], in0=w1[:], in1=are[:], op=ALU.mult), ["w1", "are"], ["w2"])
                V(lambda e: e.tensor_tensor(out=w3[:], in0=abi[:], in1=aim[:], op=ALU.mult), ["abi", "aim"], ["w3"])
                V(lambda e: e.tensor_tensor(out=cfr[:], in0=w2[:], in1=w3[:], op=ALU.add), ["w2", "w3"], ["cfr"])
                V(lambda e: e.tensor_tensor(out=w2[:], in0=abi[:], in1=are[:], op=ALU.mult), ["abi", "are"], ["w2"])
                V(lambda e: e.tensor_tensor(out=w3[:], in0=w1[:], in1=aim[:], op=ALU.mult), ["w1", "aim"], ["w3"])
                V(lambda e: e.tensor_tensor(out=cfi[:], in0=w2[:], in1=w3[:], op=ALU.subtract), ["w2", "w3"], ["cfi"])
                V(lambda e: e.tensor_tensor(out=w2[:], in0=are[:], in1=are[:], op=ALU.mult), ["are"], ["w2"])
                V(lambda e: e.tensor_tensor(out=w3[:], in0=aim[:], in1=aim[:], op=ALU.mult), ["aim"], ["w3"])
                V(lambda e: e.tensor_tensor(out=w2[:], in0=w2[:], in1=w3[:], op=ALU.add), ["w2", "w3"], ["w2"])
                V(lambda e: e.reciprocal(out=w2[:], in_=w2[:]), ["w2"], ["w2"])
                V(lambda e: e.tensor_tensor(out=cfr[:], in0=cfr[:], in1=w2[:], op=ALU.mult), ["cfr", "w2"], ["cfr"])
                V(lambda e: e.tensor_tensor(out=cfi[:], in0=cfi[:], in1=w2[:], op=ALU.mult), ["cfi", "w2"], ["cfi"])
                for d in range(2):
                    for ri, (tl, key) in enumerate(((lre, "lre"), (lim, "lim"), (cfr, "cfr"), (cfi, "cfi"))):
                        P.dma("pool", s5rows[d, ri].rearrange("(s p) -> p s", p=128), tl[:, d, :], reads=[key], writes=["s5rows"],
                              allow_slow_non_contiguous=True)

                for bk in range(4):
                    for t0 in range(0, T, 2112):
                        n = min(2112, T - t0)
                        P.dma("sp", uT[:, t0:t0 + n], pxA[bk * 128:(bk + 1) * 128, t0:t0 + n], reads=["pxT"], writes=["s5u"])
                    for d in range(2):
                        for ri in range(4):
                            P.dma("sp", rows[:, ri, :], s5rows[d, ri:ri + 1, bk * 512:(bk + 1) * 512].broadcast_to([128, 512]),
                                  reads=["s5rows"], writes=["rows"])
                        P.dma("sp", Braw[:, 0, :], s5_bre[l, d, bk], writes=["Braw"])
                        P.dma("sp", Braw[:, 1, :], s5_bim[l, d, bk], writes=["Braw"])
                        P.dma("sp", Cb[:, 0], s5_cre[l, d, bk], writes=["Cb"])
                        P.dma("sp", Cb[:, 1], s5_cim[l, d, bk], writes=["Cb"])
                        V(lambda e: e.tensor_scalar(out=Cb[:, 1], in0=Cb[:, 1], scalar1=-1.0, scalar2=None, op0=ALU.mult), ["Cb"], ["Cb"])
                        V(lambda e: e.tensor_tensor(out=tt[0][:], in0=Braw[:, 0, :], in1=rows[:, 2, :], op=ALU.mult), ["Braw", "rows"], ["t0"])
                        V(lambda e: e.tensor_tensor(out=tt[1][:], in0=Braw[:, 1, :], in1=rows[:, 3, :], op=ALU.mult), ["Braw", "rows"], ["t1"])
                        V(lambda e: e.tensor_tensor(out=Bb[:, 0, :], in0=tt[0][:], in1=tt[1][:], op=ALU.subtract), ["t0", "t1"], ["Bb"])
                        V(lambda e: e.tensor_tensor(out=tt[0][:], in0=Braw[:, 0, :], in1=rows[:, 3, :], op=ALU.mult), ["Braw", "rows"], ["t0"])
                        V(lambda e: e.tensor_tensor(out=tt[1][:], in0=Braw[:, 1, :], in1=rows[:, 2, :], op=ALU.mult), ["Braw", "rows"], ["t1"])
                        V(lambda e: e.tensor_tensor(out=Bb[:, 1, :], in0=tt[0][:], in1=tt[1][:], op=ALU.add), ["t0", "t1"], ["Bb"])
                        V(lambda e, d=d: e.tensor_scalar(out=ph[:], in0=rows[:, 1, :], scalar1=ck[:, d:d + 1], scalar2=None, op0=ALU.mult),
                          ["rows", "ck"], ["ph"])
                        P.op("act", lambda e, d=d: e.activation(out=mg[:], in_=rows[:, 0, :], func=AF.Exp, scale=ck[:, 2 + d:3 + d]),
                             reads=["rows", "ck"], writes=["mg"])
                        sincos(ph[:], tt[0][:], tt[1][:], ph2[:], ["ph"], "ph2", ["t0", "t1"])
                        V(lambda e: e.tensor_tensor(out=Pm[:, 0, :], in0=mg[:], in1=tt[1][:], op=ALU.mult), ["mg", "t1"], ["Pm"])
                        V(lambda e: e.scalar_tensor_tensor(out=Pm[:, 1, :], in0=mg[:], scalar=-1.0, in1=tt[0][:], op0=ALU.mult, op1=ALU.mult),
                          ["mg", "t0"], ["Pm"])
                        for s_ in range(4):
                            sl = bk * 4 + s_
                            V(lambda e, d=d, sl=sl, s_=s_: e.tensor_scalar(out=ph[:, s_ * 128:(s_ + 1) * 128], in0=c128[:, 2 + d, :],
                                                                         scalar1=lim[:, d, sl:sl + 1], scalar2=None, op0=ALU.mult),
                              ["c128", "lim"], ["ph"])
                            P.op("act", lambda e, d=d, sl=sl, s_=s_: e.activation(out=mg[:, s_ * 128:(s_ + 1) * 128], in_=c128[:, 2 + d, :],
                                                                                  func=AF.Exp, scale=lre[:, d, sl:sl + 1]),
                                 reads=["c128", "lre"], writes=["mg"])
                        sincos(ph[:], tt[0][:], tt[1][:], ph2[:], ["ph"], "ph2", ["t0", "t1"])
                        V(lambda e: e.tensor_tensor(out=Pp[:, 0].rearrange("p s t -> p (s t)"), in0=mg[:], in1=tt[1][:], op=ALU.mult),
                          ["mg", "t1"], ["Pp"])
                        V(lambda e: e.tensor_tensor(out=Pp[:, 1].rearrange("p s t -> p (s t)"), in0=mg[:], in1=tt[0][:], op=ALU.mult),
                          ["mg", "t0"], ["Pp"])
                        P.op("pool", lambda e: e.memset(aS[:], 0.0), writes=["aS"])
                        tri = c128[:, d, :]
                        nq = nctx // 128
                        chunks = [nx + 128 * q for q in range(nq)] + [128 * q for q in range(nx // 128)]
                        if d == 1:
                            chunks = [nx + 128 * q for q in range(nq)][::-1] + [128 * q for q in range(nx // 128)][::-1]
                        last = 127 if d == 0 else 0
                        for it, t0 in enumerate(chunks):
                            b = it % 2
                            pz = 2 + 2 * b
                            for ri in range(2):
                                P.op("pe", lambda e, ri=ri, t0=t0: e.matmul(ps[ri][:, 0:512], uT[:, t0:t0 + 128], Bb[:, ri, :], start=True, stop=True),
                                     reads=["s5u", "Bb"], writes=[f"s5p{ri}"])
                            V(lambda e: e.tensor_tensor(out=tt[0][:], in0=Pm[:, 0, :], in1=ps[0][:, 0:512], op=ALU.mult), ["Pm", "s5p0"], ["t0"])
                            V(lambda e: e.tensor_tensor(out=tt[1][:], in0=Pm[:, 1, :], in1=ps[1][:, 0:512], op=ALU.mult), ["Pm", "s5p1"], ["t1"])
                            V(lambda e: e.tensor_tensor(out=tt[2][:], in0=Pm[:, 0, :], in1=ps[1][:, 0:512], op=ALU.mult), ["Pm", "s5p1"], ["t2"])
                            V(lambda e: e.tensor_tensor(out=tt[3][:], in0=Pm[:, 1, :], in1=ps[0][:, 0:512], op=ALU.mult), ["Pm", "s5p0"], ["t3"])
                            P.op("pool", lambda e, b=b: e.tensor_tensor(out=Wt[b][:, 0, :], in0=tt[0][:], in1=tt[1][:], op=ALU.subtract),
                                 reads=["t0", "t1"], writes=[f"W{b}"])
                            P.op("pool", lambda e, b=b: e.tensor_tensor(out=Wt[b][:, 1, :], in0=tt[2][:], in1=tt[3][:], op=ALU.add),
                                 reads=["t2", "t3"], writes=[f"W{b}"])
                            for ri in range(2):
                                for s_ in range(4):
                                    P.op("pe", lambda e, b=b, ri=ri, s_=s_, pz=pz, tri=tri: e.matmul(
                                        ps[pz + ri][:, s_ * 128:(s_ + 1) * 128], Wt[b][:, ri, s_ * 128:(s_ + 1) * 128], tri, start=True, stop=True),
                                        reads=[f"W{b}", "c128"], writes=[f"s5p{pz + ri}"])
                            for ri in range(2):
                                V(lambda e, ri=ri, pz=pz: e.tensor_tensor(out=tmp[:, ri], in0=ps[pz + ri][:, 0:512].rearrange("p (s t) -> p s t", s=4),
                                                                        in1=aS[:, ri, :].unsqueeze(2).to_broadcast([128, 4, 128]), op=ALU.add),
                                  [f"s5p{pz + ri}", "aS"], ["tmp"])
                            V(lambda e: e.tensor_tensor(out=xx[0][:], in0=Pp[:, 0], in1=tmp[:, 0], op=ALU.mult), ["Pp", "tmp"], ["x0"])
                            V(lambda e: e.tensor_tensor(out=xx[1][:], in0=Pp[:, 1], in1=tmp[:, 1], op=ALU.mult), ["Pp", "tmp"], ["x1"])
                            P.op("pool", lambda e: e.tensor_tensor(out=xx[2][:], in0=Pp[:, 0], in1=tmp[:, 1], op=ALU.mult), reads=["Pp", "tmp"], writes=["x2"])
                            P.op("pool", lambda e: e.tensor_tensor(out=xx[3][:], in0=Pp[:, 1], in1=tmp[:, 0], op=ALU.mult), reads=["Pp", "tmp"], writes=["x3"])
                            V(lambda e: e.tensor_tensor(out=Xt[:, 0], in0=xx[0][:], in1=xx[1][:], op=ALU.subtract), ["x0", "x1"], ["X"])
                            P.op("pool", lambda e: e.tensor_tensor(out=Xt[:, 1], in0=xx[2][:], in1=xx[3][:], op=ALU.add), reads=["x2", "x3"], writes=["X"])
                            ar_, ai_ = abr[:, d, bk * 4:bk * 4 + 4], abi[:, d, bk * 4:bk * 4 + 4]
                            xr_, xi_ = Xt[:, 0, :, last], Xt[:, 1, :, last]
                            V(lambda e, ar_=ar_, xr_=xr_: e.tensor_tensor(out=sw[0][:], in0=ar_, in1=xr_, op=ALU.mult), ["abr", "X"], ["sw0"])
                            V(lambda e, ai_=ai_, xi_=xi_: e.tensor_tensor(out=sw[1][:], in0=ai_, in1=xi_, op=ALU.mult), ["abi", "X"], ["sw1"])
                            V(lambda e, ar_=ar_, xi_=xi_: e.tensor_tensor(out=sw[2][:], in0=ar_, in1=xi_, op=ALU.mult), ["abr", "X"], ["sw2"])
                            V(lambda e, ai_=ai_, xr_=xr_: e.tensor_tensor(out=sw[3][:], in0=ai_, in1=xr_, op=ALU.mult), ["abi", "X"], ["sw3"])
                            V(lambda e: e.tensor_tensor(out=aS[:, 0, :], in0=sw[0][:], in1=sw[1][:], op=ALU.subtract), ["sw0", "sw1"], ["aS"])
                            V(lambda e: e.tensor_tensor(out=aS[:, 1, :], in0=sw[2][:], in1=sw[3][:], op=ALU.add), ["sw2", "sw3"], ["aS"])
                            for ri in range(2):
                                for s_ in range(4):
                                    P.op("pe", lambda e, ri=ri, s_=s_: e.matmul(ps[6][:, 0:128], Cb[:, ri, s_, :], Xt[:, ri, s_, :],
                                                                                start=(ri == 0 and s_ == 0), stop=(ri == 1 and s_ == 3)),
                                         reads=["Cb", "X"], writes=["s5p6"])
                            if d == 0:
                                V(lambda e, t0=t0: e.tensor_copy(out=acc[:, t0:t0 + 128], in_=ps[6][:, 0:128]), ["s5p6"], ["s5acc"])
                            else:
                                V(lambda e, t0=t0: e.tensor_tensor(out=acc[:, t0:t0 + 128], in0=acc[:, t0:t0 + 128], in1=ps[6][:, 0:128], op=ALU.add),
                                  ["s5p6", "s5acc"], ["s5acc"])
                    for bi, (t0, nb, col, _, _) in enumerate(blks):
                        yv = acc[:, t0:t0 + nb]
                        V(lambda e, yv=yv, t0=t0, nb=nb, bk=bk: e.scalar_tensor_tensor(out=yv, in0=uT[:, t0:t0 + nb], scalar=ddt[:, bk:bk + 1], in1=yv,
                                                                                     op0=ALU.mult, op1=ALU.add), ["s5u", "ddt", "s5acc"], ["s5acc"])
                        P.op("pool", lambda e, yv=yv, nb=nb: e.tensor_tensor(out=tt[0][:, 0:nb], in0=yv, in1=yv, op=ALU.mult), reads=["s5acc"], writes=["t0"])
                        V(lambda e, nb=nb: e.tensor_scalar(out=tt[0][:, 0:nb], in0=tt[0][:, 0:nb], scalar1=0.044715, scalar2=1.0, op0=ALU.mult, op1=ALU.add),
                          ["t0"], ["t0"])
                        P.op("pool", lambda e, yv=yv, nb=nb: e.tensor_tensor(out=tt[0][:, 0:nb], in0=tt[0][:, 0:nb], in1=yv, op=ALU.mult),
                             reads=["t0", "s5acc"], writes=["t0"])
                        P.op("act", lambda e, nb=nb: e.activation(out=tt[0][:, 0:nb], in_=tt[0][:, 0:nb], func=AF.Sigmoid, scale=1.5957691216057308),
                             reads=["t0"], writes=["t0"])
                        ob = Wt[bi % 2][:, 0, :].bitcast(BF16)
                        V(lambda e, yv=yv, nb=nb, ob=ob: e.tensor_tensor(out=ob[:, 0:nb], in0=tt[0][:, 0:nb], in1=yv, op=ALU.mult),
                          ["t0", "s5acc"], [f"W{bi % 2}"])
                        P.dma("pool", ygT[bi, :, bk, 0:nb], ob[:, 0:nb], reads=[f"W{bi % 2}"], writes=["ygT"])
                P.barrier()
            with contextlib.ExitStack() as st:
                gwf = SB(st, "s5_gwf", [128, 4, 4, 128], F32)
                gwb = SB(st, "s5_gwb", [128, 4, 4, 128], BF16)
                gbt = SB(st, "s5_gbt", [128, 4], F32)
                yb_ = [SB(st, f"s5_yb{i}", [128, 4, 512], BF16) for i in range(2)]
                sg = [SB(st, f"s5_sg{i}", [128, 512], F32) for i in range(2)]
                ot = [SB(st, f"s5_ot{i}", [128, 512], BF16) for i in range(2)]
                P.dma("sp", gwf[:], s5_gw[l], writes=["gwf"])
                P.dma("sp", gbt[:], s5_gb[l], writes=["gbt"])
                P.op("dve", lambda e: e.tensor_copy(out=gwb[:], in_=gwf[:]), reads=["gwf"], writes=["gwb"])
                it = 0
                for bi, (t0, nb, col, _, _) in enumerate(blks):
                    j = bi % 2
                    P.dma("sp", yb_[j][:], ygT[bi], reads=["ygT"], writes=[f"yb{j}"])
                    for m in range(4):
                        k_ = it % 2
                        it += 1
                        for kc in range(4):
                            P.op("pe", lambda e, j=j, m=m, kc=kc, k_=k_, nb=nb: e.matmul(ps[k_][:, 0:nb], gwb[:, m, kc, :], yb_[j][:, kc, 0:nb],
                                                                                      start=(kc == 0), stop=(kc == 3)),
                                 reads=["gwb", f"yb{j}"], writes=[f"glu{k_}"])
                        P.op("act", lambda e, k_=k_, m=m, nb=nb: e.activation(out=sg[k_][:, 0:nb], in_=ps[k_][:, 0:nb], func=AF.Sigmoid,
                                                                             bias=gbt[:, m:m + 1]),
                             reads=[f"glu{k_}", "gbt"], writes=[f"sg{k_}"])
                        P.op("dve", lambda e, k_=k_, j=j, m=m, nb=nb: e.tensor_tensor(out=ot[k_][:, 0:nb], in0=yb_[j][:, m, 0:nb], in1=sg[k_][:, 0:nb],
                                                                                     op=ALU.mult),
                             reads=[f"yb{j}", f"sg{k_}"], writes=[f"ot{k_}"])
                        P.dma("pool", yT[bi, :, m, 0:nb], ot[k_][:, 0:nb], reads=[f"ot{k_}"], writes=["yT"])
                P.barrier()


        def hyena_stage(l, with_ctx):
            seqs = [q for q in HSEQ if (q[0] == "x" or with_ctx)]
            import os
            if os.environ.get("HY_REV"):
                seqs = seqs[::-1]
            def h0_pass(nm, toff, n_):
                hc = hyc[nm]
                N2, A = hc["N2"], hc["A"]
                NT = 2 * n_
                with contextlib.ExitStack() as st:
                    I32 = mybir.dt.int32
                    w1t = SB(st, "hy_w1", [33, 64], F32)
                    w2t = SB(st, "hy_w2", [64, 64], F32)
                    w3t = SB(st, "hy_w3", [65, 2, 4, 256], F32)
                    fbt = SB(st, "hy_fbt", [64, 6], F32)
                    fb1 = SB(st, "hy_fb1", [64, 2], F32)
                    h2a = SB(st, "hy_h2a", [65, NT], BF16)
                    w3b = SB(st, "hy_w3b", [65, 2, 4, 256], BF16)
                    ft = [SB(st, f"hy_ft{i}", [33, 512], F32) for i in range(2)]
                    th = SB(st, "hy_th", [64, 512], F32)
                    h1 = SB(st, "hy_h1", [64, 512], F32)
                    sci = SB(st, "hy_sci", [128, 512], I32)
                    scr = SB(st, "hy_scr", [128, 512], F32)
                    scf = SB(st, "hy_scf", [128, 512], F32)
                    scm = SB(st, "hy_scm", [128, 512], F32)
                    mskt = SB(st, "hy_msk", [N2, 2], F32)
                    filt = SB(st, "hy_filt", [N2, 128, 128], F32)
                    dct = [SB(st, f"hy_dct{i}", [N2, 16, 128], F32) for i in range(2)]
                    red = SB(st, "hy_red", [N2, 128], F32)
                    inv = SB(st, "hy_inv", [N2, 128], F32)
                    F2f = SB(st, "hy_F2f", [N2, 2, N2], F32)
                    TW = SB(st, "hy_TW", [128, 2, N2], F32)
                    F1 = SB(st, "hy_F1t", [128, 3, 128], F32)
                    CS = 16
                    Y1t = SB(st, "hy_Y1t", [128, CS, 2, N2], F32)
                    xs = [SB(st, f"hy_xs{i}", [128, 2, 512], F32) for i in range(2)]
                    tq = [SB(st, f"hy_tq{i}", [128, 512], F32) for i in range(4)]
                    V = lambda fn, r, w: P.op("dve", fn, reads=r, writes=w)

                    def sin_red(theta_ap, np_, ncol, out_ap, rkeys, wkey):
                        rr_, ri_, rf_, rm_ = scr[0:np_, 0:ncol], sci[0:np_, 0:ncol], scf[0:np_, 0:ncol], scm[0:np_, 0:ncol]
                        V(lambda e: e.tensor_scalar(out=rr_, in0=theta_ap, scalar1=1.0 / (2.0 * PI), scalar2=None, op0=ALU.mult), rkeys, ["hscr"])
                        V(lambda e: e.tensor_copy(out=ri_, in_=rr_), ["hscr"], ["hsci"])
                        V(lambda e: e.tensor_copy(out=rf_, in_=ri_), ["hsci"], ["hscf"])
                        V(lambda e: e.tensor_tensor(out=rr_, in0=rr_, in1=rf_, op=ALU.subtract), ["hscr", "hscf"], ["hscr"])
                        V(lambda e: e.tensor_scalar(out=rm_, in0=rr_, scalar1=0.5, scalar2=None, op0=ALU.is_gt), ["hscr"], ["hscm"])
                        V(lambda e: e.tensor_tensor(out=rr_, in0=rr_, in1=rm_, op=ALU.subtract), ["hscr", "hscm"], ["hscr"])
                        V(lambda e: e.tensor_scalar(out=rm_, in0=rr_, scalar1=-0.5, scalar2=None, op0=ALU.is_lt), ["hscr"], ["hscm"])
                        V(lambda e: e.tensor_tensor(out=rr_, in0=rr_, in1=rm_, op=ALU.add), ["hscr", "hscm"], ["hscr"])
                        P.op("act", lambda e: e.activation(out=out_ap, in_=rr_, func=AF.Sin, scale=2.0 * PI), reads=["hscr"], writes=[wkey])

                    P.dma("sp", w1t[:], hy_w1[l], writes=["hw1"])
                    P.dma("sp", w2t[:], hy_w2[l], writes=["hw2"])
                    P.dma("sp", w3t[:], hy_w3[l], writes=["hw3f"])
                    P.op("pool", lambda e: e.tensor_copy(out=w3b[:], in_=w3t[:]), reads=["hw3f"], writes=["hw3"])
                    P.dma("sp", fbt[:], hy_fb[l], writes=["hfb"])
                    P.dma("sp", mskt[:], hc["msk"], writes=["hmsk"])
                    P.dma("sp", F2f[:], hc["F2f"], writes=["hF2f"])
                    P.dma("sp", TW[:], hc["TW"], writes=["hTW"])
                    P.dma("sp", F1[:], hy_F1, writes=["hF1"])
                    V(lambda e: e.tensor_tensor(out=fb1[:, 0:1], in0=fbt[:, 0:1], in1=fbt[:, 1:2], op=ALU.mult), ["hfb"], ["hfb1"])
                    V(lambda e: e.tensor_tensor(out=fb1[:, 1:2], in0=fbt[:, 2:3], in1=fbt[:, 3:4], op=ALU.mult), ["hfb"], ["hfb1"])
                    P.op("pool", lambda e: e.memset(h2a[:], 1.0), writes=["hh2a"])
                    P.barrier()
                    for ci, c0 in enumerate(range(0, NT, 512)):
                        j = ci % 2
                        P.dma("sp", ft[j][:], hc["feats"][:, c0:c0 + 512], writes=[f"hft{j}"])
                        P.op("pe", lambda e, j=j: e.matmul(ps[0][0:64, 0:512], w1t[:], ft[j][:], start=True, stop=True),
                             reads=["hw1", f"hft{j}"], writes=["hp0"])
                        V(lambda e: e.tensor_scalar(out=th[:], in0=ps[0][0:64, 0:512], scalar1=fbt[:, 0:1], scalar2=fb1[:, 0:1], op0=ALU.mult, op1=ALU.add),
                          ["hp0", "hfb", "hfb1"], ["hth"])
                        sin_red(th[:], 64, 512, h1[:], ["hth"], "hh1")
                        P.op("pe", lambda e: e.matmul(ps[1][0:64, 0:512], w2t[:], h1[:], start=True, stop=True), reads=["hw2", "hh1"], writes=["hp1"])
                        V(lambda e: e.tensor_scalar(out=th[:], in0=ps[1][0:64, 0:512], scalar1=fbt[:, 2:3], scalar2=fb1[:, 1:2], op0=ALU.mult, op1=ALU.add),
                          ["hp1", "hfb", "hfb1"], ["hth"])
                        sin_red(th[:], 64, 512, h2a[0:64, c0:c0 + 512], ["hth"], "hh2a")
                    if only == "hy" and nm == "x":
                        V(lambda e: e.tensor_copy(out=h1[:, 0:128], in_=h2a[0:64, 0:128]), ["hh2a"], ["hh1"])
                        V(lambda e: e.tensor_copy(out=h1[:, 128:256], in_=h2a[0:64, NT - 128:NT]), ["hh2a"], ["hh1"])
                        P.dma("pool", dbg_out[:, 2, :], h1[:, 0:128], reads=["hh1"], writes=["dbg"])
                        P.dma("pool", dbg_out[:, 0, :], h1[:, 128:256], reads=["hh1"], writes=["dbg"])
                    P.barrier()
                    h2v = h2a[:].rearrange("k (a b) -> k a b", b=128)
                    for o in range(2):
                        for cb in range(4):
                            for b_ in range(128):
                                pb = 2 + (b_ % 2)
                                P.op("pe", lambda e, b_=b_, pb=pb, o=o, cb=cb: e.matmul(ps[pb][0:N2, 0:256], h2v[:, :, b_], w3b[:, o, cb, :], start=True, stop=True),
                                     reads=["hh2a", "hw3"], writes=[f"hp{pb}"])
                                V(lambda e, b_=b_, pb=pb: e.tensor_scalar(out=filt[:, :, b_], in0=ps[pb][0:N2, 0:128], scalar1=mskt[:, 0:1], scalar2=None, op0=ALU.mult),
                                  [f"hp{pb}", "hmsk"], ["hfilt"])
                                V(lambda e, b_=b_, pb=pb: e.scalar_tensor_tensor(out=filt[:, :, b_], in0=ps[pb][0:N2, 128:256], scalar=mskt[:, 1:2], in1=filt[:, :, b_],
                                                                                 op0=ALU.mult, op1=ALU.add), [f"hp{pb}", "hmsk", "hfilt"], ["hfilt"])
                            P.barrier()
                            for pc_ in range(8):
                                j = pc_ % 2
                                P.dma("sp", dct[j][:], hc["dec"][cb, :, pc_ * 16:(pc_ + 1) * 16, :], writes=[f"hdct{j}"])
                                eng = "dve" if pc_ % 2 == 0 else "pool"
                                P.op(eng, lambda e, j=j, pc_=pc_: e.tensor_tensor(out=filt[:, pc_ * 16:(pc_ + 1) * 16, :], in0=filt[:, pc_ * 16:(pc_ + 1) * 16, :],
                                                                                 in1=dct[j][:], op=ALU.mult), reads=["hfilt", f"hdct{j}"], writes=["hfilt"])
                            P.barrier()
                            for c_ in range(128):
                                P.op("act", lambda e, c_=c_: e.activation(out=scr[0:N2, 0:128], in_=filt[:, c_, :], func=AF.Abs, accum_out=red[:, c_:c_ + 1]),
                                     reads=["hfilt"], writes=["hred", "hscr"])
                            P.op("act", lambda e: e.activation(out=scr[0:N2, 0:128], in_=filt[:, 0, :], func=AF.Abs), reads=["hfilt", "hred"], writes=["hred", "hscr"])
                            P.op("pe", lambda e: e.matmul(ps[4][0:N2, 0:128], ones[0:N2, 0:N2], red[:], start=True, stop=True), reads=["ones", "hred"], writes=["hp4"])
                            V(lambda e: e.tensor_scalar(out=inv[:], in0=ps[4][0:N2, 0:128], scalar1=EPS, scalar2=None, op0=ALU.add), ["hp4"], ["hinv"])
                            if only == "hy" and nm == "x" and o == 0 and cb == 0:
                                P.dma("pool", dbg_out[:, 3, :], filt[:, 5, :], reads=["hfilt"], writes=["dbg"])
                            V(lambda e: e.reciprocal(out=inv[:], in_=inv[:]), ["hinv"], ["hinv"])
                            for pc_ in range(8):
                                cs_ = slice(pc_ * 16, (pc_ + 1) * 16)
                                eng = "dve" if pc_ % 2 == 0 else "pool"
                                P.op(eng, lambda e, cs_=cs_: e.tensor_tensor(out=filt[:, cs_, :], in0=filt[:, cs_, :],
                                                                          in1=inv[:, cs_].unsqueeze(2).to_broadcast([N2, 16, 128]), op=ALU.mult),
                                     reads=["hfilt", "hinv"], writes=["hfilt"])
                            P.barrier()
                            if only == "hy" and nm == "x" and o == 0 and cb == 0:
                                P.dma("pool", dbg_out[:, 1, :], filt[:, 5, :], reads=["hfilt"], writes=["dbg"])
                            fwd_transform(filt, N2, N2, hc, F2f, TW, F1, Y1t, CS, xs, tq,
                                          lambda c0, ncg, xre, xim, o=o, cb=cb, hc=hc: store_spec(hc, o, cb, c0, ncg, xre, xim), V)
                    P.barrier()

            for nm_, toff_, n__ in seqs:
                h0_pass(nm_, toff_, n__)

            with contextlib.ExitStack() as st:
                cwt_ = SB(st, "hy_cw", [128, 12, 3], F32)
                cbt_ = SB(st, "hy_cb", [128, 12], F32)
                aw = [SB(st, f"hy_a{i}", [128, 514], F32) for i in range(2)]
                ow = [SB(st, f"hy_o{i}", [128, 512], F32) for i in range(2)]
                P.dma("sp", cwt_[:], hy_cw[l], writes=["hcw"])
                P.dma("sp", cbt_[:], hy_cb[l], writes=["hcb"])
                it = 0
                for bi, (t0, nb, col, le, re) in enumerate(blks):
                    for jc in range(12):
                        j = it % 2
                        it += 1
                        r0 = 3616 + jc * 128
                        lo = t0 if le else t0 - 1
                        hi = t0 + nb if re else t0 + nb + 1
                        if le or re:
                            P.op("pool", lambda e, j=j: e.memset(aw[j][:], 0.0), writes=[f"hya{j}"])
                        P.dma("sp", aw[j][:, (lo - (t0 - 1)):(hi - (t0 - 1))], pxA[r0:r0 + 128, lo:hi], reads=["pxT"], writes=[f"hya{j}"])
                        P.op("dve", lambda e, j=j, jc=jc, nb=nb: e.tensor_scalar(out=ow[j][:, 0:nb], in0=aw[j][:, 0:nb], scalar1=cwt_[:, jc, 0:1], scalar2=cbt_[:, jc:jc + 1],
                                                                               op0=ALU.mult, op1=ALU.add), reads=[f"hya{j}", "hcw", "hcb"], writes=[f"hyo{j}"])
                        P.op("dve", lambda e, j=j, jc=jc, nb=nb: e.scalar_tensor_tensor(out=ow[j][:, 0:nb], in0=aw[j][:, 1:nb + 1], scalar=cwt_[:, jc, 1:2], in1=ow[j][:, 0:nb],
                                                                                      op0=ALU.mult, op1=ALU.add), reads=[f"hya{j}", "hcw", f"hyo{j}"], writes=[f"hyo{j}"])
                        P.op("dve", lambda e, j=j, jc=jc, nb=nb: e.scalar_tensor_tensor(out=ow[j][:, 0:nb], in0=aw[j][:, 2:nb + 2], scalar=cwt_[:, jc, 2:3], in1=ow[j][:, 0:nb],
                                                                                      op0=ALU.mult, op1=ALU.add), reads=[f"hya{j}", "hcw", f"hyo{j}"], writes=[f"hyo{j}"])
                        P.dma("pool", hvT[jc // 4, (jc % 4) * 128:(jc % 4 + 1) * 128, t0:t0 + nb], ow[j][:, 0:nb], reads=[f"hyo{j}"], writes=["hvT"])
                P.barrier()

            for o in range(2):
                src = hvT[0] if o == 0 else hy1T
                for nm, toff, n_ in seqs:
                    conv_transform(src, nm, toff, n_, o)
                hy_gate(l, o, src, with_ctx)

        def hy_gate(l, o, src, with_ctx):
            if True:
                with contextlib.ExitStack() as st:
                    bt_ = SB(st, "hy_bias", [128, 2, 4], F32)
                    cv = [SB(st, f"hy_cv{i}", [128, 512], F32) for i in range(2)]
                    yv = [SB(st, f"hy_yv{i}", [128, 512], F32) for i in range(2)]
                    gv = [SB(st, f"hy_gv{i}", [128, 512], F32) for i in range(2)]
                    ob = [SB(st, f"hy_ob{i}", [128, 512], BF16) for i in range(2)]
                    P.dma("sp", bt_[:], hy_bias[l], writes=["hbias"])
                    it = 0
                    for bi, (t0, nb, col, le, re) in enumerate(blks):
                        if col == 1 and not with_ctx:
                            continue
                        for cb in range(4):
                            j = it % 2
                            it += 1
                            rs_ = slice(cb * 128, (cb + 1) * 128)
                            P.dma("sp", cv[j][:, 0:nb], hc1T[rs_, t0:t0 + nb], reads=["hc1T"], writes=[f"hcv{j}"])
                            P.dma("sp", yv[j][:, 0:nb], src[rs_, t0:t0 + nb], reads=["hvT", "hy1T"], writes=[f"hyv{j}"])
                            P.dma("sp", gv[j][:, 0:nb], hvT[1 + o, rs_, t0:t0 + nb], reads=["hvT"], writes=[f"hgv{j}"])
                            P.op("dve", lambda e, j=j, nb=nb, o=o, cb=cb: e.scalar_tensor_tensor(out=cv[j][:, 0:nb], in0=yv[j][:, 0:nb], scalar=bt_[:, o, cb:cb + 1],
                                                                                               in1=cv[j][:, 0:nb], op0=ALU.mult, op1=ALU.add),
                                 reads=[f"hyv{j}", "hbias", f"hcv{j}"], writes=[f"hcv{j}"])
                            if o == 0:
                                P.op("pool", lambda e, j=j, nb=nb: e.tensor_tensor(out=cv[j][:, 0:nb], in0=cv[j][:, 0:nb], in1=gv[j][:, 0:nb], op=ALU.mult),
                                     reads=[f"hcv{j}", f"hgv{j}"], writes=[f"hcv{j}"])
                                P.dma("pool", hy1T[rs_, t0:t0 + nb], cv[j][:, 0:nb], reads=[f"hcv{j}"], writes=["hy1T"])
                            else:
                                P.op("pool", lambda e, j=j, nb=nb: e.tensor_tensor(out=ob[j][:, 0:nb], in0=cv[j][:, 0:nb], in1=gv[j][:, 0:nb], op=ALU.mult),
                                     reads=[f"hcv{j}", f"hgv{j}"], writes=[f"hob{j}"])
                                P.dma("pool", yT[bi, :, 4 + cb, 0:nb], ob[j][:, 0:nb], reads=[f"hob{j}"], writes=["yT"])
                    P.barrier()

        def store_spec(hc, o, cb, c0, ncg, xre, xim):
            N2 = hc["N2"]
            P.dma("pool", hc["fspec"][o, cb, :, 0, c0:c0 + ncg, :], xre.rearrange("p (c k) -> p c k", k=N2), reads=["hxs"], writes=["fspec"])
            P.dma("pool", hc["fspec"][o, cb, :, 1, c0:c0 + ncg, :], xim.rearrange("p (c k) -> p c k", k=N2), reads=["hxs"], writes=["fspec"])

        def fwd_transform(Yin, K_rows, N2, hc, F2f, TW, F1, Y1t, CS, xs, tq, consume, V):
            cpb = max(1, min(CS, 512 // (2 * N2)))
            cgB = max(1, min(CS, 512 // N2))
            xi = [0]
            for cs0 in range(0, 128, CS):
                for g0 in range(0, CS, cpb):
                    pb = 5 + ((cs0 // CS * (CS // cpb) + g0 // cpb) % 2)
                    for ci in range(cpb):
                        c = cs0 + g0 + ci
                        P.op("pe", lambda e, c=c, ci=ci, pb=pb: e.matmul(ps[pb][:, ci * 2 * N2:(ci + 1) * 2 * N2], Yin[0:K_rows, c, :],
                                                                          F2f[0:K_rows].rearrange("k r n -> k (r n)"), start=True, stop=True),
                             reads=["hfilt", "hF2f"], writes=[f"hp{pb}"])
                    pv = ps[pb][:, 0:cpb * 2 * N2].rearrange("p (c r k) -> p c r k", c=cpb, r=2)
                    twr = TW[:, 0, :].unsqueeze(1).to_broadcast([128, cpb, N2])
                    twi = TW[:, 1, :].unsqueeze(1).to_broadcast([128, cpb, N2])
                    t0v, t1v, t2v, t3v = [tq[i][:, 0:cpb * N2].rearrange("p (c k) -> p c k", k=N2) for i in range(4)]
                    V(lambda e, pv=pv, twr=twr, t0v=t0v: e.tensor_tensor(out=t0v, in0=pv[:, :, 0, :], in1=twr, op=ALU.mult), [f"hp{pb}", "hTW"], ["htq0"])
                    V(lambda e, pv=pv, twi=twi, t1v=t1v: e.tensor_tensor(out=t1v, in0=pv[:, :, 1, :], in1=twi, op=ALU.mult), [f"hp{pb}", "hTW"], ["htq1"])
                    V(lambda e, pv=pv, twi=twi, t2v=t2v: e.tensor_tensor(out=t2v, in0=pv[:, :, 0, :], in1=twi, op=ALU.mult), [f"hp{pb}", "hTW"], ["htq2"])
                    V(lambda e, pv=pv, twr=twr, t3v=t3v: e.tensor_tensor(out=t3v, in0=pv[:, :, 1, :], in1=twr, op=ALU.mult), [f"hp{pb}", "hTW"], ["htq3"])
                    P.op("pool", lambda e, g0=g0, t0v=t0v, t1v=t1v: e.tensor_tensor(out=Y1t[:, g0:g0 + cpb, 0, :], in0=t0v, in1=t1v, op=ALU.subtract),
                         reads=["htq0", "htq1"], writes=["hY1t"])
                    P.op("pool", lambda e, g0=g0, t2v=t2v, t3v=t3v: e.tensor_tensor(out=Y1t[:, g0:g0 + cpb, 1, :], in0=t2v, in1=t3v, op=ALU.add),
                         reads=["htq2", "htq3"], writes=["hY1t"])
                for g0 in range(0, CS, cgB):
                    j = xi[0] % 2
                    xi[0] += 1
                    yre = Y1t[:, g0:g0 + cgB, 0, :]
                    yim = Y1t[:, g0:g0 + cgB, 1, :]
                    ncol = cgB * N2
                    P.op("pe", lambda e, yre=yre, ncol=ncol: e.matmul(ps[7][:, 0:ncol], F1[:, 0, :], yre, start=True, stop=False), reads=["hF1", "hY1t"], writes=["hp7"])
                    P.op("pe", lambda e, yim=yim, ncol=ncol: e.matmul(ps[7][:, 0:ncol], F1[:, 2, :], yim, start=False, stop=True), reads=["hF1", "hY1t"], writes=["hp7"])
                    V(lambda e, j=j, ncol=ncol: e.tensor_copy(out=xs[j][:, 0, 0:ncol], in_=ps[7][:, 0:ncol]), ["hp7"], ["hxs"])
                    P.op("pe", lambda e, yim=yim, ncol=ncol: e.matmul(ps[7][:, 0:ncol], F1[:, 0, :], yim, start=True, stop=False), reads=["hF1", "hY1t"], writes=["hp7"])
                    P.op("pe", lambda e, yre=yre, ncol=ncol: e.matmul(ps[7][:, 0:ncol], F1[:, 1, :], yre, start=False, stop=True), reads=["hF1", "hY1t"], writes=["hp7"])
                    V(lambda e, j=j, ncol=ncol: e.tensor_copy(out=xs[j][:, 1, 0:ncol], in_=ps[7][:, 0:ncol]), ["hp7"], ["hxs"])
                    consume(cs0 + g0, cgB, xs[j][:, 0, 0:ncol], xs[j][:, 1, 0:ncol])

        def conv_transform(src, nm, toff, n_, o):
            hc = hyc[nm]
            N2, A = hc["N2"], hc["A"]
            with contextlib.ExitStack() as st:
                F2f = SB(st, "hc_F2f", [N2, 2, N2], F32)
                TW = SB(st, "hc_TW", [128, 2, N2], F32)
                F1 = SB(st, "hc_F1", [128, 3, 128], F32)
                F2i = SB(st, "hc_F2i", [N2, 2, A], F32)
                TWT = SB(st, "hc_TWT", [N2, 2, 128], F32)
                CS = 16
                Yin = SB(st, "hc_Yin", [A, 128, 128], F32)
                Y1t = SB(st, "hc_Y1t", [128, CS, 2, N2], F32)
                xs = [SB(st, f"hc_xs{i}", [128, 2, 512], F32) for i in range(2)]
                tq = [SB(st, f"hc_tq{i}", [128, 512], F32) for i in range(4)]
                hf = [SB(st, f"hc_hf{i}", [128, 2, 512], F32) for i in range(2)]
                Zt = SB(st, "hc_Zt", [128, 2, CS, N2], F32)
                Gt = SB(st, "hc_Gt", [N2, CS, 2, 128], F32)
                ot = [SB(st, f"hc_ot{i}", [A, 512], F32) for i in range(2)]
                V = lambda fn, r, w: P.op("dve", fn, reads=r, writes=w)
                P.dma("sp", F2f[:], hc["F2f"], writes=["hF2f"])
                P.dma("sp", TW[:], hc["TW"], writes=["hTW"])
                P.dma("sp", F1[:], hy_F1, writes=["hF1"])
                P.dma("sp", F2i[:], hc["F2i"], writes=["hF2i"])
                P.dma("sp", TWT[:], hc["TWT"], writes=["hTWT"])
                cgB = max(1, min(CS, 512 // N2))
                cpA = max(1, min(CS, 512 // 256))
                state = {"hi": 0, "oi": 0}
                F1a = SB(st, "hc_F1a", [128, 2, 128], F32)
                F1m = SB(st, "hc_F1m", [128, 2, 128], F32)
                V(lambda e: e.tensor_copy(out=F1a[:, 0, :], in_=F1[:, 0, :]), ["hF1"], ["hF1m"])
                V(lambda e: e.tensor_copy(out=F1a[:, 1, :], in_=F1[:, 2, :]), ["hF1"], ["hF1m"])
                V(lambda e: e.tensor_copy(out=F1m[:, 0, :], in_=F1[:, 1, :]), ["hF1"], ["hF1m"])
                V(lambda e: e.tensor_copy(out=F1m[:, 1, :], in_=F1[:, 0, :]), ["hF1"], ["hF1m"])
                for cb in range(4):
                    for c4 in range(0, 128, 32):
                        P.dma("sp", Yin[:, c4:c4 + 32, :],
                              src[cb * 128 + c4:cb * 128 + c4 + 32, toff:toff + n_].rearrange("c (a b) -> a c b", b=128),
                              reads=["hvT", "hy1T"], writes=["hfilt"])

                    def consume(c0, ncg, xre, xim, cb=cb):
                        j = state["hi"] % 2
                        state["hi"] += 1
                        ncol = ncg * N2
                        P.dma("sp", hf[j][:, 0, 0:ncol].rearrange("p (c k) -> p c k", k=N2), hc["fspec"][o, cb, :, 0, c0:c0 + ncg, :], reads=["fspec"], writes=[f"hhf{j}"])
                        P.dma("sp", hf[j][:, 1, 0:ncol].rearrange("p (c k) -> p c k", k=N2), hc["fspec"][o, cb, :, 1, c0:c0 + ncg, :], reads=["fspec"], writes=[f"hhf{j}"])
                        cl = c0 % CS
                        V(lambda e, j=j, ncol=ncol, xre=xre: e.tensor_tensor(out=tq[0][:, 0:ncol], in0=xre, in1=hf[j][:, 0, 0:ncol], op=ALU.mult), ["hxs", f"hhf{j}"], ["htq0"])
                        V(lambda e, j=j, ncol=ncol, xim=xim: e.tensor_tensor(out=tq[1][:, 0:ncol], in0=xim, in1=hf[j][:, 1, 0:ncol], op=ALU.mult), ["hxs", f"hhf{j}"], ["htq1"])
                        V(lambda e, j=j, ncol=ncol, xre=xre: e.tensor_tensor(out=tq[2][:, 0:ncol], in0=xre, in1=hf[j][:, 1, 0:ncol], op=ALU.mult), ["hxs", f"hhf{j}"], ["htq2"])
                        V(lambda e, j=j, ncol=ncol, xim=xim: e.tensor_tensor(out=tq[3][:, 0:ncol], in0=xim, in1=hf[j][:, 0, 0:ncol], op=ALU.mult), ["hxs", f"hhf{j}"], ["htq3"])
                        P.op("pool", lambda e, cl=cl, ncg=ncg, ncol=ncol: e.tensor_tensor(out=Zt[:, 0, cl:cl + ncg, :], in0=tq[0][:, 0:ncol].rearrange("p (c k) -> p c k", k=N2),
                                                                                       in1=tq[1][:, 0:ncol].rearrange("p (c k) -> p c k", k=N2), op=ALU.subtract),
                             reads=["htq0", "htq1"], writes=["hZt"])
                        P.op("pool", lambda e, cl=cl, ncg=ncg, ncol=ncol: e.tensor_tensor(out=Zt[:, 1, cl:cl + ncg, :], in0=tq[2][:, 0:ncol].rearrange("p (c k) -> p c k", k=N2),
                                                                                       in1=tq[3][:, 0:ncol].rearrange("p (c k) -> p c k", k=N2), op=ALU.add),
                             reads=["htq2", "htq3"], writes=["hZt"])
                        if cl + ncg < CS:
                            return
                        cs0 = c0 + ncg - CS
                        for g0 in range(0, CS, cpA):
                            pb = 3 + (g0 // cpA) % 2
                            for ci in range(cpA):
                                cc = g0 + ci
                                P.op("pe", lambda e, cc=cc, ci=ci, pb=pb: e.matmul(ps[pb][0:N2, ci * 256:(ci + 1) * 256], Zt[:, 0, cc, :],
                                                                                   F1a[:].rearrange("k r n -> k (r n)"), start=True, stop=False),
                                     reads=["hZt", "hF1m"], writes=[f"hq{pb}"])
                                P.op("pe", lambda e, cc=cc, ci=ci, pb=pb: e.matmul(ps[pb][0:N2, ci * 256:(ci + 1) * 256], Zt[:, 1, cc, :],
                                                                                   F1m[:].rearrange("k r n -> k (r n)"), start=False, stop=True),
                                     reads=["hZt", "hF1m"], writes=[f"hq{pb}"])
                            pv = ps[pb][0:N2, 0:cpA * 256].rearrange("p (c r b) -> p c r b", c=cpA, r=2)
                            twr = TWT[:, 0, :].unsqueeze(1).to_broadcast([N2, cpA, 128])
                            twi = TWT[:, 1, :].unsqueeze(1).to_broadcast([N2, cpA, 128])
                            u0, u1, u2, u3 = [tq[i][0:N2, 0:cpA * 128].rearrange("p (c b) -> p c b", b=128) for i in range(4)]
                            V(lambda e, pv=pv, twr=twr, u0=u0: e.tensor_tensor(out=u0, in0=pv[:, :, 0, :], in1=twr, op=ALU.mult), [f"hq{pb}", "hTWT"], ["htq0"])
                            V(lambda e, pv=pv, twi=twi, u1=u1: e.tensor_tensor(out=u1, in0=pv[:, :, 1, :], in1=twi, op=ALU.mult), [f"hq{pb}", "hTWT"], ["htq1"])
                            V(lambda e, pv=pv, twi=twi, u2=u2: e.tensor_tensor(out=u2, in0=pv[:, :, 0, :], in1=twi, op=ALU.mult), [f"hq{pb}", "hTWT"], ["htq2"])
                            V(lambda e, pv=pv, twr=twr, u3=u3: e.tensor_tensor(out=u3, in0=pv[:, :, 1, :], in1=twr, op=ALU.mult), [f"hq{pb}", "hTWT"], ["htq3"])
                            P.op("pool", lambda e, g0=g0, u0=u0, u1=u1: e.tensor_tensor(out=Gt[:, g0:g0 + cpA, 0, :], in0=u0, in1=u1, op=ALU.subtract),
                                 reads=["htq0", "htq1"], writes=["hGt"])
                            P.op("pool", lambda e, g0=g0, u2=u2, u3=u3: e.tensor_tensor(out=Gt[:, g0:g0 + cpA, 1, :], in0=u2, in1=u3, op=ALU.add),
                                 reads=["htq2", "htq3"], writes=["hGt"])
                        for g0 in range(0, CS, 4):
                            k_ = state["oi"] % 2
                            state["oi"] += 1
                            P.op("pe", lambda e, g0=g0, k_=k_: e.matmul(ps[1 + k_][0:A, 0:512], F2i[:, 0, :], Gt[:, g0:g0 + 4, 0, :], start=True, stop=False),
                                 reads=["hF2i", "hGt"], writes=[f"hq{1 + k_}"])
                            P.op("pe", lambda e, g0=g0, k_=k_: e.matmul(ps[1 + k_][0:A, 0:512], F2i[:, 1, :], Gt[:, g0:g0 + 4, 1, :], start=False, stop=True),
                                 reads=["hF2i", "hGt"], writes=[f"hq{1 + k_}"])
                            V(lambda e, k_=k_: e.tensor_copy(out=ot[k_][:], in_=ps[1 + k_][0:A, 0:512]), [f"hq{1 + k_}"], [f"hot{k_}"])
                            r0 = cb * 128 + cs0 + g0
                            P.dma("pool", hc1T[r0:r0 + 4, toff:toff + n_].rearrange("c (a b) -> a c b", b=128),
                                  ot[k_][:].rearrange("a (c b) -> a c b", b=128), reads=[f"hot{k_}"], writes=["hc1T"])

                    fwd_transform(Yin, A, N2, hc, F2f, TW, F1, Y1t, CS, xs, tq, consume, V)
                P.barrier()

        HH = nx // 4096
        GSCALE = 128 ** -0.5

        def chunk_view(ap2, ci):
            if ci[0] == "c":
                return ap2[:, nx + 64 * ci[1]: nx + 64 * ci[1] + 64]
            return ap2[:, 0:nx].rearrange("p (h r c) -> p h r c", h=HH, r=64, c=64)[:, ci[2], :, ci[1]]

        def gla_order(d):
            nq = nctx // 64
            cs = [("c", q) for q in range(nq)]
            xs = [("x", c, hh) for c in range(64) for hh in range(HH)]
            return (cs + xs) if d == 0 else (cs[::-1] + xs[::-1])

        def gla_stage(l):
            import os
            G_HEADS = int(os.environ.get("GLA_HEADS", 4)); G_DIRS = int(os.environ.get("GLA_DIRS", 2))
            G_CHUNKS = int(os.environ.get("GLA_CHUNKS", 100000)); G_STEPS = float(os.environ.get("GLA_STEPS", 6))
            G_FINISH = int(os.environ.get("GLA_FINISH", 1))
            with contextlib.ExitStack() as st:
                c64 = SB(st, "gl_c64", [64, 6, 64], F32)
                idb = SB(st, "gl_idb", [128, 128], BF16)
                ngt = SB(st, "gl_ng", [128, 2], F32)
                kb = SB(st, "gl_k", [128, T], BF16)
                qb = SB(st, "gl_q", [128, T], BF16)
                vb = SB(st, "gl_v", [128, 2, T], BF16)
                lrb = SB(st, "gl_lr", [17, T], BF16)
                acc = SB(st, "gl_acc", [128, 2, T], F32)
                stg = [SB(st, f"gl_stg{i}", [128, 1056], F32) for i in range(2)]
                wgf = SB(st, "gl_wgf", [17, 128], F32)
                wgb = SB(st, "gl_wgb", [17, 128], BF16)
                Sst = SB(st, "gl_S", [128, 256], F32)
                NBUF = 2
                et = [SB(st, f"gl_e{i}", [64, 128], F32) for i in range(NBUF)]
                spt = [SB(st, f"gl_sp{i}", [64, 128], F32) for i in range(NBUF)]
                Ep = [SB(st, f"gl_Ep{i}", [128, 64], F32) for i in range(NBUF)]
                Em = [SB(st, f"gl_Em{i}", [128, 64], F32) for i in range(NBUF)]
                Ekv = [SB(st, f"gl_Ekv{i}", [64, 128], F32) for i in range(NBUF)]
                kkv = [SB(st, f"gl_kkv{i}", [64, 128], F32) for i in range(NBUF)]
                vtk = [SB(st, f"gl_vt{i}", [64, 256], F32) for i in range(NBUF)]
                qin = [SB(st, f"gl_qi{i}", [128, 64], F32) for i in range(NBUF)]
                kout = [SB(st, f"gl_ko{i}", [128, 64], F32) for i in range(NBUF)]
                sTt = [SB(st, f"gl_sT{i}", [64, 64], F32) for i in range(NBUF)]
                P.dma("sp", c64[:], cst64, writes=["c64"])
                P.dma("sp", idb[:], identb_in, writes=["idb"])
                P.dma("sp", ngt[:], gla_ng[l], writes=["ngt"])
                si = [0]

                def load_rows(dst_ap_fn, row0, nrows, key):
                    for t0 in range(0, T, 1056):
                        n = min(1056, T - t0)
                        j = si[0] % 2
                        si[0] += 1
                        P.dma("sp", stg[j][0:nrows, 0:n], pxA[row0:row0 + nrows, t0:t0 + n], reads=["pxT"], writes=[f"glstg{j}"])
                        eng = ("act", "pool")[si[0] % 2]
                        if eng == "act":
                            P.op("act", lambda e, j=j, t0=t0, n=n: e.activation(out=dst_ap_fn(t0, n), in_=stg[j][0:nrows, 0:n], func=AF.Copy),
                                 reads=[f"glstg{j}"], writes=[key])
                        else:
                            P.op("pool", lambda e, j=j, t0=t0, n=n: e.tensor_copy(out=dst_ap_fn(t0, n), in_=stg[j][0:nrows, 0:n]),
                                 reads=[f"glstg{j}"], writes=[key])

                for h in range(G_HEADS):
                    load_rows(lambda t0, n: kb[:, t0:t0 + n], 512 + h * 128, 128, "glk")
                    load_rows(lambda t0, n: qb[:, t0:t0 + n], 2080 + h * 128, 128, "glq")
                    for half in range(2):
                        load_rows(lambda t0, n, half=half: vb[:, half, t0:t0 + n], 1024 + h * 256 + half * 128, 128, "glv")
                    for d in range(G_DIRS):
                        P.op("pool", lambda e: e.memset(lrb[:], 1.0), writes=["gllr"])
                        load_rows(lambda t0, n: lrb[0:16, t0:t0 + n], 2048 + d * 16, 16, "gllr")
                        P.dma("sp", wgf[:], gla_wg[l, d, :, h * 128:(h + 1) * 128], writes=["glwgf"])
                        P.op("dve", lambda e: e.tensor_copy(out=wgb[:], in_=wgf[:]), reads=["glwgf"], writes=["glwgb"])
                        P.op("pool", lambda e: e.memset(Sst[:], 0.0), writes=["glS"])
                        tri, triC, msk = c64[:, d, :], c64[:, 2 + d, :], c64[:, 4 + d, :]
                        last = 63 if d == 0 else 0
                        order = gla_order(d)[:G_CHUNKS]
                        for it, ci in enumerate(order):
                            if ci[0] == "x" and it == nctx // 64:
                                pass
                            b = it % NBUF
                            kbk = f"glb{b}"
                            lrc, kc_, qc_ = chunk_view(lrb[:], ci), chunk_view(kb[:], ci), chunk_view(qb[:], ci)
                            P.op("pe", lambda e, lrc=lrc: e.matmul(ps[0][0:64, 0:128], lrc, wgb[:], start=True, stop=True),
                                 reads=["gllr", "glwgb"], writes=["glp0"])
                            P.op("act", lambda e, b=b: e.activation(out=et[b][:], in_=ps[0][0:64, 0:128], func=AF.Exp, scale=-1.0),
                                 reads=["glp0"], writes=[f"gle{b}"])
                            P.op("act", lambda e, b=b: e.activation(out=spt[b][:], in_=et[b][:], func=AF.Ln, bias=ones[0:64, 0:1]),
                                 reads=[f"gle{b}", "ones"], writes=[f"glsp{b}"])
                            if G_STEPS < 2:
                                continue
                            P.op("pe", lambda e, b=b, tri=tri: e.matmul(ps[1][:, 0:64], spt[b][:], tri, start=True, stop=True),
                                 reads=[f"glsp{b}", "c64"], writes=["glp1"])
                            P.op("pe", lambda e, b=b, triC=triC: e.matmul(ps[2][0:64, 0:128], triC, spt[b][:], start=True, stop=True),
                                 reads=[f"glsp{b}", "c64"], writes=["glp2"])
                            P.op("act", lambda e, b=b: e.activation(out=Ep[b][:], in_=ps[1][:, 0:64], func=AF.Exp),
                                 reads=["glp1"], writes=[f"glEp{b}"])
                            P.op("act", lambda e, b=b: e.activation(out=Em[b][:], in_=ps[1][:, 0:64], func=AF.Exp, scale=-1.0),
                                 reads=["glp1"], writes=[f"glEm{b}"])
                            P.op("act", lambda e, b=b: e.activation(out=Ekv[b][:], in_=ps[2][0:64, 0:128], func=AF.Exp),
                                 reads=["glp2"], writes=[f"glEkv{b}"])
                            if G_STEPS < 3:
                                continue
                            P.op("pe", lambda e, kc_=kc_: e.matmul(ps[3][0:64, 0:128], kc_, idb[:], start=True, stop=True),
                                 reads=["glk", "idb"], writes=["glp3"])
                            for half in range(2):
                                vc_ = chunk_view(vb[:, half, :], ci)
                                P.op("pe", lambda e, vc_=vc_, half=half: e.matmul(ps[3][0:64, 128 + half * 128:256 + half * 128], vc_, idb[:],
                                                                                 start=True, stop=True),
                                     reads=["glv", "idb"], writes=["glp3"])
                            if G_STEPS < 3.2:
                                continue
                            P.op("dve", lambda e, b=b: e.tensor_tensor(out=kkv[b][:], in0=Ekv[b][:], in1=ps[3][0:64, 0:128], op=ALU.mult),
                                 reads=[f"glEkv{b}", "glp3"], writes=[f"glkkv{b}"])
                            if G_STEPS < 3.3:
                                continue
                            P.op("dve", lambda e, b=b: e.tensor_copy(out=vtk[b][:], in_=ps[3][0:64, 128:384]),
                                 reads=["glp3"], writes=[f"glvt{b}"])
                            if G_STEPS < 3.5:
                                continue
                            P.op("dve", lambda e, b=b, qc_=qc_: e.scalar_tensor_tensor(out=qin[b][:], in0=qc_, scalar=GSCALE, in1=Ep[b][:],
                                                                                   op0=ALU.mult, op1=ALU.mult),
                                 reads=["glq", f"glEp{b}"], writes=[f"glqi{b}"])
                            P.op("dve", lambda e, b=b, kc_=kc_: e.tensor_tensor(out=kout[b][:], in0=kc_, in1=Em[b][:], op=ALU.mult),
                                 reads=["glk", f"glEm{b}"], writes=[f"glko{b}"])
                            if G_STEPS < 4:
                                continue
                            P.op("pe", lambda e, b=b: e.matmul(ps[4][0:64, 0:64], kout[b][:], qin[b][:], start=True, stop=True),
                                 reads=[f"glko{b}", f"glqi{b}"], writes=["glp4"])
                            P.op("dve", lambda e, b=b, msk=msk: e.tensor_tensor(out=sTt[b][:], in0=msk, in1=ps[4][0:64, 0:64], op=ALU.mult),
                                 reads=["glp4", "c64"], writes=[f"glsT{b}"])
                            if G_STEPS < 5:
                                continue
                            for half in range(2):
                                P.op("pe", lambda e, b=b, half=half: e.matmul(ps[5][:, half * 64:(half + 1) * 64],
                                                                             vtk[b][:, half * 128:(half + 1) * 128], sTt[b][:],
                                                                             start=True, stop=False),
                                     reads=[f"glvt{b}", f"glsT{b}"], writes=["glp5"])
                                P.op("pe", lambda e, b=b, half=half: e.matmul(ps[5][:, half * 64:(half + 1) * 64],
                                                                             Sst[:, half * 128:(half + 1) * 128], qin[b][:],
                                                                             start=False, stop=True),
                                     reads=["glS", f"glqi{b}"], writes=["glp5"])
                            accv = chunk_view(acc[:, 0, :], ci), chunk_view(acc[:, 1, :], ci)
                            for half in range(2):
                                if d == 0:
                                    P.op("dve", lambda e, half=half, accv=accv: e.tensor_copy(out=accv[half], in_=ps[5][:, half * 64:(half + 1) * 64]),
                                         reads=["glp5"], writes=["glacc"])
                                else:
                                    P.op("dve", lambda e, half=half, accv=accv: e.tensor_tensor(out=accv[half], in0=accv[half],
                                                                                                in1=ps[5][:, half * 64:(half + 1) * 64], op=ALU.add),
                                         reads=["glp5", "glacc"], writes=["glacc"])
                            if G_STEPS < 6:
                                continue
                            P.op("pe", lambda e, b=b: e.matmul(ps[6][:, 0:256], kkv[b][:], vtk[b][:], start=True, stop=True),
                                 reads=[f"glkkv{b}", f"glvt{b}"], writes=["glp6"])
                            P.op("dve", lambda e, b=b, last=last: e.scalar_tensor_tensor(out=Sst[:], in0=Sst[:], scalar=Ep[b][:, last:last + 1],
                                                                             in1=ps[6][:, 0:256], op0=ALU.mult, op1=ALU.add),
                                 reads=["glS", f"glEp{b}", "glp6"], writes=["glS"])
                    for bi, (t0, nb, col, _, _) in enumerate(blks if G_FINISH else []):
                        sq = stg[0]
                        rr = stg[1]
                        for half in range(2):
                            P.op("act", lambda e, half=half, t0=t0, nb=nb: e.activation(out=sq[:, half * 512:half * 512 + nb],
                                                                                        in_=acc[:, half, t0:t0 + nb], func=AF.Square),
                                 reads=["glacc"], writes=["glstg0"])
                        for half in range(2):
                            P.op("pe", lambda e, half=half, nb=nb: e.matmul(ps[7][:, 0:nb], ones[:], sq[:, half * 512:half * 512 + nb],
                                                                           start=(half == 0), stop=(half == 1)),
                                 reads=["ones", "glstg0"], writes=["glp7"])
                        rsd = Ep[0]
                        P.op("act", lambda e, nb=nb: e.activation(out=sq[:, 0:nb], in_=ps[7][:, 0:nb], func=AF.Sqrt,
                                                                  bias=epst[:, 0:1], scale=1.0 / 256),
                             reads=["glp7", "epst"], writes=["glstg0"])
                        P.op("dve", lambda e, nb=nb: e.reciprocal(out=sq[:, 0:nb], in_=sq[:, 0:nb]), reads=["glstg0"], writes=["glstg0"])
                        for half in range(2):
                            r0 = 2592 + h * 256 + half * 128
                            P.dma("sp", rr[:, half * 512:half * 512 + nb], pxA[r0:r0 + 128, t0:t0 + nb], reads=["pxT"], writes=["glstg1"])
                        P.op("act", lambda e: e.activation(out=rr[:, 0:1024], in_=rr[:, 0:1024], func=AF.Silu),
                             reads=["glstg1"], writes=["glstg1"])
                        for half in range(2):
                            ob = kout[0]
                            P.op("dve", lambda e, half=half, t0=t0, nb=nb: e.scalar_tensor_tensor(
                                out=rr[:, half * 512:half * 512 + nb], in0=acc[:, half, t0:t0 + nb], scalar=ngt[:, half:half + 1],
                                in1=rr[:, half * 512:half * 512 + nb], op0=ALU.mult, op1=ALU.mult),
                                reads=["glacc", "ngt", "glstg1"], writes=["glstg1"])
                            P.op("pool", lambda e, half=half, nb=nb: e.tensor_tensor(out=vb[:, half, 0:nb], in0=rr[:, half * 512:half * 512 + nb],
                                                                                  in1=sq[:, 0:nb], op=ALU.mult),
                                 reads=["glstg1", "glstg0"], writes=["glv"])
                            P.dma("pool", yT[bi, :, 8 + 2 * h + half, 0:nb], vb[:, half, 0:nb], reads=["glv"], writes=["yT"])
                P.barrier()

        if only:
            for r0 in range(0, PSPLIT, 512):
                P.dma("pool", pxA[r0:r0 + 512], px_dbg[r0:r0 + 512], writes=["pxT"])
            P.barrier()
            if only == "gla":
                gla_stage(0)
            if only == "s5":
                s5_stage(0)
            if only == "hy":
                hyena_stage(0, True)
                fs_out = nc.dram_tensor("fs_out", [2, 4, 128, 2, 128, hyc["x"]["N2"]], F32, kind="ExternalOutput").ap()
                for o_ in range(2):
                    for cb_ in range(4):
                        P.dma("pool", fs_out[o_, cb_], hyc["x"]["fspec"][o_, cb_], reads=["fspec"], writes=["fs_out"])
            for bi in range(len(blks)):
                P.dma("pool", y_out[bi], yT[bi], reads=["yT"], writes=["y_out"])
            P.barrier()
        for l in range(0 if only else depth):
            mod_stage(l)
            castw(w_in[l], wb_in, MC_IN, KC, "wsrc")
            norm_stage(A1, 0)

            def in_extra(st):
                return [SB(st, f"ie_o{i}", [128, 4, 512], F32) for i in range(2)]

            def in_epi(ctx, bi, blk, m0, g, pbase, go=0, oj=None):
                t0, nb = blk[0], blk[1]
                if oj is None:
                    oj = (pbase // 4) % 2
                okey = f"ieo{oj}_{go}"
                for gi in range(g):
                    P.op("act", lambda e, gi=gi: e.activation(out=ctx[oj][:, go + gi, 0:nb], in_=ps[pbase + gi][:, 0:nb], func=AF.Copy),
                         reads=[f"psg{pbase}"], writes=[okey])
                dst = pxA[m0 * 128:(m0 + g) * 128] if m0 * 128 < PSPLIT else pxB[m0 * 128 - PSPLIT:(m0 + g) * 128 - PSPLIT]
                P.dma("pool", dst[:, t0:t0 + nb].rearrange("(g p) n -> p g n", p=128),
                      ctx[oj][:, go:go + g, 0:nb], reads=[okey], writes=["pxT"])

            gemm_stage(hT, "hT", KC, wb_in, MC_IN, in_epi, in_extra, pair=True)

            with contextlib.ExitStack() as st:
                if debug:
                    P.dma("pool", yT, y_dbg, writes=["yT"])
                else:
                    z = SB(st, "zfill", [128, KC, 512], BF16)
                    P.op("pool", lambda e: e.memset(z[:], 0.0), writes=["zf"])
                    for bi in range(len(blks)):
                        P.dma("pool", yT[bi], z[:], reads=["zf"], writes=["yT"])
                P.barrier()

            if not debug:
                s5_stage(l)
                hyena_stage(l, l < depth - 1 or depth < DEPTH)
                gla_stage(l)
            castw(w_br[l], wb_br, KC, KC, "wsrc")
            castw(w_out[l], wb_out, KC, KC, "wsrc2")

            with contextlib.ExitStack() as st:
                yb = [SB(st, f"mg_y{i}", [128, KC, 512], BF16) for i in range(2)]
                xb = [SB(st, f"mg_x{i}", [128, KC, 512], F32) for i in range(2)]
                mT = SB(st, "mg_m", [128, KC, 512], BF16)
                wt = [SB(st, f"mg_w{i}", [128, KC, 128], BF16) for i in range(3)]
                gt = [SB(st, f"mg_g{i}", [128, 3, 512], F32) for i in range(2)]
                t1 = [SB(st, f"mg_t{i}", [128, 512], F32) for i in range(2)]
                t2 = [SB(st, f"mg_u{i}", [128, 512], F32) for i in range(2)]
                wi = 0
                for bi, (t0, nb, col, _, _) in enumerate(blks):
                    j = bi % 2
                    P.dma("sp", yb[j][:], yT[bi], reads=["yT"], writes=[f"mgy{j}"])
                    P.dma("sp", xb[j][:, :, 0:nb], xT[:, t0:t0 + nb].rearrange("(c p) n -> p c n", p=128),
                          reads=["xT"], writes=[f"mgx{j}"])
                    for mi in range(KC):
                        wj = wi % 3
                        gj = wi % 2
                        wi += 1
                        P.dma("sp", wt[wj][:], wb_br[mi], reads=["wsrc"], writes=[f"mgw{wj}"])
                        for gi3 in range(3):
                            r0 = GATE0 + gi3 * D + mi * 128
                            srcg = pxA[r0:r0 + 128] if r0 < PSPLIT else pxB[r0 - PSPLIT:r0 - PSPLIT + 128]
                            P.dma("pool", gt[gj][:, gi3, 0:nb], srcg[:, t0:t0 + nb], reads=["pxT"], writes=[f"mgg{gj}"])
                        P.op("act", lambda e, gj=gj, nb=nb: e.activation(out=gt[gj][:, :, 0:nb], in_=gt[gj][:, :, 0:nb], func=AF.Sigmoid),
                             reads=[f"mgg{gj}"], writes=[f"mgg{gj}"])
                        for bri, (k0, k1) in enumerate(((0, 4), (4, 8), (8, 16))):
                            pi = bri
                            for kc in range(k0, k1):
                                P.op("pe", lambda e, wj=wj, j=j, kc=kc, pi=pi, nb=nb, k0=k0, k1=k1: e.matmul(
                                    ps[pi][:, 0:nb], wt[wj][:, kc, :], yb[j][:, kc, 0:nb], start=(kc == k0), stop=(kc == k1 - 1)),
                                    reads=[f"mgw{wj}", f"mgy{j}"], writes=[f"ps{pi}"])
                        P.op("dve", lambda e, gj=gj, nb=nb: e.tensor_tensor(out=t1[gj][:, 0:nb], in0=gt[gj][:, 0, 0:nb], in1=ps[0][:, 0:nb], op=ALU.mult),
                             reads=[f"mgg{gj}", "ps0"], writes=[f"mgt{gj}"])
                        P.op("dve", lambda e, gj=gj, nb=nb: e.tensor_tensor(out=t2[gj][:, 0:nb], in0=gt[gj][:, 1, 0:nb], in1=ps[1][:, 0:nb], op=ALU.mult),
                             reads=[f"mgg{gj}", "ps1"], writes=[f"mgu{gj}"])
                        P.op("pool", lambda e, gj=gj, nb=nb: e.tensor_tensor(out=t1[gj][:, 0:nb], in0=t1[gj][:, 0:nb], in1=t2[gj][:, 0:nb], op=ALU.add),
                             reads=[f"mgt{gj}", f"mgu{gj}"], writes=[f"mgt{gj}"])
                        P.op("dve", lambda e, gj=gj, nb=nb: e.tensor_tensor(out=t2[gj][:, 0:nb], in0=gt[gj][:, 2, 0:nb], in1=ps[2][:, 0:nb], op=ALU.mult),
                             reads=[f"mgg{gj}", "ps2"], writes=[f"mgu{gj}"])
                        P.op("pool", lambda e, gj=gj, nb=nb, mi=mi: e.tensor_tensor(out=mT[:, mi, 0:nb], in0=t1[gj][:, 0:nb], in1=t2[gj][:, 0:nb], op=ALU.add),
                             reads=[f"mgt{gj}", f"mgu{gj}"], writes=["mgm"])
                    for mo in range(KC):
                        wj = wi % 3
                        pi = 4 + (wi % 2)
                        wi += 1
                        P.dma("sp", wt[wj][:], wb_out[mo], reads=["wsrc2"], writes=[f"mgw{wj}"])
                        for kc in range(KC):
                            P.op("pe", lambda e, wj=wj, kc=kc, pi=pi, nb=nb: e.matmul(
                                ps[pi][:, 0:nb], wt[wj][:, kc, :], mT[:, kc, 0:nb], start=(kc == 0), stop=(kc == KC - 1)),
                                reads=[f"mgw{wj}", "mgm"], writes=[f"ps{pi}"])
                        P.op("dve", lambda e, j=j, mo=mo, pi=pi, nb=nb, col=col: e.scalar_tensor_tensor(
                            out=xb[j][:, mo, 0:nb], in0=ps[pi][:, 0:nb], scalar=modv[:, 2 * KC + mo, col:col + 1],
                            in1=xb[j][:, mo, 0:nb], op0=ALU.mult, op1=ALU.add),
                            reads=[f"ps{pi}", "modv", f"mgx{j}"], writes=[f"mgx{j}"])
                    P.dma("pool", xT[:, t0:t0 + nb].rearrange("(c p) n -> p c n", p=128), xb[j][:, :, 0:nb],
                          reads=[f"mgx{j}"], writes=["xT"])
                P.barrier()

            castw(w_up[l], wb_up, MC_UP, KC, "wsrc")
            norm_stage(A2, 3 * KC)

            def up_extra(st):
                return [SB(st, f"ue_o{i}", [128, 4, 512], F32) for i in range(2)]

            def up_epi(ctx, bi, blk, m0, g, pbase, go=0, oj=None):
                t0, nb = blk[0], blk[1]
                if oj is None:
                    oj = (pbase // 4) % 2
                okey = f"ueo{oj}_{go}"
                for gi in range(g):
                    P.op("act", lambda e, gi=gi: e.activation(out=ctx[oj][:, go + gi, 0:nb], in_=ps[pbase + gi][:, 0:nb], func=AF.Copy),
                         reads=[f"psg{pbase}"], writes=[okey])
                dst = aT[m0 * 128:(m0 + g) * 128] if m0 < KC_FF else bT[(m0 - KC_FF) * 128:(m0 - KC_FF + g) * 128]
                P.dma("pool", dst[:, t0:t0 + nb].rearrange("(g p) n -> p g n", p=128),
                      ctx[oj][:, go:go + g, 0:nb], reads=[okey], writes=["abT"])

            gemm_stage(hT, "hT", KC, wb_up, MC_UP, up_epi, up_extra, pair=True)

            with contextlib.ExitStack() as st:
                CG = 4
                aw = [SB(st, f"cv_a{i}", [128, CG, 514], F32) for i in range(2)]
                bw = [SB(st, f"cv_b{i}", [128, CG, 512], F32) for i in range(2)]
                ow = [SB(st, f"cv_o{i}", [128, CG, 512], F32) for i in range(2)]
                uw = [SB(st, f"cv_u{i}", [128, CG, 512], BF16) for i in range(2)]
                it = 0
                for bi, (t0, nb, col, le, re) in enumerate(blks):
                    for c0 in range(0, KC_FF, CG):
                        j = it % 2
                        it += 1
                        lo = t0 if le else t0 - 1
                        hi = t0 + nb if re else t0 + nb + 1
                        if le or re:
                            P.op("pool", lambda e, j=j: e.memset(aw[j][:], 0.0), writes=[f"cva{j}"])
                        P.dma("sp", aw[j][:, :, (lo - (t0 - 1)):(hi - (t0 - 1))],
                              aT[c0 * 128:(c0 + CG) * 128, lo:hi].rearrange("(g p) n -> p g n", p=128),
                              reads=["abT"], writes=[f"cva{j}"])
                        P.dma("sp", bw[j][:, :, 0:nb],
                              bT[c0 * 128:(c0 + CG) * 128, t0:t0 + nb].rearrange("(g p) n -> p g n", p=128),
                              reads=["abT"], writes=[f"cvb{j}"])
                        for gi in range(CG):
                            cc = c0 + gi
                            P.op("dve", lambda e, j=j, cc=cc, nb=nb, gi=gi: e.tensor_scalar(
                                out=ow[j][:, gi, 0:nb], in0=aw[j][:, gi, 0:nb], scalar1=cwt[:, cc, 0:1], scalar2=cbt[:, cc:cc + 1],
                                op0=ALU.mult, op1=ALU.add), reads=[f"cva{j}", "cwt", "cbt"], writes=[f"cvo{j}"])
                            P.op("dve", lambda e, j=j, cc=cc, nb=nb, gi=gi: e.scalar_tensor_tensor(
                                out=ow[j][:, gi, 0:nb], in0=aw[j][:, gi, 1:nb + 1], scalar=cwt[:, cc, 1:2], in1=ow[j][:, gi, 0:nb],
                                op0=ALU.mult, op1=ALU.add), reads=[f"cva{j}", "cwt", f"cvo{j}"], writes=[f"cvo{j}"])
                            P.op("dve", lambda e, j=j, cc=cc, nb=nb, gi=gi: e.scalar_tensor_tensor(
                                out=ow[j][:, gi, 0:nb], in0=aw[j][:, gi, 2:nb + 2], scalar=cwt[:, cc, 2:3], in1=ow[j][:, gi, 0:nb],
                                op0=ALU.mult, op1=ALU.add), reads=[f"cva{j}", "cwt", f"cvo{j}"], writes=[f"cvo{j}"])
                        P.op("act", lambda e, j=j, nb=nb: e.activation(out=ow[j][:, :, 0:nb], in_=ow[j][:, :, 0:nb], func=AF.Silu),
                             reads=[f"cvo{j}"], writes=[f"cvo{j}"])
                        P.op("pool", lambda e, j=j, nb=nb: e.tensor_tensor(out=uw[j][:, :, 0:nb], in0=ow[j][:, :, 0:nb], in1=bw[j][:, :, 0:nb], op=ALU.mult),
                             reads=[f"cvo{j}", f"cvb{j}"], writes=[f"cvu{j}"])
                        P.dma("pool", uT[bi, :, c0:c0 + CG, 0:nb], uw[j][:, :, 0:nb], reads=[f"cvu{j}"], writes=["uT"])
                P.barrier()

            castw(w_dn[l], wb_dn, KC, KC_FF, "wsrc")

            def dn_extra(st):
                return SB(st, "de_x", [128, KC, 512], F32)

            def dn_epi(ctx, bi, blk, m0, g, pbase):
                t0, nb, col = blk[0], blk[1], blk[2]
                if m0 == 0:
                    P.dma("pool", ctx[:, :, 0:nb], xT[:, t0:t0 + nb].rearrange("(c p) n -> p c n", p=128),
                          reads=["xT"], writes=["dex"])
                for gi in range(g):
                    m = m0 + gi
                    P.op("dve", lambda e, gi=gi, m=m: e.scalar_tensor_tensor(
                        out=ctx[:, m, 0:nb], in0=ps[pbase + gi][:, 0:nb], scalar=modv[:, 5 * KC + m, col:col + 1],
                        in1=ctx[:, m, 0:nb], op0=ALU.mult, op1=ALU.add),
                        reads=[f"psg{pbase}", "modv", "dex"], writes=["dex"])
                if m0 + g >= KC:
                    P.dma("pool", xT[:, t0:t0 + nb].rearrange("(c p) n -> p c n", p=128), ctx[:, :, 0:nb],
                          reads=["dex"], writes=["xT"])

            gemm_stage(uT, "uT", KC_FF, wb_dn, KC, dn_epi, dn_extra, G=2)

        if not only:
            norm_stage(None, 0, final=True)
        P.emit()
    return nc


def _tile_w(w, kc_n):
    K, M = w.shape
    return np.ascontiguousarray(w.reshape(kc_n, 128, M // 128, 128).transpose(2, 1, 0, 3))


def _vec(v):
    return np.ascontiguousarray(v.reshape(-1, 128).T)


def const_tables():
    j = np.arange(64)[:, None]
    i = np.arange(64)[None, :]
    c = -1.0 / 16.0
    t = np.stack([c * (j <= i), c * (j >= i), c * (j > i), c * (j < i), 1.0 * (j <= i), 1.0 * (j >= i)], axis=1)
    return np.ascontiguousarray(t.astype(np.float32))


def hy_tables(n):
    N = 2 * n
    N2, A = n // 64, n // 128
    tau = np.arange(N)
    pos = np.where(tau < n, tau, N - tau).astype(np.float64)
    t = pos / max(n - 1, 1)
    freqs = np.linspace(1e-4, 15, 16)
    ang = (2.0 * np.pi / n) * pos[:, None] * freqs[None]
    feats = np.concatenate([t[:, None], np.cos(ang), -np.sin(ang)], axis=1)
    rates = np.abs(np.linspace(np.log(1e-2) / 1.5, np.log(1e-2) / 0.3, 512))
    dec = np.exp(-t[:, None] * rates[None])
    dec[n] = 0.0
    dec = dec.reshape(N2, 128, 4, 128).transpose(2, 0, 3, 1)
    a = np.arange(N2)
    k1 = np.arange(128)
    th2 = 2 * np.pi * np.outer(a, a) / N2
    F2f = np.stack([np.cos(th2), -np.sin(th2)], axis=1)
    thw = 2 * np.pi * np.outer(k1, a) / N
    TW = np.stack([np.cos(thw), -np.sin(thw)], axis=1)
    thi = 2 * np.pi * np.outer(a, np.arange(A)) / N2
    F2i = np.stack([np.cos(thi) / N, -np.sin(thi) / N], axis=1)
    tht = 2 * np.pi * np.outer(a, k1) / N
    TWT = np.stack([np.cos(tht), np.sin(tht)], axis=1)
    msk = np.stack([(a < A), (a >= A)], axis=1)
    c = lambda v: np.ascontiguousarray(v.astype(np.float32))
    return dict(feats=c(feats.T), dec=c(dec), F2f=c(F2f), TW=c(TW), F2i=c(F2i), TWT=c(TWT), msk=c(msk))


def hy_f1_table():
    k = np.arange(128)
    th = 2 * np.pi * np.outer(k, k) / 128
    return np.ascontiguousarray(np.stack([np.cos(th), -np.sin(th), np.sin(th)], axis=1).astype(np.float32))


def hy_host_layout(inp, depth):
    f = lambda a: np.asarray(a, dtype=np.float32)
    out = {}
    out["hy_cw"] = np.stack([np.ascontiguousarray(f(inp["hy_conv_w"][l]).reshape(3, 12, 128).transpose(2, 1, 0)) for l in range(depth)])
    out["hy_cb"] = np.stack([np.ascontiguousarray(f(inp["hy_conv_b"][l]).reshape(12, 128).T) for l in range(depth)])
    out["hy_w1"] = np.ascontiguousarray(f(inp["hy_f_w1"])[:depth])
    out["hy_w2"] = np.ascontiguousarray(f(inp["hy_f_w2"])[:depth])
    w3 = np.concatenate([f(inp["hy_f_w3"])[:depth], f(inp["hy_f_b3"])[:depth, None, :]], axis=1)
    w3 = w3.reshape(depth, 65, 2, 2, 4, 128).transpose(0, 1, 2, 4, 3, 5).reshape(depth, 65, 2, 4, 256)
    out["hy_w3"] = np.ascontiguousarray(w3)
    fb = np.zeros((depth, 64, 6), np.float32)
    fb[:, :, 0] = f(inp["hy_f_freq1"])[:depth]
    fb[:, :, 1] = f(inp["hy_f_b1"])[:depth]
    fb[:, :, 2] = f(inp["hy_f_freq2"])[:depth]
    fb[:, :, 3] = f(inp["hy_f_b2"])[:depth]
    out["hy_fb"] = fb
    out["hy_bias"] = np.stack([np.ascontiguousarray(f(inp["hy_bias"][l]).reshape(2, 4, 128).transpose(2, 0, 1)) for l in range(depth)])
    return out


def s5_const_tables():
    a = np.arange(128)
    tri_f = (a[:, None] <= a[None, :]).astype(np.float32)
    tri_b = (a[:, None] >= a[None, :]).astype(np.float32)
    krow_f = np.broadcast_to(a[None, :], (128, 128)).astype(np.float32)
    krow_b = np.broadcast_to((127 - a)[None, :], (128, 128)).astype(np.float32)
    c128 = np.ascontiguousarray(np.stack([tri_f, tri_b, krow_f, krow_b], axis=1))
    ck = np.ascontiguousarray(np.stack([a, 127 - a, -a, -(127 - a)], axis=1).astype(np.float32))
    return c128, ck


def s5_host_layout(inp, depth):
    f = lambda a: np.asarray(a, dtype=np.float32)
    out = {}
    sp = lambda a: np.ascontiguousarray(a.reshape(depth, 2, 16, 2, 64).transpose(0, 3, 4, 1, 2).reshape(depth, 128, 2, 16))
    out["s5_are"] = sp(f(inp["s5_a_re"])[:depth])
    out["s5_aim"] = sp(f(inp["s5_a_im"])[:depth])
    ls = np.broadcast_to(f(inp["s5_log_step"])[:depth, :, :, None], (depth, 2, 32, 64))
    out["s5_lst"] = sp(np.ascontiguousarray(ls))
    def bblk(bm):
        o = np.zeros((depth, 2, 4, 128, 512), np.float32)
        b5 = bm.reshape(depth, 2, 4, 8, 64, 16)
        for gl in range(8):
            o[:, :, :, gl * 16:(gl + 1) * 16, gl * 64:(gl + 1) * 64] = b5[:, :, :, gl].transpose(0, 1, 2, 4, 3)
        return o
    out["s5_bre"] = bblk(f(inp["s5_b_re"])[:depth])
    out["s5_bim"] = bblk(f(inp["s5_b_im"])[:depth])
    def cblk(cm):
        o = np.zeros((depth, 2, 4, 128, 4, 128), np.float32)
        c6 = cm.reshape(depth, 2, 4, 4, 2, 16, 64)
        for s_ in range(4):
            for gi in range(2):
                ch0 = (2 * s_ + gi) * 16
                o[:, :, :, gi * 64:(gi + 1) * 64, s_, ch0:ch0 + 16] = c6[:, :, :, s_, gi].transpose(0, 1, 2, 4, 3)
        return o
    out["s5_cre"] = cblk(f(inp["s5_c_re"])[:depth])
    out["s5_cim"] = cblk(f(inp["s5_c_im"])[:depth])
    out["s5_dd"] = np.stack([np.ascontiguousarray(f(inp["s5_d"][l]).reshape(4, 128).T) for l in range(depth)])
    out["s5_gw"] = np.stack([np.ascontiguousarray(f(inp["s5_glu_w"][l]).reshape(4, 128, 4, 128).transpose(1, 2, 0, 3)) for l in range(depth)])
    out["s5_gb"] = np.stack([np.ascontiguousarray(f(inp["s5_glu_b"][l]).reshape(4, 128).T) for l in range(depth)])
    return out


def prep_inputs(inp, depth=DEPTH, nx=SEQ, nctx=CTX):
    f = lambda a: np.asarray(a, dtype=np.float32)
    KC = D // 128
    shared = {}
    shared["w_mod"] = np.stack([_tile_w(f(inp["w_mod"][l]), KC) for l in range(depth)])
    shared["b_mod"] = np.stack([_vec(f(inp["b_mod"][l])) for l in range(depth)])
    shared["n1g"] = np.stack([_vec(f(inp["norm1_g"][l])) for l in range(depth)])
    shared["n2g"] = np.stack([_vec(f(inp["norm2_g"][l])) for l in range(depth)])
    shared["fng"] = _vec(f(inp["final_norm_g"]))
    win = np.zeros((depth, D, IN_WP), np.float32)
    win[:, :, :C_MG] = f(inp["w_in"])[:depth, :, :C_MG]
    win[:, :, GATE0:] = f(inp["w_in"])[:depth, :, C_MG:]
    shared["w_in"] = np.stack([_tile_w(win[l], KC) for l in range(depth)])
    shared["w_br"] = np.stack([_tile_w(f(inp["w_branch"][l]), KC) for l in range(depth)])
    shared["w_out"] = np.stack([_tile_w(f(inp["w_out"][l]), KC) for l in range(depth)])
    shared["w_up"] = np.stack([_tile_w(f(inp["ff_w_up"][l]), KC) for l in range(depth)])
    shared["w_dn"] = np.stack([_tile_w(f(inp["ff_w_down"][l]), FF // 128) for l in range(depth)])
    shared["fcw"] = np.stack([np.ascontiguousarray(f(inp["ff_conv_w"][l]).reshape(3, FF // 128, 128).transpose(2, 1, 0))
                              for l in range(depth)])
    shared["fcb"] = np.stack([_vec(f(inp["ff_conv_b"][l])) for l in range(depth)])
    shared.update(s5_host_layout(inp, depth))
    shared.update(hy_host_layout(inp, depth))
    shared["hy_F1"] = hy_f1_table()
    for nm, n_ in (("x", nx), ("c", nctx)):
        for k_, v_ in hy_tables(n_).items():
            shared[f"hy_{k_}_{nm}"] = v_
    shared["cst128"], shared["cstk"] = s5_const_tables()
    wgaug = np.concatenate([f(inp["gla_wg"])[:depth], f(inp["gla_bg"])[:depth, :, None, :]], axis=2)
    shared["gla_wg"] = np.ascontiguousarray(wgaug)
    shared["gla_ng"] = np.stack([np.ascontiguousarray(f(inp["gla_norm_g"][l]).reshape(2, 128).T) for l in range(depth)])
    shared["cst64"] = const_tables()
    shared["identb"] = np.eye(128, dtype=np.float32).astype(ml_dtypes.bfloat16)
    maps = []
    for b in range(2):
        m = dict(shared)
        m["xT"] = np.ascontiguousarray(np.concatenate([f(inp["x"][b, :nx]).T, f(inp["ctx"][b, :nctx]).T], axis=1))
        cc = np.stack([f(inp["c"][b]), f(inp["c_ctx"])], axis=1)
        m["cT"] = np.ascontiguousarray(cc.reshape(KC, 128, 2).transpose(1, 0, 2))
        maps.append(m)
    return maps


def kernel(**inputs):
    nc = build()
    maps = prep_inputs(inputs)
    res = run_bass_kernel_spmd(nc, maps, core_ids=[0, 1])
    out = np.stack([np.ascontiguousarray(res.results[b]["outT"].T) for b in range(2)], axis=0)
    return out.astype(np.float32)
```

```python
import contextlib
import numpy as np
import ml_dtypes
import concourse.bass as bass
import concourse.mybir as mybir
from concourse.bass_utils import run_bass_kernel_spmd

F32 = mybir.dt.float32
BF16 = mybir.dt.bfloat16
AF = mybir.ActivationFunctionType
ALU = mybir.AluOpType

D = 2048
DEPTH = 4
SEQ = 8192
CTX = 256
IN_W = 11296
IN_WP = 11392
FF = 5632
C_MG = IN_W - 3 * D
GATE0 = 5248
PSPLIT = 5632
EPS = 1e-6
ENGS = ("pe", "act", "dve", "pool", "sp")


class Prog:
    def __init__(self, nc, n_dma_slots=6):
        self.nc = nc
        self.ops = {e: [] for e in ENGS}
        self.count = {e: 0 for e in ENGS}
        self.waited = {e: {} for e in ENGS}
        self.last_writer = {}
        self.readers = {}
        self.n_dma_slots = n_dma_slots
        self.dma_uses = {e: [0] * n_dma_slots for e in ENGS}
        self.dma_next = {e: 0 for e in ENGS}

    def _deps(self, reads, writes):
        deps = {}

        def add(tok):
            if tok:
                for s, v in tok.items():
                    if deps.get(s, 0) < v:
                        deps[s] = v

        for r in reads:
            add(self.last_writer.get(r))
        for w in writes:
            add(self.last_writer.get(w))
            add(self.readers.get(w))
        return deps

    def _record(self, tok, reads, writes):
        for w in writes:
            self.last_writer[w] = dict(tok)
            self.readers[w] = {}
        for r in reads:
            d = self.readers.setdefault(r, {})
            for s, v in tok.items():
                if d.get(s, 0) < v:
                    d[s] = v

    def _waits(self, eng, deps):
        out = []
        wd = self.waited[eng]
        for s, v in deps.items():
            if s == ("e", "pe") and eng == "pe":
                continue
            if wd.get(s, 0) < v:
                wd[s] = v
                out.append((s, v))
        return out

    def op(self, eng, fn, reads=(), writes=()):
        deps = self._deps(reads, writes)
        waits = self._waits(eng, deps)
        self.count[eng] += 1
        tok = {("e", eng): self.count[eng]}
        self.ops[eng].append((waits, fn, (("e", eng), 1)))
        self._record(tok, reads, writes)

    def dma(self, eng, out, in_, reads=(), writes=(), **kw):
        deps = self._deps(reads, writes)
        slot = self.dma_next[eng]
        self.dma_next[eng] = (slot + 1) % self.n_dma_slots
        s = ("d", eng, slot)
        prev = self.dma_uses[eng][slot]
        if prev > 0 and deps.get(s, 0) < 16 * prev:
            deps[s] = 16 * prev
        waits = self._waits(eng, deps)
        self.dma_uses[eng][slot] = prev + 1
        tok = {s: 16 * (prev + 1)}
        self.ops[eng].append((waits, lambda e: e.dma_start(out=out, in_=in_, **kw), (s, 16)))
        self._record(tok, reads, writes)

    def barrier(self):
        deps = {}
        for e in ENGS:
            if self.count[e] > 0:
                deps[("e", e)] = self.count[e]
            for slot, u in enumerate(self.dma_uses[e]):
                if u > 0:
                    deps[("d", e, slot)] = 16 * u
        for e in ENGS:
            waits = []
            wd = self.waited[e]
            for s, v in deps.items():
                if wd.get(s, 0) < v:
                    wd[s] = v
                    waits.append((s, v))
            self.ops[e].append((waits, None, None))
        self.last_writer = {}
        self.readers = {}

    def emit(self):
        nc = self.nc
        names = set()
        for e in ENGS:
            for waits, fn, inc in self.ops[e]:
                for s, v in waits:
                    names.add(s)
                if inc:
                    names.add(inc[0])
        names = sorted(names, key=str)
        with contextlib.ExitStack() as st:
            sem = {}
            for n in names:
                sem[n] = st.enter_context(nc.semaphore("s_" + "_".join(str(x) for x in n)))
            block = st.enter_context(nc.Block())

            def run(e):
                def f(eng):
                    for waits, fn, inc in self.ops[e]:
                        for s, v in waits:
                            eng.wait_ge(sem[s], v)
                        if fn is not None:
                            fn(eng).then_inc(sem[inc[0]], inc[1])
                return f

            block.tensor(run("pe"))
            block.scalar(run("act"))
            block.vector(run("dve"))
            block.gpsimd(run("pool"))
            block.sync(run("sp"))


def token_blocks(nx, nctx):
    blks = []
    for t0 in range(0, nx, 512):
        blks.append((t0, min(512, nx - t0), 0, t0 == 0, t0 + 512 >= nx))
    for t0 in range(0, nctx, 512):
        blks.append((nx + t0, min(512, nctx - t0), 1, t0 == 0, t0 + 512 >= nctx))
    return blks


def build(depth=DEPTH, nx=SEQ, nctx=CTX, debug=False, only=None):
    T = nx + nctx
    KC = D // 128
    MC_IN = IN_WP // 128
    MC_UP = 2 * FF // 128
    KC_FF = FF // 128
    blks = token_blocks(nx, nctx)
    nc = bass.Bass("TRN2", target_bir_lowering=False)
    dt_in = lambda name, shape, dt=F32: nc.dram_tensor(name, shape, dt, kind="ExternalInput").ap()
    xT_in = dt_in("xT", [D, T])
    cT = dt_in("cT", [128, KC, 2])
    w_mod = dt_in("w_mod", [depth, 6 * KC, 128, KC, 128])
    b_mod = dt_in("b_mod", [depth, 128, 6 * KC])
    n1g = dt_in("n1g", [depth, 128, KC])
    n2g = dt_in("n2g", [depth, 128, KC])
    fng = dt_in("fng", [128, KC])
    w_in = dt_in("w_in", [depth, MC_IN, 128, KC, 128])
    w_br = dt_in("w_br", [depth, KC, 128, KC, 128])
    w_out = dt_in("w_out", [depth, KC, 128, KC, 128])
    w_up = dt_in("w_up", [depth, MC_UP, 128, KC, 128])
    w_dn = dt_in("w_dn", [depth, KC, 128, KC_FF, 128])
    fcw = dt_in("fcw", [depth, 128, KC_FF, 3])
    fcb = dt_in("fcb", [depth, 128, KC_FF])
    s5_are = dt_in("s5_are", [depth, 128, 2, 16])
    s5_aim = dt_in("s5_aim", [depth, 128, 2, 16])
    s5_lst = dt_in("s5_lst", [depth, 128, 2, 16])
    s5_bre = dt_in("s5_bre", [depth, 2, 4, 128, 512])
    s5_bim = dt_in("s5_bim", [depth, 2, 4, 128, 512])
    s5_cre = dt_in("s5_cre", [depth, 2, 4, 128, 4, 128])
    s5_cim = dt_in("s5_cim", [depth, 2, 4, 128, 4, 128])
    s5_dd = dt_in("s5_dd", [depth, 128, 4])
    s5_gw = dt_in("s5_gw", [depth, 128, 4, 4, 128])
    s5_gb = dt_in("s5_gb", [depth, 128, 4])
    cst128 = dt_in("cst128", [128, 4, 128])
    cstk = dt_in("cstk", [128, 4])
    hy_cw = dt_in("hy_cw", [depth, 128, 12, 3])
    hy_cb = dt_in("hy_cb", [depth, 128, 12])
    hy_w1 = dt_in("hy_w1", [depth, 33, 64])
    hy_w2 = dt_in("hy_w2", [depth, 64, 64])
    hy_w3 = dt_in("hy_w3", [depth, 65, 2, 4, 256])
    hy_fb = dt_in("hy_fb", [depth, 64, 6])
    hy_bias = dt_in("hy_bias", [depth, 128, 2, 4])
    HSEQ = [("x", 0, nx), ("c", nx, nctx)]
    hyc = {}
    for nm, _, n_ in HSEQ:
        N2_ = n_ // 64
        A_ = n_ // 128
        hyc[nm] = dict(N2=N2_, A=A_,
                       feats=dt_in(f"hy_feats_{nm}", [33, 2 * n_]),
                       dec=dt_in(f"hy_dec_{nm}", [4, N2_, 128, 128]),
                       F2f=dt_in(f"hy_F2f_{nm}", [N2_, 2, N2_]),
                       TW=dt_in(f"hy_TW_{nm}", [128, 2, N2_]),
                       F2i=dt_in(f"hy_F2i_{nm}", [N2_, 2, A_]),
                       TWT=dt_in(f"hy_TWT_{nm}", [N2_, 2, 128]),
                       msk=dt_in(f"hy_msk_{nm}", [N2_, 2]),
                       fspec=nc.dram_tensor(f"hy_fspec_{nm}", [2, 4, 128, 2, 128, N2_], F32).ap())
    hy_F1 = dt_in("hy_F1", [128, 3, 128])
    gla_wg = dt_in("gla_wg", [depth, 2, 17, 512])
    gla_ng = dt_in("gla_ng", [depth, 128, 2])
    cst64 = dt_in("cst64", [64, 6, 64])
    identb_in = dt_in("identb", [128, 128], BF16)
    outT = nc.dram_tensor("outT", [D, nx], F32, kind="ExternalOutput").ap()
    if only:
        dbg_out = nc.dram_tensor("dbg_out", [nx // 64, 4, 128], F32, kind="ExternalOutput").ap()
        px_dbg = dt_in("px_dbg", [PSPLIT, T])
        y_out = nc.dram_tensor("y_out", [len(blks), 128, KC, 512], BF16, kind="ExternalOutput").ap()
    if debug:
        y_dbg = dt_in("y_dbg", [len(blks), 128, KC, 512], BF16)
    xT = nc.dram_tensor("xTs", [D, T], F32).ap()
    hT = nc.dram_tensor("hTs", [len(blks), 128, KC, 512], BF16).ap()
    pxA = nc.dram_tensor("pxAs", [PSPLIT, T], F32).ap()
    pxB = nc.dram_tensor("pxBs", [IN_WP - PSPLIT, T], F32).ap()
    yT = nc.dram_tensor("yTs", [len(blks), 128, KC, 512], BF16).ap()
    aT = nc.dram_tensor("aTs", [FF, T], F32).ap()
    bT = nc.dram_tensor("bTs", [FF, T], F32).ap()
    uT = nc.dram_tensor("uTs", [len(blks), 128, KC_FF, 512], BF16).ap()
    wb_in = nc.dram_tensor("wb_in", [MC_IN, 128, KC, 128], BF16).ap()
    wb_br = nc.dram_tensor("wb_br", [KC, 128, KC, 128], BF16).ap()
    wb_out = nc.dram_tensor("wb_out", [KC, 128, KC, 128], BF16).ap()
    wb_up = nc.dram_tensor("wb_up", [MC_UP, 128, KC, 128], BF16).ap()
    wb_dn = nc.dram_tensor("wb_dn", [KC, 128, KC_FF, 128], BF16).ap()
    s5rows = nc.dram_tensor("s5rows", [2, 4, 2048], F32).ap()
    ygT = nc.dram_tensor("ygT", [len(blks), 128, 4, 512], BF16).ap()
    hvT = nc.dram_tensor("hvT", [3, 512, T], F32).ap()
    hc1T = nc.dram_tensor("hc1T", [512, T], F32).ap()
    hy1T = nc.dram_tensor("hy1T", [512, T], F32).ap()

    P = Prog(nc)
    root = contextlib.ExitStack()
    with root:
        uid = [0]

        def SB(st, name, shape, dt):
            uid[0] += 1
            return st.enter_context(nc.sbuf_tensor(f"{name}_{uid[0]}", shape, dt))
        PS = lambda st, name: st.enter_context(nc.psum_tensor(name, [128, 512], F32))
        ones = SB(root, "ones", [128, 128], F32)
        sc = SB(root, "sc", [128, KC, 2], F32)
        modv = SB(root, "modv", [128, 6 * KC, 2], F32)
        A1 = SB(root, "A1", [128, KC, 2], F32)
        A2 = SB(root, "A2", [128, KC, 2], F32)
        bm = SB(root, "bm", [128, 6 * KC], F32)
        g1t = SB(root, "g1t", [128, KC], F32)
        g2t = SB(root, "g2t", [128, KC], F32)
        gft = SB(root, "gft", [128, KC], F32)
        cwt = SB(root, "cwt", [128, KC_FF, 3], F32)
        cbt = SB(root, "cbt", [128, KC_FF], F32)
        ps = [PS(root, f"ps{i}") for i in range(8)]

        P.op("pool", lambda e: e.memset(ones[:], 1.0), writes=["ones"])
        epst = SB(root, "epst", [128, 1], F32)
        P.op("pool", lambda e: e.memset(epst[:], EPS), writes=["epst"])
        P.dma("sp", sc[:], cT, writes=["sc"])
        P.op("act", lambda e: e.activation(out=sc[:], in_=sc[:], func=AF.Silu), reads=["sc"], writes=["sc"])
        P.dma("sp", gft[:], fng, writes=["gft"])
        for r0 in range(0, D, 256):
            P.dma("pool", xT[r0:r0 + 256], xT_in[r0:r0 + 256], writes=["xT"])
        P.barrier()

        def castw(src, dst, n, kc, tag):
            with contextlib.ExitStack() as st:
                f = [SB(st, f"cw_f{i}", [128, kc, 128], F32) for i in range(3)]
                b = [SB(st, f"cw_b{i}", [128, kc, 128], BF16) for i in range(3)]
                for i in range(n):
                    j = i % 3
                    P.dma("sp", f[j][:], src[i], writes=[f"cwf{j}"])
                    eng = ("dve", "pool", "act")[i % 3]
                    if eng == "act":
                        P.op("act", lambda e, j=j: e.activation(out=b[j][:], in_=f[j][:], func=AF.Copy),
                             reads=[f"cwf{j}"], writes=[f"cwb{j}"])
                    else:
                        P.op(eng, lambda e, j=j: e.tensor_copy(out=b[j][:], in_=f[j][:]),
                             reads=[f"cwf{j}"], writes=[f"cwb{j}"])
                    P.dma("pool", dst[i], b[j][:], reads=[f"cwb{j}"], writes=[tag])
                P.barrier()

        def mod_stage(l):
            with contextlib.ExitStack() as st:
                wt = [SB(st, f"md_w{i}", [128, KC, 128], F32) for i in range(3)]
                P.dma("sp", bm[:], b_mod[l], writes=["bm"])
                for mi in range(6 * KC):
                    j = mi % 3
                    P.dma("sp", wt[j][:], w_mod[l, mi], writes=[f"mdw{j}"])
                    pb = ps[mi % 2]
                    for kc in range(KC):
                        P.op("pe", lambda e, j=j, kc=kc, pb=pb: e.matmul(pb[:, 0:2], wt[j][:, kc, :], sc[:, kc, :],
                                                                         start=(kc == 0), stop=(kc == KC - 1)),
                             reads=[f"mdw{j}", "sc"], writes=[f"ps{mi % 2}"])
                    P.op("act", lambda e, mi=mi, pb=pb: e.activation(out=modv[:, mi, :], in_=pb[:, 0:2], func=AF.Identity,
                                                                     bias=bm[:, mi:mi + 1]),
                         reads=[f"ps{mi % 2}", "bm"], writes=["modv"])
                P.dma("sp", g1t[:], n1g[l], writes=["g1t"])
                P.dma("sp", g2t[:], n2g[l], writes=["g2t"])
                P.dma("sp", cwt[:], fcw[l], writes=["cwt"])
                P.dma("sp", cbt[:], fcb[l], writes=["cbt"])
                for col in range(2):
                    P.op("dve", lambda e, col=col: e.scalar_tensor_tensor(
                        out=A1[:, :, col], in0=modv[:, KC:2 * KC, col], scalar=1.0, in1=g1t[:], op0=ALU.add, op1=ALU.mult),
                        reads=["modv", "g1t"], writes=["A1"])
                    P.op("dve", lambda e, col=col: e.scalar_tensor_tensor(
                        out=A2[:, :, col], in0=modv[:, 4 * KC:5 * KC, col], scalar=1.0, in1=g2t[:], op0=ALU.add, op1=ALU.mult),
                        reads=["modv", "g2t"], writes=["A2"])
                P.barrier()

        def norm_stage(Asc, shift_base, final=False):
            with contextlib.ExitStack() as st:
                xb = [SB(st, f"nm_x{i}", [128, KC, 512], F32) for i in range(2)]
                sq = SB(st, "nm_sq", [128, KC, 512], F32)
                rs = SB(st, "nm_rs", [128, 512], F32)
                hb = [SB(st, f"nm_h{i}", [128, KC, 512], F32 if final else BF16) for i in range(2)]
                for bi, (t0, nb, col, _, _) in enumerate(blks):
                    if final and col == 1:
                        continue
                    j = bi % 2
                    P.dma("sp", xb[j][:, :, 0:nb], xT[:, t0:t0 + nb].rearrange("(c p) n -> p c n", p=128),
                          reads=["xT"], writes=[f"nmx{j}"])
                    P.op("act", lambda e, j=j, nb=nb: e.activation(out=sq[:, :, 0:nb], in_=xb[j][:, :, 0:nb], func=AF.Square),
                         reads=[f"nmx{j}"], writes=["nmsq"])
                    for kc in range(KC):
                        P.op("pe", lambda e, kc=kc, nb=nb: e.matmul(ps[0][:, 0:nb], ones[:], sq[:, kc, 0:nb],
                                                                    start=(kc == 0), stop=(kc == KC - 1)),
                             reads=["ones", "nmsq"], writes=["ps0"])
                    P.op("act", lambda e, nb=nb: e.activation(out=rs[:, 0:nb], in_=ps[0][:, 0:nb], func=AF.Sqrt,
                                                              bias=epst[:, 0:1], scale=1.0 / D),
                         reads=["ps0", "epst"], writes=["nmrs"])
                    P.op("dve", lambda e, nb=nb: e.reciprocal(out=rs[:, 0:nb], in_=rs[:, 0:nb]),
                         reads=["nmrs"], writes=["nmrs"])
                    for kc in range(KC):
                        eng = "dve" if kc % 2 == 0 else "pool"
                        P.op(eng, lambda e, j=j, kc=kc, nb=nb: e.tensor_tensor(out=xb[j][:, kc, 0:nb], in0=xb[j][:, kc, 0:nb],
                                                                              in1=rs[:, 0:nb], op=ALU.mult),
                             reads=[f"nmx{j}", "nmrs"], writes=[f"nmx{j}"])
                    for kc in range(KC):
                        if final:
                            P.op("act", lambda e, j=j, kc=kc, nb=nb: e.activation(
                                out=hb[j][:, kc, 0:nb], in_=xb[j][:, kc, 0:nb], func=AF.Copy, scale=gft[:, kc:kc + 1]),
                                reads=[f"nmx{j}", "gft"], writes=[f"nmh{j}"])
                        else:
                            P.op("act", lambda e, j=j, kc=kc, nb=nb, col=col: e.activation(
                                out=hb[j][:, kc, 0:nb], in_=xb[j][:, kc, 0:nb], func=AF.Identity,
                                scale=Asc[:, kc, col:col + 1], bias=modv[:, shift_base + kc, col:col + 1]),
                                reads=[f"nmx{j}", "A1", "A2", "modv"], writes=[f"nmh{j}"])
                    if final:
                        P.dma("pool", outT[:, t0:t0 + nb].rearrange("(c p) n -> p c n", p=128), hb[j][:, :, 0:nb],
                              reads=[f"nmh{j}"], writes=["outT"])
                    else:
                        P.dma("pool", hT[bi], hb[j][:], reads=[f"nmh{j}"], writes=["hT"])
                P.barrier()

        def gemm_stage(src_blocks, src_key, kc_n, wsrc, mc_n, epilogue, extra=None, G=4, nib=2, pair=False):
            with contextlib.ExitStack() as st:
                if pair:
                    G, nib = 2, 4
                ib = [SB(st, f"gm_i{i}", [128, kc_n, 512], BF16) for i in range(nib)]
                wt = [SB(st, f"gm_w{i}", [128, G, kc_n, 128], BF16) for i in range(2)]
                ctx = extra(st) if extra else None
                wi = 0
                step = 2 if pair else 1
                for b0 in range(0, len(blks), step):
                    grp = list(range(b0, min(b0 + step, len(blks))))
                    for bi in grp:
                        P.dma("sp", ib[bi % nib][:], src_blocks[bi], reads=[src_key], writes=[f"gmi{bi % nib}"])
                    for m0 in range(0, mc_n, G):
                        g = min(G, mc_n - m0)
                        wj = wi % 2
                        if pair:
                            pgrp = 4 * (wi % 2)
                        else:
                            pgrp = 4 * (wi % 2) if G == 4 else 2 * (wi % 4) if G == 2 else (wi % 8)
                        wi += 1
                        P.dma("sp", wt[wj][:, 0:g], wsrc[m0:m0 + g].rearrange("g p k m -> p g k m"),
                              reads=["wsrc"], writes=[f"gmw{wj}"])
                        for k_, bi in enumerate(grp):
                            nb = blks[bi][1]
                            j = bi % nib
                            pbase = pgrp + 2 * k_ if pair else pgrp
                            for gi in range(g):
                                pi = pbase + gi
                                for kc in range(kc_n):
                                    P.op("pe", lambda e, wj=wj, j=j, kc=kc, pi=pi, nb=nb, gi=gi: e.matmul(
                                        ps[pi][:, 0:nb], wt[wj][:, gi, kc, :], ib[j][:, kc, 0:nb], start=(kc == 0), stop=(kc == kc_n - 1)),
                                        reads=[f"gmw{wj}", f"gmi{j}"], writes=[f"psg{pbase}"])
                        for k_, bi in enumerate(grp):
                            pbase = pgrp + 2 * k_ if pair else pgrp
                            if pair:
                                epilogue(ctx, bi, blks[bi], m0, g, pbase, 2 * k_, (pgrp // 4) % 2)
                            else:
                                epilogue(ctx, bi, blks[bi], m0, g, pbase)
                P.barrier()

        PI = float(np.pi)

        def s5_stage(l):
            with contextlib.ExitStack() as st:
                c128 = SB(st, "s5_c128", [128, 4, 128], F32)
                ck = SB(st, "s5_ck", [128, 4], F32)
                negpi = SB(st, "s5_negpi", [128, 1], F32)
                are = SB(st, "s5_are", [128, 2, 16], F32)
                aim = SB(st, "s5_aim", [128, 2, 16], F32)
                stp = SB(st, "s5_stp", [128, 2, 16], F32)
                lre = SB(st, "s5_lre", [128, 2, 16], F32)
                lim = SB(st, "s5_lim", [128, 2, 16], F32)
                abr = SB(st, "s5_abr", [128, 2, 16], F32)
                abi = SB(st, "s5_abi", [128, 2, 16], F32)
                cfr = SB(st, "s5_cfr", [128, 2, 16], F32)
                cfi = SB(st, "s5_cfi", [128, 2, 16], F32)
                w1 = SB(st, "s5_w1", [128, 2, 16], F32)
                w2 = SB(st, "s5_w2", [128, 2, 16], F32)
                w3 = SB(st, "s5_w3", [128, 2, 16], F32)
                ddt = SB(st, "s5_dd", [128, 4], F32)
                uT = SB(st, "s5_u", [128, T], F32)
                acc = SB(st, "s5_acc", [128, T], F32)
                rows = SB(st, "s5_rows", [128, 4, 512], F32)
                Pm = SB(st, "s5_Pm", [128, 2, 512], F32)
                Pp = SB(st, "s5_Pp", [128, 2, 4, 128], F32)
                Bb = SB(st, "s5_Bb", [128, 2, 512], F32)
                Braw = SB(st, "s5_Braw", [128, 2, 512], F32)
                Cb = SB(st, "s5_Cb", [128, 2, 4, 128], F32)
                ph = SB(st, "s5_ph", [128, 512], F32)
                ph2 = SB(st, "s5_ph2", [128, 512], F32)
                mg = SB(st, "s5_mg", [128, 512], F32)
                tt = [SB(st, f"s5_t{i}", [128, 512], F32) for i in range(4)]
                Wt = [SB(st, f"s5_W{i}", [128, 2, 512], F32) for i in range(2)]
                tmp = SB(st, "s5_tmp", [128, 2, 4, 128], F32)
                Xt = SB(st, "s5_X", [128, 2, 4, 128], F32)
                xx = [SB(st, f"s5_x{i}", [128, 4, 128], F32) for i in range(4)]
                aS = SB(st, "s5_aS", [128, 2, 4], F32)
                sw = [SB(st, f"s5_sw{i}", [128, 4], F32) for i in range(4)]
                P.dma("sp", c128[:], cst128, writes=["c128"])
                P.dma("sp", ck[:], cstk, writes=["ck"])
                P.op("pool", lambda e: e.memset(negpi[:], -PI), writes=["negpi"])
                P.dma("sp", are[:], s5_are[l], writes=["are"])
                P.dma("sp", aim[:], s5_aim[l], writes=["aim"])
                P.dma("sp", stp[:], s5_lst[l], writes=["stp"])
                P.dma("sp", ddt[:], s5_dd[l], writes=["ddt"])
                V = lambda fn, r, w: P.op("dve", fn, reads=r, writes=w)

                I32 = mybir.dt.int32
                sc_i = SB(st, "s5_sci", [128, 512], I32)
                sc_r = SB(st, "s5_scr", [128, 512], F32)
                sc_f = SB(st, "s5_scf", [128, 512], F32)
                sc_m = SB(st, "s5_scm", [128, 512], F32)

                def sincos(theta_ap, sin_out, cos_out, scratch, rkeys, skey, wkeys):
                    shp = list(theta_ap.shape)
                    n = 1
                    for v in shp[1:]:
                        n *= v
                    flat = lambda ap: ap if len(shp) == 2 else ap.rearrange("p a b -> p (a b)")
                    th = flat(theta_ap)
                    ri_, rr_, rf_, rm_ = sc_i[:, 0:n], sc_r[:, 0:n], sc_f[:, 0:n], sc_m[:, 0:n]
                    for out_ap, off, wk in ((sin_out, 0.0, wkeys[0]), (cos_out, 0.25, wkeys[1])):
                        if out_ap is None:
                            continue
                        V(lambda e, off=off: e.tensor_scalar(out=rr_, in0=th, scalar1=1.0 / (2.0 * PI), scalar2=off, op0=ALU.mult, op1=ALU.add),
                          rkeys, ["scr"])
                        V(lambda e: e.tensor_copy(out=ri_, in_=rr_), ["scr"], ["sci"])
                        V(lambda e: e.tensor_copy(out=rf_, in_=ri_), ["sci"], ["scf"])
                        V(lambda e: e.tensor_tensor(out=rr_, in0=rr_, in1=rf_, op=ALU.subtract), ["scr", "scf"], ["scr"])
                        V(lambda e: e.tensor_scalar(out=rm_, in0=rr_, scalar1=0.5, scalar2=None, op0=ALU.is_gt), ["scr"], ["scm"])
                        V(lambda e: e.tensor_tensor(out=rr_, in0=rr_, in1=rm_, op=ALU.subtract), ["scr", "scm"], ["scr"])
                        V(lambda e: e.tensor_scalar(out=rm_, in0=rr_, scalar1=-0.5, scalar2=None, op0=ALU.is_lt), ["scr"], ["scm"])
                        V(lambda e: e.tensor_tensor(out=rr_, in0=rr_, in1=rm_, op=ALU.add), ["scr", "scm"], ["scr"])
                        P.op("act", lambda e, out_ap=out_ap: e.activation(out=flat(out_ap), in_=rr_, func=AF.Sin, scale=2.0 * PI),
                             reads=["scr"], writes=[wk])

                P.op("act", lambda e: e.activation(out=stp[:], in_=stp[:], func=AF.Exp), reads=["stp"], writes=["stp"])
                V(lambda e: e.tensor_tensor(out=lre[:], in0=are[:], in1=stp[:], op=ALU.mult), ["are", "stp"], ["lre"])
                V(lambda e: e.tensor_tensor(out=lim[:], in0=aim[:], in1=stp[:], op=ALU.mult), ["aim", "stp"], ["lim"])
                P.op("act", lambda e: e.activation(out=w1[:], in_=lre[:], func=AF.Exp), reads=["lre"], writes=["w1"])
                sincos(lim[:], w2[:], w3[:], cfr[:], ["lim"], "cfr", ["w2", "w3"])
                V(lambda e: e.tensor_tensor(out=abr[:], in0=w1[:], in1=w3[:], op=ALU.mult), ["w1", "w3"], ["abr"])
                V(lambda e: e.tensor_tensor(out=abi[:], in0=w1[:], in1=w2[:], op=ALU.mult), ["w1", "w2"], ["abi"])
                V(lambda e: e.tensor_scalar(out=w1[:], in0=abr[:], scalar1=-1.0, scalar2=None, op0=ALU.add), ["abr"], ["w1"])
                V(lambda e: e.tensor_tensor(out=w2[:], in0=w1[:], in1=are[:], op=ALU.mult), ["w1", "are"], ["w2"])
                V(lambda e: e.tensor_tensor(out=w3[:], in0=abi[:], in1=aim[:], op=ALU.mult), ["abi", "aim"], ["w3"])
                V(lambda e: e.tensor_tensor(out=cfr[:], in0=w2[:], in1=w3[:], op=ALU.add), ["w2", "w3"], ["cfr"])
                V(lambda e: e.tensor_tensor(out=w2[:], in0=abi[:], in1=are[:], op=ALU.mult), ["abi", "are"], ["w2"])
                V(lambda e: e.tensor_tensor(out=w3[:], in0=w1[:], in1=aim[:], op=ALU.mult), ["w1", "aim"], ["w3"])
                V(lambda e: e.tensor_tensor(out=cfi[:], in0=w2[:], in1=w3[:], op=ALU.subtract), ["w2", "w3"], ["cfi"])
                V(lambda e: e.tensor_tensor(out=w2[:], in0=are[:], in1=are[:], op=ALU.mult), ["are"], ["w2"])
                V(lambda e: e.tensor_tensor(out=w3[:], in0=aim[:], in1=aim[:], op=ALU.mult), ["aim"], ["w3"])
                V(lambda e: e.tensor_tensor(out=w2[:], in0=w2[:], in1=w3[:], op=ALU.add), ["w2", "w3"], ["w2"])
                V(lambda e: e.reciprocal(out=w2[:], in_=w2[:]), ["w2"], ["w2"])
                V(lambda e: e.tensor_tensor(out=cfr[:], in0=cfr[:], in1=w2[:], op=ALU.mult), ["cfr", "w2"], ["cfr"])
                V(lambda e: e.tensor_tensor(out=cfi[:], in0=cfi[:], in1=w2[:], op=ALU.mult), ["cfi", "w2"], ["cfi"])
                for d in range(2):
                    for ri, (tl, key) in enumerate(((lre, "lre"), (lim, "lim"), (cfr, "cfr"), (cfi, "cfi"))):
                        P.dma("pool", s5rows[d, ri].rearrange("(s p) -> p s", p=128), tl[:, d, :], reads=[key], writes=["s5rows"],
                              allow_slow_non_contiguous=True)

                for bk in range(4):
                    for t0 in range(0, T, 2112):
                        n = min(2112, T - t0)
                        P.dma("sp", uT[:, t0:t0 + n], pxA[bk * 128:(bk + 1) * 128, t0:t0 + n], reads=["pxT"], writes=["s5u"])
                    for d in range(2):
                        for ri in range(4):
                            P.dma("sp", rows[:, ri, :], s5rows[d, ri:ri + 1, bk * 512:(bk + 1) * 512].broadcast_to([128, 512]),
                                  reads=["s5rows"], writes=["rows"])
                        P.dma("sp", Braw[:, 0, :], s5_bre[l, d, bk], writes=["Braw"])
                        P.dma("sp", Braw[:, 1, :], s5_bim[l, d, bk], writes=["Braw"])
                        P.dma("sp", Cb[:, 0], s5_cre[l, d, bk], writes=["Cb"])
                        P.dma("sp", Cb[:, 1], s5_cim[l, d, bk], writes=["Cb"])
                        V(lambda e: e.tensor_scalar(out=Cb[:, 1], in0=Cb[:, 1], scalar1=-1.0, scalar2=None, op0=ALU.mult), ["Cb"], ["Cb"])
                        V(lambda e: e.tensor_tensor(out=tt[0][:], in0=Braw[:, 0, :], in1=rows[:, 2, :], op=ALU.mult), ["Braw", "rows"], ["t0"])
                        V(lambda e: e.tensor_tensor(out=tt[1][:], in0=Braw[:, 1, :], in1=rows[:, 3, :], op=ALU.mult), ["Braw", "rows"], ["t1"])
                        V(lambda e: e.tensor_tensor(out=Bb[:, 0, :], in0=tt[0][:], in1=tt[1][:], op=ALU.subtract), ["t0", "t1"], ["Bb"])
                        V(lambda e: e.tensor_tensor(out=tt[0][:], in0=Braw[:, 0, :], in1=rows[:, 3, :], op=ALU.mult), ["Braw", "rows"], ["t0"])
                        V(lambda e: e.tensor_tensor(out=tt[1][:], in0=Braw[:, 1, :], in1=rows[:, 2, :], op=ALU.mult), ["Braw", "rows"], ["t1"])
                        V(lambda e: e.tensor_tensor(out=Bb[:, 1, :], in0=tt[0][:], in1=tt[1][:], op=ALU.add), ["t0", "t1"], ["Bb"])
                        V(lambda e, d=d: e.tensor_scalar(out=ph[:], in0=rows[:, 1, :], scalar1=ck[:, d:d + 1], scalar2=None, op0=ALU.mult),
                          ["rows", "ck"], ["ph"])
                        P.op("act", lambda e, d=d: e.activation(out=mg[:], in_=rows[:, 0, :], func=AF.Exp, scale=ck[:, 2 + d:3 + d]),
                             reads=["rows", "ck"], writes=["mg"])
                        sincos(ph[:], tt[0][:], tt[1][:], ph2[:], ["ph"], "ph2", ["t0", "t1"])
                        V(lambda e: e.tensor_tensor(out=Pm[:, 0, :], in0=mg[:], in1=tt[1][:], op=ALU.mult), ["mg", "t1"], ["Pm"])
                        V(lambda e: e.scalar_tensor_tensor(out=Pm[:, 1, :], in0=mg[:], scalar=-1.0, in1=tt[0][:], op0=ALU.mult, op1=ALU.mult),
                          ["mg", "t0"], ["Pm"])
                        for s_ in range(4):
                            sl = bk * 4 + s_
                            V(lambda e, d=d, sl=sl, s_=s_: e.tensor_scalar(out=ph[:, s_ * 128:(s_ + 1) * 128], in0=c128[:, 2 + d, :],
                                                                         scalar1=lim[:, d, sl:sl + 1], scalar2=None, op0=ALU.mult),
                              ["c128", "lim"], ["ph"])
                            P.op("act", lambda e, d=d, sl=sl, s_=s_: e.activation(out=mg[:, s_ * 128:(s_ + 1) * 128], in_=c128[:, 2 + d, :],
                                                                                  func=AF.Exp, scale=lre[:, d, sl:sl + 1]),
                                 reads=["c128", "lre"], writes=["mg"])
                        sincos(ph[:], tt[0][:], tt[1][:], ph2[:], ["ph"], "ph2", ["t0", "t1"])
                        V(lambda e: e.tensor_tensor(out=Pp[:, 0].rearrange("p s t -> p (s t)"), in0=mg[:], in1=tt[1][:], op=ALU.mult),
                          ["mg", "t1"], ["Pp"])
                        V(lambda e: e.tensor_tensor(out=Pp[:, 1].rearrange("p s t -> p (s t)"), in0=mg[:], in1=tt[0][:], op=ALU.mult),
                          ["mg", "t0"], ["Pp"])
                        P.op("pool", lambda e: e.memset(aS[:], 0.0), writes=["aS"])
                        tri = c128[:, d, :]
                        nq = nctx // 128
                        chunks = [nx + 128 * q for q in range(nq)] + [128 * q for q in range(nx // 128)]
                        if d == 1:
                            chunks = [nx + 128 * q for q in range(nq)][::-1] + [128 * q for q in range(nx // 128)][::-1]
                        last = 127 if d == 0 else 0
                        for it, t0 in enumerate(chunks):
                            b = it % 2
                            pz = 2 + 2 * b
                            for ri in range(2):
                                P.op("pe", lambda e, ri=ri, t0=t0: e.matmul(ps[ri][:, 0:512], uT[:, t0:t0 + 128], Bb[:, ri, :], start=True, stop=True),
                                     reads=["s5u", "Bb"], writes=[f"s5p{ri}"])
                            V(lambda e: e.tensor_tensor(out=tt[0][:], in0=Pm[:, 0, :], in1=ps[0][:, 0:512], op=ALU.mult), ["Pm", "s5p0"], ["t0"])
                            V(lambda e: e.tensor_tensor(out=tt[1][:], in0=Pm[:, 1, :], in1=ps[1][:, 0:512], op=ALU.mult), ["Pm", "s5p1"], ["t1"])
                            V(lambda e: e.tensor_tensor(out=tt[2][:], in0=Pm[:, 0, :], in1=ps[1][:, 0:512], op=ALU.mult), ["Pm", "s5p1"], ["t2"])
                            V(lambda e: e.tensor_tensor(out=tt[3][:], in0=Pm[:, 1, :], in1=ps[0][:, 0:512], op=ALU.mult), ["Pm", "s5p0"], ["t3"])
                            P.op("pool", lambda e, b=b: e.tensor_tensor(out=Wt[b][:, 0, :], in0=tt[0][:], in1=tt[1][:], op=ALU.subtract),
                                 reads=["t0", "t1"], writes=[f"W{b}"])
                            P.op("pool", lambda e, b=b: e.tensor_tensor(out=Wt[b][:, 1, :], in0=tt[2][:], in1=tt[3][:], op=ALU.add),
                                 reads=["t2", "t3"], writes=[f"W{b}"])
                            for ri in range(2):
                                for s_ in range(4):
                                    P.op("pe", lambda e, b=b, ri=ri, s_=s_, pz=pz, tri=tri: e.matmul(
                                        ps[pz + ri][:, s_ * 128:(s_ + 1) * 128], Wt[b][:, ri, s_ * 128:(s_ + 1) * 128], tri, start=True, stop=True),
                                        reads=[f"W{b}", "c128"], writes=[f"s5p{pz + ri}"])
                            for ri in range(2):
                                V(lambda e, ri=ri, pz=pz: e.tensor_tensor(out=tmp[:, ri], in0=ps[pz + ri][:, 0:512].rearrange("p (s t) -> p s t", s=4),
                                                                        in1=aS[:, ri, :].unsqueeze(2).to_broadcast([128, 4, 128]), op=ALU.add),
                                  [f"s5p{pz + ri}", "aS"], ["tmp"])
                            V(lambda e: e.tensor_tensor(out=xx[0][:], in0=Pp[:, 0], in1=tmp[:, 0], op=ALU.mult), ["Pp", "tmp"], ["x0"])
                            V(lambda e: e.tensor_tensor(out=xx[1][:], in0=Pp[:, 1], in1=tmp[:, 1], op=ALU.mult), ["Pp", "tmp"], ["x1"])
                            P.op("pool", lambda e: e.tensor_tensor(out=xx[2][:], in0=Pp[:, 0], in1=tmp[:, 1], op=ALU.mult), reads=["Pp", "tmp"], writes=["x2"])
                            P.op("pool", lambda e: e.tensor_tensor(out=xx[3][:], in0=Pp[:, 1], in1=tmp[:, 0], op=ALU.mult), reads=["Pp", "tmp"], writes=["x3"])
                            V(lambda e: e.tensor_tensor(out=Xt[:, 0], in0=xx[0][:], in1=xx[1][:], op=ALU.subtract), ["x0", "x1"], ["X"])
                            P.op("pool", lambda e: e.tensor_tensor(out=Xt[:, 1], in0=xx[2][:], in1=xx[3][:], op=ALU.add), reads=["x2", "x3"], writes=["X"])
                            ar_, ai_ = abr[:, d, bk * 4:bk * 4 + 4], abi[:, d, bk * 4:bk * 4 + 4]
                            xr_, xi_ = Xt[:, 0, :, last], Xt[:, 1, :, last]
                            V(lambda e, ar_=ar_, xr_=xr_: e.tensor_tensor(out=sw[0][:], in0=ar_, in1=xr_, op=ALU.mult), ["abr", "X"], ["sw0"])
                            V(lambda e, ai_=ai_, xi_=xi_: e.tensor_tensor(out=sw[1][:], in0=ai_, in1=xi_, op=ALU.mult), ["abi", "X"], ["sw1"])
                            V(lambda e, ar_=ar_, xi_=xi_: e.tensor_tensor(out=sw[2][:], in0=ar_, in1=xi_, op=ALU.mult), ["abr", "X"], ["sw2"])
                            V(lambda e, ai_=ai_, xr_=xr_: e.tensor_tensor(out=sw[3][:], in0=ai_, in1=xr_, op=ALU.mult), ["abi", "X"], ["sw3"])
                            V(lambda e: e.tensor_tensor(out=aS[:, 0, :], in0=sw[0][:], in1=sw[1][:], op=ALU.subtract), ["sw0", "sw1"], ["aS"])
                            V(lambda e: e.tensor_tensor(out=aS[:, 1, :], in0=sw[2][:], in1=sw[3][:], op=ALU.add), ["sw2", "sw3"], ["aS"])
                            for ri in range(2):
                                for s_ in range(4):
                                    P.op("pe", lambda e, ri=ri, s_=s_: e.matmul(ps[6][:, 0:128], Cb[:, ri, s_, :], Xt[:, ri, s_, :],
                                                                                start=(ri == 0 and s_ == 0), stop=(ri == 1 and s_ == 3)),
                                         reads=["Cb", "X"], writes=["s5p6"])
                            if d == 0:
                                V(lambda e, t0=t0: e.tensor_copy(out=acc[:, t0:t0 + 128], in_=ps[6][:, 0:128]), ["s5p6"], ["s5acc"])
                            else:
                                V(lambda e, t0=t0: e.tensor_tensor(out=acc[:, t0:t0 + 128], in0=acc[:, t0:t0 + 128], in1=ps[6][:, 0:128], op=ALU.add),
                                  ["s5p6", "s5acc"], ["s5acc"])
                    for bi, (t0, nb, col, _, _) in enumerate(blks):
                        yv = acc[:, t0:t0 + nb]
                        V(lambda e, yv=yv, t0=t0, nb=nb, bk=bk: e.scalar_tensor_tensor(out=yv, in0=uT[:, t0:t0 + nb], scalar=ddt[:, bk:bk + 1], in1=yv,
                                                                                     op0=ALU.mult, op1=ALU.add), ["s5u", "ddt", "s5acc"], ["s5acc"])
                        P.op("pool", lambda e, yv=yv, nb=nb: e.tensor_tensor(out=tt[0][:, 0:nb], in0=yv, in1=yv, op=ALU.mult), reads=["s5acc"], writes=["t0"])
                        V(lambda e, nb=nb: e.tensor_scalar(out=tt[0][:, 0:nb], in0=tt[0][:, 0:nb], scalar1=0.044715, scalar2=1.0, op0=ALU.mult, op1=ALU.add),
                          ["t0"], ["t0"])
                        P.op("pool", lambda e, yv=yv, nb=nb: e.tensor_tensor(out=tt[0][:, 0:nb], in0=tt[0][:, 0:nb], in1=yv, op=ALU.mult),
                             reads=["t0", "s5acc"], writes=["t0"])
                        P.op("act", lambda e, nb=nb: e.activation(out=tt[0][:, 0:nb], in_=tt[0][:, 0:nb], func=AF.Sigmoid, scale=1.5957691216057308),
                             reads=["t0"], writes=["t0"])
                        ob = Wt[bi % 2][:, 0, :].bitcast(BF16)
                        V(lambda e, yv=yv, nb=nb, ob=ob: e.tensor_tensor(out=ob[:, 0:nb], in0=tt[0][:, 0:nb], in1=yv, op=ALU.mult),
                          ["t0", "s5acc"], [f"W{bi % 2}"])
                        P.dma("pool", ygT[bi, :, bk, 0:nb], ob[:, 0:nb], reads=[f"W{bi % 2}"], writes=["ygT"])
                P.barrier()
            with contextlib.ExitStack() as st:
                gwf = SB(st, "s5_gwf", [128, 4, 4, 128], F32)
                gwb = SB(st, "s5_gwb", [128, 4, 4, 128], BF16)
                gbt = SB(st, "s5_gbt", [128, 4], F32)
                yb_ = [SB(st, f"s5_yb{i}", [128, 4, 512], BF16) for i in range(2)]
                sg = [SB(st, f"s5_sg{i}", [128, 512], F32) for i in range(2)]
                ot = [SB(st, f"s5_ot{i}", [128, 512], BF16) for i in range(2)]
                P.dma("sp", gwf[:], s5_gw[l], writes=["gwf"])
                P.dma("sp", gbt[:], s5_gb[l], writes=["gbt"])
                P.op("dve", lambda e: e.tensor_copy(out=gwb[:], in_=gwf[:]), reads=["gwf"], writes=["gwb"])
                it = 0
                for bi, (t0, nb, col, _, _) in enumerate(blks):
                    j = bi % 2
                    P.dma("sp", yb_[j][:], ygT[bi], reads=["ygT"], writes=[f"yb{j}"])
                    for m in range(4):
                        k_ = it % 2
                        it += 1
                        for kc in range(4):
                            P.op("pe", lambda e, j=j, m=m, kc=kc, k_=k_, nb=nb: e.matmul(ps[k_][:, 0:nb], gwb[:, m, kc, :], yb_[j][:, kc, 0:nb],
                                                                                      start=(kc == 0), stop=(kc == 3)),
                                 reads=["gwb", f"yb{j}"], writes=[f"glu{k_}"])
                        P.op("act", lambda e, k_=k_, m=m, nb=nb: e.activation(out=sg[k_][:, 0:nb], in_=ps[k_][:, 0:nb], func=AF.Sigmoid,
                                                                             bias=gbt[:, m:m + 1]),
                             reads=[f"glu{k_}", "gbt"], writes=[f"sg{k_}"])
                        P.op("dve", lambda e, k_=k_, j=j, m=m, nb=nb: e.tensor_tensor(out=ot[k_][:, 0:nb], in0=yb_[j][:, m, 0:nb], in1=sg[k_][:, 0:nb],
                                                                                     op=ALU.mult),
                             reads=[f"yb{j}", f"sg{k_}"], writes=[f"ot{k_}"])
                        P.dma("pool", yT[bi, :, m, 0:nb], ot[k_][:, 0:nb], reads=[f"ot{k_}"], writes=["yT"])
                P.barrier()


        def hyena_stage(l, with_ctx):
            seqs = [q for q in HSEQ if (q[0] == "x" or with_ctx)]
            import os
            if os.environ.get("HY_REV"):
                seqs = seqs[::-1]
            def h0_pass(nm, toff, n_):
                hc = hyc[nm]
                N2, A = hc["N2"], hc["A"]
                NT = 2 * n_
                with contextlib.ExitStack() as st:
                    I32 = mybir.dt.int32
                    w1t = SB(st, "hy_w1", [33, 64], F32)
                    w2t = SB(st, "hy_w2", [64, 64], F32)
                    w3t = SB(st, "hy_w3", [65, 2, 4, 256], F32)
                    fbt = SB(st, "hy_fbt", [64, 6], F32)
                    fb1 = SB(st, "hy_fb1", [64, 2], F32)
                    h2a = SB(st, "hy_h2a", [65, NT], BF16)
                    w3b = SB(st, "hy_w3b", [65, 2, 4, 256], BF16)
                    ft = [SB(st, f"hy_ft{i}", [33, 512], F32) for i in range(2)]
                    th = SB(st, "hy_th", [64, 512], F32)
                    h1 = SB(st, "hy_h1", [64, 512], F32)
                    sci = SB(st, "hy_sci", [128, 512], I32)
                    scr = SB(st, "hy_scr", [128, 512], F32)
                    scf = SB(st, "hy_scf", [128, 512], F32)
                    scm = SB(st, "hy_scm", [128, 512], F32)
                    mskt = SB(st, "hy_msk", [N2, 2], F32)
                    filt = SB(st, "hy_filt", [N2, 128, 128], F32)
                    dct = [SB(st, f"hy_dct{i}", [N2, 16, 128], F32) for i in range(2)]
                    red = SB(st, "hy_red", [N2, 128], F32)
                    inv = SB(st, "hy_inv", [N2, 128], F32)
                    F2f = SB(st, "hy_F2f", [N2, 2, N2], F32)
                    TW = SB(st, "hy_TW", [128, 2, N2], F32)
                    F1 = SB(st, "hy_F1t", [128, 3, 128], F32)
                    CS = 16
                    Y1t = SB(st, "hy_Y1t", [128, CS, 2, N2], F32)
                    xs = [SB(st, f"hy_xs{i}", [128, 2, 512], F32) for i in range(2)]
                    tq = [SB(st, f"hy_tq{i}", [128, 512], F32) for i in range(4)]
                    V = lambda fn, r, w: P.op("dve", fn, reads=r, writes=w)

                    def sin_red(theta_ap, np_, ncol, out_ap, rkeys, wkey):
                        rr_, ri_, rf_, rm_ = scr[0:np_, 0:ncol], sci[0:np_, 0:ncol], scf[0:np_, 0:ncol], scm[0:np_, 0:ncol]
                        V(lambda e: e.tensor_scalar(out=rr_, in0=theta_ap, scalar1=1.0 / (2.0 * PI), scalar2=None, op0=ALU.mult), rkeys, ["hscr"])
                        V(lambda e: e.tensor_copy(out=ri_, in_=rr_), ["hscr"], ["hsci"])
                        V(lambda e: e.tensor_copy(out=rf_, in_=ri_), ["hsci"], ["hscf"])
                        V(lambda e: e.tensor_tensor(out=rr_, in0=rr_, in1=rf_, op=ALU.subtract), ["hscr", "hscf"], ["hscr"])
                        V(lambda e: e.tensor_scalar(out=rm_, in0=rr_, scalar1=0.5, scalar2=None, op0=ALU.is_gt), ["hscr"], ["hscm"])
                        V(lambda e: e.tensor_tensor(out=rr_, in0=rr_, in1=rm_, op=ALU.subtract), ["hscr", "hscm"], ["hscr"])
                        V(lambda e: e.tensor_scalar(out=rm_, in0=rr_, scalar1=-0.5, scalar2=None, op0=ALU.is_lt), ["hscr"], ["hscm"])
                        V(lambda e: e.tensor_tensor(out=rr_, in0=rr_, in1=rm_, op=ALU.add), ["hscr", "hscm"], ["hscr"])
                        P.op("act", lambda e: e.activation(out=out_ap, in_=rr_, func=AF.Sin, scale=2.0 * PI), reads=["hscr"], writes=[wkey])

                    P.dma("sp", w1t[:], hy_w1[l], writes=["hw1"])
                    P.dma("sp", w2t[:], hy_w2[l], writes=["hw2"])
                    P.dma("sp", w3t[:], hy_w3[l], writes=["hw3f"])
                    P.op("pool", lambda e: e.tensor_copy(out=w3b[:], in_=w3t[:]), reads=["hw3f"], writes=["hw3"])
                    P.dma("sp", fbt[:], hy_fb[l], writes=["hfb"])
                    P.dma("sp", mskt[:], hc["msk"], writes=["hmsk"])
                    P.dma("sp", F2f[:], hc["F2f"], writes=["hF2f"])
                    P.dma("sp", TW[:], hc["TW"], writes=["hTW"])
                    P.dma("sp", F1[:], hy_F1, writes=["hF1"])
                    V(lambda e: e.tensor_tensor(out=fb1[:, 0:1], in0=fbt[:, 0:1], in1=fbt[:, 1:2], op=ALU.mult), ["hfb"], ["hfb1"])
                    V(lambda e: e.tensor_tensor(out=fb1[:, 1:2], in0=fbt[:, 2:3], in1=fbt[:, 3:4], op=ALU.mult), ["hfb"], ["hfb1"])
                    P.op("pool", lambda e: e.memset(h2a[:], 1.0), writes=["hh2a"])
                    for ci, c0 in enumerate(range(0, NT, 512)):
                        j = ci % 2
                        P.dma("sp", ft[j][:], hc["feats"][:, c0:c0 + 512], writes=[f"hft{j}"])
                        P.op("pe", lambda e, j=j: e.matmul(ps[0][0:64, 0:512], w1t[:], ft[j][:], start=True, stop=True),
                             reads=["hw1", f"hft{j}"], writes=["hp0"])
                        V(lambda e: e.tensor_scalar(out=th[:], in0=ps[0][0:64, 0:512], scalar1=fbt[:, 0:1], scalar2=fb1[:, 0:1], op0=ALU.mult, op1=ALU.add),
                          ["hp0", "hfb", "hfb1"], ["hth"])
                        sin_red(th[:], 64, 512, h1[:], ["hth"], "hh1")
                        P.op("pe", lambda e: e.matmul(ps[1][0:64, 0:512], w2t[:], h1[:], start=True, stop=True), reads=["hw2", "hh1"], writes=["hp1"])
                        V(lambda e: e.tensor_scalar(out=th[:], in0=ps[1][0:64, 0:512], scalar1=fbt[:, 2:3], scalar2=fb1[:, 1:2], op0=ALU.mult, op1=ALU.add),
                          ["hp1", "hfb", "hfb1"], ["hth"])
                        sin_red(th[:], 64, 512, h2a[0:64, c0:c0 + 512], ["hth"], "hh2a")
                    if only == "hy" and nm == "x":
                        V(lambda e: e.tensor_copy(out=h1[:, 0:128], in_=h2a[0:64, 0:128]), ["hh2a"], ["hh1"])
                        V(lambda e: e.tensor_copy(out=h1[:, 128:256], in_=h2a[0:64, NT - 128:NT]), ["hh2a"], ["hh1"])
                        P.dma("pool", dbg_out[:, 2, :], h1[:, 0:128], reads=["hh1"], writes=["dbg"])
                        P.dma("pool", dbg_out[:, 0, :], h1[:, 128:256], reads=["hh1"], writes=["dbg"])
                    h2v = h2a[:].rearrange("k (a b) -> k a b", b=128)
                    for o in range(2):
                        for cb in range(4):
                            for b_ in range(128):
                                pb = 2 + (b_ % 2)
                                P.op("pe", lambda e, b_=b_, pb=pb, o=o, cb=cb: e.matmul(ps[pb][0:N2, 0:256], h2v[:, :, b_], w3b[:, o, cb, :], start=True, stop=True),
                                     reads=["hh2a", "hw3"], writes=[f"hp{pb}"])
                                V(lambda e, b_=b_, pb=pb: e.tensor_scalar(out=filt[:, :, b_], in0=ps[pb][0:N2, 0:128], scalar1=mskt[:, 0:1], scalar2=None, op0=ALU.mult),
                                  [f"hp{pb}", "hmsk"], ["hfilt"])
                                V(lambda e, b_=b_, pb=pb: e.scalar_tensor_tensor(out=filt[:, :, b_], in0=ps[pb][0:N2, 128:256], scalar=mskt[:, 1:2], in1=filt[:, :, b_],
                                                                                 op0=ALU.mult, op1=ALU.add), [f"hp{pb}", "hmsk", "hfilt"], ["hfilt"])
                            for pc_ in range(8):
                                j = pc_ % 2
                                P.dma("sp", dct[j][:], hc["dec"][cb, :, pc_ * 16:(pc_ + 1) * 16, :], writes=[f"hdct{j}"])
                                eng = "dve" if pc_ % 2 == 0 else "pool"
                                P.op(eng, lambda e, j=j, pc_=pc_: e.tensor_tensor(out=filt[:, pc_ * 16:(pc_ + 1) * 16, :], in0=filt[:, pc_ * 16:(pc_ + 1) * 16, :],
                                                                                 in1=dct[j][:], op=ALU.mult), reads=["hfilt", f"hdct{j}"], writes=["hfilt"])
                            for c_ in range(128):
                                P.op("act", lambda e, c_=c_: e.activation(out=scr[0:N2, 0:128], in_=filt[:, c_, :], func=AF.Abs, accum_out=red[:, c_:c_ + 1]),
                                     reads=["hfilt"], writes=["hred", "hscr"])
                            P.op("act", lambda e: e.activation(out=scr[0:N2, 0:128], in_=filt[:, 0, :], func=AF.Abs), reads=["hfilt", "hred"], writes=["hred", "hscr"])
                            P.op("pe", lambda e: e.matmul(ps[4][0:N2, 0:128], ones[0:N2, 0:N2], red[:], start=True, stop=True), reads=["ones", "hred"], writes=["hp4"])
                            V(lambda e: e.tensor_scalar(out=inv[:], in0=ps[4][0:N2, 0:128], scalar1=EPS, scalar2=None, op0=ALU.add), ["hp4"], ["hinv"])
                            if only == "hy" and nm == "x" and o == 0 and cb == 0:
                                P.dma("pool", dbg_out[:, 3, :], filt[:, 5, :], reads=["hfilt"], writes=["dbg"])
                            V(lambda e: e.reciprocal(out=inv[:], in_=inv[:]), ["hinv"], ["hinv"])
                            for pc_ in range(8):
                                cs_ = slice(pc_ * 16, (pc_ + 1) * 16)
                                eng = "dve" if pc_ % 2 == 0 else "pool"
                                P.op(eng, lambda e, cs_=cs_: e.tensor_tensor(out=filt[:, cs_, :], in0=filt[:, cs_, :],
                                                                          in1=inv[:, cs_].unsqueeze(2).to_broadcast([N2, 16, 128]), op=ALU.mult),
                                     reads=["hfilt", "hinv"], writes=["hfilt"])
                            fwd_transform(filt, N2, N2, hc, F2f, TW, F1, Y1t, CS, xs, tq,
                                          lambda c0, ncg, xre, xim, o=o, cb=cb, hc=hc: store_spec(hc, o, cb, c0, ncg, xre, xim), V)
                    P.barrier()

            for nm_, toff_, n__ in seqs:
                h0_pass(nm_, toff_, n__)

            with contextlib.ExitStack() as st:
                cwt_ = SB(st, "hy_cw", [128, 12, 3], F32)
                cbt_ = SB(st, "hy_cb", [128, 12], F32)
                aw = [SB(st, f"hy_a{i}", [128, 514], F32) for i in range(2)]
                ow = [SB(st, f"hy_o{i}", [128, 512], F32) for i in range(2)]
                P.dma("sp", cwt_[:], hy_cw[l], writes=["hcw"])
                P.dma("sp", cbt_[:], hy_cb[l], writes=["hcb"])
                it = 0
                for bi, (t0, nb, col, le, re) in enumerate(blks):
                    for jc in range(12):
                        j = it % 2
                        it += 1
                        r0 = 3616 + jc * 128
                        lo = t0 if le else t0 - 1
                        hi = t0 + nb if re else t0 + nb + 1
                        if le or re:
                            P.op("pool", lambda e, j=j: e.memset(aw[j][:], 0.0), writes=[f"hya{j}"])
                        P.dma("sp", aw[j][:, (lo - (t0 - 1)):(hi - (t0 - 1))], pxA[r0:r0 + 128, lo:hi], reads=["pxT"], writes=[f"hya{j}"])
                        P.op("dve", lambda e, j=j, jc=jc, nb=nb: e.tensor_scalar(out=ow[j][:, 0:nb], in0=aw[j][:, 0:nb], scalar1=cwt_[:, jc, 0:1], scalar2=cbt_[:, jc:jc + 1],
                                                                               op0=ALU.mult, op1=ALU.add), reads=[f"hya{j}", "hcw", "hcb"], writes=[f"hyo{j}"])
                        P.op("dve", lambda e, j=j, jc=jc, nb=nb: e.scalar_tensor_tensor(out=ow[j][:, 0:nb], in0=aw[j][:, 1:nb + 1], scalar=cwt_[:, jc, 1:2], in1=ow[j][:, 0:nb],
                                                                                      op0=ALU.mult, op1=ALU.add), reads=[f"hya{j}", "hcw", f"hyo{j}"], writes=[f"hyo{j}"])
                        P.op("dve", lambda e, j=j, jc=jc, nb=nb: e.scalar_tensor_tensor(out=ow[j][:, 0:nb], in0=aw[j][:, 2:nb + 2], scalar=cwt_[:, jc, 2:3], in1=ow[j][:, 0:nb],
                                                                                      op0=ALU.mult, op1=ALU.add), reads=[f"hya{j}", "hcw", f"hyo{j}"], writes=[f"hyo{j}"])
                        P.dma("pool", hvT[jc // 4, (jc % 4) * 128:(jc % 4 + 1) * 128, t0:t0 + nb], ow[j][:, 0:nb], reads=[f"hyo{j}"], writes=["hvT"])
                P.barrier()

            for o in range(2):
                src = hvT[0] if o == 0 else hy1T
                for nm, toff, n_ in seqs:
                    conv_transform(src, nm, toff, n_, o)
                hy_gate(l, o, src, with_ctx)

        def hy_gate(l, o, src, with_ctx):
            if True:
                with contextlib.ExitStack() as st:
                    bt_ = SB(st, "hy_bias", [128, 2, 4], F32)
                    cv = [SB(st, f"hy_cv{i}", [128, 512], F32) for i in range(2)]
                    yv = [SB(st, f"hy_yv{i}", [128, 512], F32) for i in range(2)]
                    gv = [SB(st, f"hy_gv{i}", [128, 512], F32) for i in range(2)]
                    ob = [SB(st, f"hy_ob{i}", [128, 512], BF16) for i in range(2)]
                    P.dma("sp", bt_[:], hy_bias[l], writes=["hbias"])
                    it = 0
                    for bi, (t0, nb, col, le, re) in enumerate(blks):
                        if col == 1 and not with_ctx:
                            continue
                        for cb in range(4):
                            j = it % 2
                            it += 1
                            rs_ = slice(cb * 128, (cb + 1) * 128)
                            P.dma("sp", cv[j][:, 0:nb], hc1T[rs_, t0:t0 + nb], reads=["hc1T"], writes=[f"hcv{j}"])
                            P.dma("sp", yv[j][:, 0:nb], src[rs_, t0:t0 + nb], reads=["hvT", "hy1T"], writes=[f"hyv{j}"])
                            P.dma("sp", gv[j][:, 0:nb], hvT[1 + o, rs_, t0:t0 + nb], reads=["hvT"], writes=[f"hgv{j}"])
                            P.op("dve", lambda e, j=j, nb=nb, o=o, cb=cb: e.scalar_tensor_tensor(out=cv[j][:, 0:nb], in0=yv[j][:, 0:nb], scalar=bt_[:, o, cb:cb + 1],
                                                                                               in1=cv[j][:, 0:nb], op0=ALU.mult, op1=ALU.add),
                                 reads=[f"hyv{j}", "hbias", f"hcv{j}"], writes=[f"hcv{j}"])
                            if o == 0:
                                P.op("pool", lambda e, j=j, nb=nb: e.tensor_tensor(out=cv[j][:, 0:nb], in0=cv[j][:, 0:nb], in1=gv[j][:, 0:nb], op=ALU.mult),
                                     reads=[f"hcv{j}", f"hgv{j}"], writes=[f"hcv{j}"])
                                P.dma("pool", hy1T[rs_, t0:t0 + nb], cv[j][:, 0:nb], reads=[f"hcv{j}"], writes=["hy1T"])
                            else:
                                P.op("pool", lambda e, j=j, nb=nb: e.tensor_tensor(out=ob[j][:, 0:nb], in0=cv[j][:, 0:nb], in1=gv[j][:, 0:nb], op=ALU.mult),
                                     reads=[f"hcv{j}", f"hgv{j}"], writes=[f"hob{j}"])
                                P.dma("pool", yT[bi, :, 4 + cb, 0:nb], ob[j][:, 0:nb], reads=[f"hob{j}"], writes=["yT"])
                    P.barrier()

        def store_spec(hc, o, cb, c0, ncg, xre, xim):
            N2 = hc["N2"]
            P.dma("pool", hc["fspec"][o, cb, :, 0, c0:c0 + ncg, :], xre.rearrange("p (c k) -> p c k", k=N2), reads=["hxs"], writes=["fspec"])
            P.dma("pool", hc["fspec"][o, cb, :, 1, c0:c0 + ncg, :], xim.rearrange("p (c k) -> p c k", k=N2), reads=["hxs"], writes=["fspec"])

        def fwd_transform(Yin, K_rows, N2, hc, F2f, TW, F1, Y1t, CS, xs, tq, consume, V):
            cpb = max(1, min(CS, 512 // (2 * N2)))
            cgB = max(1, min(CS, 512 // N2))
            xi = [0]
            for cs0 in range(0, 128, CS):
                for g0 in range(0, CS, cpb):
                    pb = 5 + ((cs0 // CS * (CS // cpb) + g0 // cpb) % 2)
                    for ci in range(cpb):
                        c = cs0 + g0 + ci
                        P.op("pe", lambda e, c=c, ci=ci, pb=pb: e.matmul(ps[pb][:, ci * 2 * N2:(ci + 1) * 2 * N2], Yin[0:K_rows, c, :],
                                                                          F2f[0:K_rows].rearrange("k r n -> k (r n)"), start=True, stop=True),
                             reads=["hfilt", "hF2f"], writes=[f"hp{pb}"])
                    pv = ps[pb][:, 0:cpb * 2 * N2].rearrange("p (c r k) -> p c r k", c=cpb, r=2)
                    twr = TW[:, 0, :].unsqueeze(1).to_broadcast([128, cpb, N2])
                    twi = TW[:, 1, :].unsqueeze(1).to_broadcast([128, cpb, N2])
                    t0v, t1v, t2v, t3v = [tq[i][:, 0:cpb * N2].rearrange("p (c k) -> p c k", k=N2) for i in range(4)]
                    V(lambda e, pv=pv, twr=twr, t0v=t0v: e.tensor_tensor(out=t0v, in0=pv[:, :, 0, :], in1=twr, op=ALU.mult), [f"hp{pb}", "hTW"], ["htq0"])
                    V(lambda e, pv=pv, twi=twi, t1v=t1v: e.tensor_tensor(out=t1v, in0=pv[:, :, 1, :], in1=twi, op=ALU.mult), [f"hp{pb}", "hTW"], ["htq1"])
                    V(lambda e, pv=pv, twi=twi, t2v=t2v: e.tensor_tensor(out=t2v, in0=pv[:, :, 0, :], in1=twi, op=ALU.mult), [f"hp{pb}", "hTW"], ["htq2"])
                    V(lambda e, pv=pv, twr=twr, t3v=t3v: e.tensor_tensor(out=t3v, in0=pv[:, :, 1, :], in1=twr, op=ALU.mult), [f"hp{pb}", "hTW"], ["htq3"])
                    P.op("pool", lambda e, g0=g0, t0v=t0v, t1v=t1v: e.tensor_tensor(out=Y1t[:, g0:g0 + cpb, 0, :], in0=t0v, in1=t1v, op=ALU.subtract),
                         reads=["htq0", "htq1"], writes=["hY1t"])
                    P.op("pool", lambda e, g0=g0, t2v=t2v, t3v=t3v: e.tensor_tensor(out=Y1t[:, g0:g0 + cpb, 1, :], in0=t2v, in1=t3v, op=ALU.add),
                         reads=["htq2", "htq3"], writes=["hY1t"])
                for g0 in range(0, CS, cgB):
                    j = xi[0] % 2
                    xi[0] += 1
                    yre = Y1t[:, g0:g0 + cgB, 0, :]
                    yim = Y1t[:, g0:g0 + cgB, 1, :]
                    ncol = cgB * N2
                    P.op("pe", lambda e, yre=yre, ncol=ncol: e.matmul(ps[7][:, 0:ncol], F1[:, 0, :], yre, start=True, stop=False), reads=["hF1", "hY1t"], writes=["hp7"])
                    P.op("pe", lambda e, yim=yim, ncol=ncol: e.matmul(ps[7][:, 0:ncol], F1[:, 2, :], yim, start=False, stop=True), reads=["hF1", "hY1t"], writes=["hp7"])
                    V(lambda e, j=j, ncol=ncol: e.tensor_copy(out=xs[j][:, 0, 0:ncol], in_=ps[7][:, 0:ncol]), ["hp7"], ["hxs"])
                    P.op("pe", lambda e, yim=yim, ncol=ncol: e.matmul(ps[7][:, 0:ncol], F1[:, 0, :], yim, start=True, stop=False), reads=["hF1", "hY1t"], writes=["hp7"])
                    P.op("pe", lambda e, yre=yre, ncol=ncol: e.matmul(ps[7][:, 0:ncol], F1[:, 1, :], yre, start=False, stop=True), reads=["hF1", "hY1t"], writes=["hp7"])
                    V(lambda e, j=j, ncol=ncol: e.tensor_copy(out=xs[j][:, 1, 0:ncol], in_=ps[7][:, 0:ncol]), ["hp7"], ["hxs"])
                    consume(cs0 + g0, cgB, xs[j][:, 0, 0:ncol], xs[j][:, 1, 0:ncol])

        def conv_transform(src, nm, toff, n_, o):
            hc = hyc[nm]
            N2, A = hc["N2"], hc["A"]
            with contextlib.ExitStack() as st:
                F2f = SB(st, "hc_F2f", [N2, 2, N2], F32)
                TW = SB(st, "hc_TW", [128, 2, N2], F32)
                F1 = SB(st, "hc_F1", [128, 3, 128], F32)
                F2i = SB(st, "hc_F2i", [N2, 2, A], F32)
                TWT = SB(st, "hc_TWT", [N2, 2, 128], F32)
                CS = 16
                Yin = SB(st, "hc_Yin", [A, 128, 128], F32)
                Y1t = SB(st, "hc_Y1t", [128, CS, 2, N2], F32)
                xs = [SB(st, f"hc_xs{i}", [128, 2, 512], F32) for i in range(2)]
                tq = [SB(st, f"hc_tq{i}", [128, 512], F32) for i in range(4)]
                hf = [SB(st, f"hc_hf{i}", [128, 2, 512], F32) for i in range(2)]
                Zt = SB(st, "hc_Zt", [128, 2, CS, N2], F32)
                Gt = SB(st, "hc_Gt", [N2, CS, 2, 128], F32)
                ot = [SB(st, f"hc_ot{i}", [A, 512], F32) for i in range(2)]
                V = lambda fn, r, w: P.op("dve", fn, reads=r, writes=w)
                P.dma("sp", F2f[:], hc["F2f"], writes=["hF2f"])
                P.dma("sp", TW[:], hc["TW"], writes=["hTW"])
                P.dma("sp", F1[:], hy_F1, writes=["hF1"])
                P.dma("sp", F2i[:], hc["F2i"], writes=["hF2i"])
                P.dma("sp", TWT[:], hc["TWT"], writes=["hTWT"])
                cgB = max(1, min(CS, 512 // N2))
                cpA = max(1, min(CS, 512 // 256))
                state = {"hi": 0, "oi": 0}
                F1a = SB(st, "hc_F1a", [128, 2, 128], F32)
                F1m = SB(st, "hc_F1m", [128, 2, 128], F32)
                V(lambda e: e.tensor_copy(out=F1a[:, 0, :], in_=F1[:, 0, :]), ["hF1"], ["hF1m"])
                V(lambda e: e.tensor_copy(out=F1a[:, 1, :], in_=F1[:, 2, :]), ["hF1"], ["hF1m"])
                V(lambda e: e.tensor_copy(out=F1m[:, 0, :], in_=F1[:, 1, :]), ["hF1"], ["hF1m"])
                V(lambda e: e.tensor_copy(out=F1m[:, 1, :], in_=F1[:, 0, :]), ["hF1"], ["hF1m"])
                for cb in range(4):
                    for c4 in range(0, 128, 32):
                        P.dma("sp", Yin[:, c4:c4 + 32, :],
                              src[cb * 128 + c4:cb * 128 + c4 + 32, toff:toff + n_].rearrange("c (a b) -> a c b", b=128),
                              reads=["hvT", "hy1T"], writes=["hfilt"])

                    def consume(c0, ncg, xre, xim, cb=cb):
                        j = state["hi"] % 2
                        state["hi"] += 1
                        ncol = ncg * N2
                        P.dma("sp", hf[j][:, 0, 0:ncol].rearrange("p (c k) -> p c k", k=N2), hc["fspec"][o, cb, :, 0, c0:c0 + ncg, :], reads=["fspec"], writes=[f"hhf{j}"])
                        P.dma("sp", hf[j][:, 1, 0:ncol].rearrange("p (c k) -> p c k", k=N2), hc["fspec"][o, cb, :, 1, c0:c0 + ncg, :], reads=["fspec"], writes=[f"hhf{j}"])
                        cl = c0 % CS
                        V(lambda e, j=j, ncol=ncol, xre=xre: e.tensor_tensor(out=tq[0][:, 0:ncol], in0=xre, in1=hf[j][:, 0, 0:ncol], op=ALU.mult), ["hxs", f"hhf{j}"], ["htq0"])
                        V(lambda e, j=j, ncol=ncol, xim=xim: e.tensor_tensor(out=tq[1][:, 0:ncol], in0=xim, in1=hf[j][:, 1, 0:ncol], op=ALU.mult), ["hxs", f"hhf{j}"], ["htq1"])
                        V(lambda e, j=j, ncol=ncol, xre=xre: e.tensor_tensor(out=tq[2][:, 0:ncol], in0=xre, in1=hf[j][:, 1, 0:ncol], op=ALU.mult), ["hxs", f"hhf{j}"], ["htq2"])
                        V(lambda e, j=j, ncol=ncol, xim=xim: e.tensor_tensor(out=tq[3][:, 0:ncol], in0=xim, in1=hf[j][:, 0, 0:ncol], op=ALU.mult), ["hxs", f"hhf{j}"], ["htq3"])
                        P.op("pool", lambda e, cl=cl, ncg=ncg, ncol=ncol: e.tensor_tensor(out=Zt[:, 0, cl:cl + ncg, :], in0=tq[0][:, 0:ncol].rearrange("p (c k) -> p c k", k=N2),
                                                                                       in1=tq[1][:, 0:ncol].rearrange("p (c k) -> p c k", k=N2), op=ALU.subtract),
                             reads=["htq0", "htq1"], writes=["hZt"])
                        P.op("pool", lambda e, cl=cl, ncg=ncg, ncol=ncol: e.tensor_tensor(out=Zt[:, 1, cl:cl + ncg, :], in0=tq[2][:, 0:ncol].rearrange("p (c k) -> p c k", k=N2),
                                                                                       in1=tq[3][:, 0:ncol].rearrange("p (c k) -> p c k", k=N2), op=ALU.add),
                             reads=["htq2", "htq3"], writes=["hZt"])
                        if cl + ncg < CS:
                            return
                        cs0 = c0 + ncg - CS
                        for g0 in range(0, CS, cpA):
                            pb = 3 + (g0 // cpA) % 2
                            for ci in range(cpA):
                                cc = g0 + ci
                                P.op("pe", lambda e, cc=cc, ci=ci, pb=pb: e.matmul(ps[pb][0:N2, ci * 256:(ci + 1) * 256], Zt[:, 0, cc, :],
                                                                                   F1a[:].rearrange("k r n -> k (r n)"), start=True, stop=False),
                                     reads=["hZt", "hF1m"], writes=[f"hq{pb}"])
                                P.op("pe", lambda e, cc=cc, ci=ci, pb=pb: e.matmul(ps[pb][0:N2, ci * 256:(ci + 1) * 256], Zt[:, 1, cc, :],
                                                                                   F1m[:].rearrange("k r n -> k (r n)"), start=False, stop=True),
                                     reads=["hZt", "hF1m"], writes=[f"hq{pb}"])
                            pv = ps[pb][0:N2, 0:cpA * 256].rearrange("p (c r b) -> p c r b", c=cpA, r=2)
                            twr = TWT[:, 0, :].unsqueeze(1).to_broadcast([N2, cpA, 128])
                            twi = TWT[:, 1, :].unsqueeze(1).to_broadcast([N2, cpA, 128])
                            u0, u1, u2, u3 = [tq[i][0:N2, 0:cpA * 128].rearrange("p (c b) -> p c b", b=128) for i in range(4)]
                            V(lambda e, pv=pv, twr=twr, u0=u0: e.tensor_tensor(out=u0, in0=pv[:, :, 0, :], in1=twr, op=ALU.mult), [f"hq{pb}", "hTWT"], ["htq0"])
                            V(lambda e, pv=pv, twi=twi, u1=u1: e.tensor_tensor(out=u1, in0=pv[:, :, 1, :], in1=twi, op=ALU.mult), [f"hq{pb}", "hTWT"], ["htq1"])
                            V(lambda e, pv=pv, twi=twi, u2=u2: e.tensor_tensor(out=u2, in0=pv[:, :, 0, :], in1=twi, op=ALU.mult), [f"hq{pb}", "hTWT"], ["htq2"])
                            V(lambda e, pv=pv, twr=twr, u3=u3: e.tensor_tensor(out=u3, in0=pv[:, :, 1, :], in1=twr, op=ALU.mult), [f"hq{pb}", "hTWT"], ["htq3"])
                            P.op("pool", lambda e, g0=g0, u0=u0, u1=u1: e.tensor_tensor(out=Gt[:, g0:g0 + cpA, 0, :], in0=u0, in1=u1, op=ALU.subtract),
                                 reads=["htq0", "htq1"], writes=["hGt"])
                            P.op("pool", lambda e, g0=g0, u2=u2, u3=u3: e.tensor_tensor(out=Gt[:, g0:g0 + cpA, 1, :], in0=u2, in1=u3, op=ALU.add),
                                 reads=["htq2", "htq3"], writes=["hGt"])
                        for g0 in range(0, CS, 4):
                            k_ = state["oi"] % 2
                            state["oi"] += 1
                            P.op("pe", lambda e, g0=g0, k_=k_: e.matmul(ps[1 + k_][0:A, 0:512], F2i[:, 0, :], Gt[:, g0:g0 + 4, 0, :], start=True, stop=False),
                                 reads=["hF2i", "hGt"], writes=[f"hq{1 + k_}"])
                            P.op("pe", lambda e, g0=g0, k_=k_: e.matmul(ps[1 + k_][0:A, 0:512], F2i[:, 1, :], Gt[:, g0:g0 + 4, 1, :], start=False, stop=True),
                                 reads=["hF2i", "hGt"], writes=[f"hq{1 + k_}"])
                            V(lambda e, k_=k_: e.tensor_copy(out=ot[k_][:], in_=ps[1 + k_][0:A, 0:512]), [f"hq{1 + k_}"], [f"hot{k_}"])
                            r0 = cb * 128 + cs0 + g0
                            P.dma("pool", hc1T[r0:r0 + 4, toff:toff + n_].rearrange("c (a b) -> a c b", b=128),
                                  ot[k_][:].rearrange("a (c b) -> a c b", b=128), reads=[f"hot{k_}"], writes=["hc1T"])

                    fwd_transform(Yin, A, N2, hc, F2f, TW, F1, Y1t, CS, xs, tq, consume, V)
                P.barrier()

        HH = nx // 4096
        GSCALE = 128 ** -0.5

        def chunk_view(ap2, ci):
            if ci[0] == "c":
                return ap2[:, nx + 64 * ci[1]: nx + 64 * ci[1] + 64]
            return ap2[:, 0:nx].rearrange("p (h r c) -> p h r c", h=HH, r=64, c=64)[:, ci[2], :, ci[1]]

        def gla_order(d):
            nq = nctx // 64
            cs = [("c", q) for q in range(nq)]
            xs = [("x", c, hh) for c in range(64) for hh in range(HH)]
            return (cs + xs) if d == 0 else (cs[::-1] + xs[::-1])

        def gla_stage(l):
            import os
            G_HEADS = int(os.environ.get("GLA_HEADS", 4)); G_DIRS = int(os.environ.get("GLA_DIRS", 2))
            G_CHUNKS = int(os.environ.get("GLA_CHUNKS", 100000)); G_STEPS = float(os.environ.get("GLA_STEPS", 6))
            G_FINISH = int(os.environ.get("GLA_FINISH", 1))
            with contextlib.ExitStack() as st:
                c64 = SB(st, "gl_c64", [64, 6, 64], F32)
                idb = SB(st, "gl_idb", [128, 128], BF16)
                ngt = SB(st, "gl_ng", [128, 2], F32)
                kb = SB(st, "gl_k", [128, T], BF16)
                qb = SB(st, "gl_q", [128, T], BF16)
                vb = SB(st, "gl_v", [128, 2, T], BF16)
                lrb = SB(st, "gl_lr", [17, T], BF16)
                acc = SB(st, "gl_acc", [128, 2, T], F32)
                stg = [SB(st, f"gl_stg{i}", [128, 1056], F32) for i in range(2)]
                wgf = SB(st, "gl_wgf", [17, 128], F32)
                wgb = SB(st, "gl_wgb", [17, 128], BF16)
                Sst = SB(st, "gl_S", [128, 256], F32)
                NBUF = 2
                et = [SB(st, f"gl_e{i}", [64, 128], F32) for i in range(NBUF)]
                spt = [SB(st, f"gl_sp{i}", [64, 128], F32) for i in range(NBUF)]
                Ep = [SB(st, f"gl_Ep{i}", [128, 64], F32) for i in range(NBUF)]
                Em = [SB(st, f"gl_Em{i}", [128, 64], F32) for i in range(NBUF)]
                Ekv = [SB(st, f"gl_Ekv{i}", [64, 128], F32) for i in range(NBUF)]
                kkv = [SB(st, f"gl_kkv{i}", [64, 128], F32) for i in range(NBUF)]
                vtk = [SB(st, f"gl_vt{i}", [64, 256], F32) for i in range(NBUF)]
                qin = [SB(st, f"gl_qi{i}", [128, 64], F32) for i in range(NBUF)]
                kout = [SB(st, f"gl_ko{i}", [128, 64], F32) for i in range(NBUF)]
                sTt = [SB(st, f"gl_sT{i}", [64, 64], F32) for i in range(NBUF)]
                P.dma("sp", c64[:], cst64, writes=["c64"])
                P.dma("sp", idb[:], identb_in, writes=["idb"])
                P.dma("sp", ngt[:], gla_ng[l], writes=["ngt"])
                si = [0]

                def load_rows(dst_ap_fn, row0, nrows, key):
                    for t0 in range(0, T, 1056):
                        n = min(1056, T - t0)
                        j = si[0] % 2
                        si[0] += 1
                        P.dma("sp", stg[j][0:nrows, 0:n], pxA[row0:row0 + nrows, t0:t0 + n], reads=["pxT"], writes=[f"glstg{j}"])
                        eng = ("act", "pool")[si[0] % 2]
                        if eng == "act":
                            P.op("act", lambda e, j=j, t0=t0, n=n: e.activation(out=dst_ap_fn(t0, n), in_=stg[j][0:nrows, 0:n], func=AF.Copy),
                                 reads=[f"glstg{j}"], writes=[key])
                        else:
                            P.op("pool", lambda e, j=j, t0=t0, n=n: e.tensor_copy(out=dst_ap_fn(t0, n), in_=stg[j][0:nrows, 0:n]),
                                 reads=[f"glstg{j}"], writes=[key])

                for h in range(G_HEADS):
                    load_rows(lambda t0, n: kb[:, t0:t0 + n], 512 + h * 128, 128, "glk")
                    load_rows(lambda t0, n: qb[:, t0:t0 + n], 2080 + h * 128, 128, "glq")
                    for half in range(2):
                        load_rows(lambda t0, n, half=half: vb[:, half, t0:t0 + n], 1024 + h * 256 + half * 128, 128, "glv")
                    for d in range(G_DIRS):
                        P.op("pool", lambda e: e.memset(lrb[:], 1.0), writes=["gllr"])
                        load_rows(lambda t0, n: lrb[0:16, t0:t0 + n], 2048 + d * 16, 16, "gllr")
                        P.dma("sp", wgf[:], gla_wg[l, d, :, h * 128:(h + 1) * 128], writes=["glwgf"])
                        P.op("dve", lambda e: e.tensor_copy(out=wgb[:], in_=wgf[:]), reads=["glwgf"], writes=["glwgb"])
                        P.op("pool", lambda e: e.memset(Sst[:], 0.0), writes=["glS"])
                        tri, triC, msk = c64[:, d, :], c64[:, 2 + d, :], c64[:, 4 + d, :]
                        last = 63 if d == 0 else 0
                        order = gla_order(d)[:G_CHUNKS]
                        for it, ci in enumerate(order):
                            if ci[0] == "x" and it == nctx // 64:
                                pass
                            b = it % NBUF
                            kbk = f"glb{b}"
                            lrc, kc_, qc_ = chunk_view(lrb[:], ci), chunk_view(kb[:], ci), chunk_view(qb[:], ci)
                            P.op("pe", lambda e, lrc=lrc: e.matmul(ps[0][0:64, 0:128], lrc, wgb[:], start=True, stop=True),
                                 reads=["gllr", "glwgb"], writes=["glp0"])
                            P.op("act", lambda e, b=b: e.activation(out=et[b][:], in_=ps[0][0:64, 0:128], func=AF.Exp, scale=-1.0),
                                 reads=["glp0"], writes=[f"gle{b}"])
                            P.op("act", lambda e, b=b: e.activation(out=spt[b][:], in_=et[b][:], func=AF.Ln, bias=ones[0:64, 0:1]),
                                 reads=[f"gle{b}", "ones"], writes=[f"glsp{b}"])
                            if G_STEPS < 2:
                                continue
                            P.op("pe", lambda e, b=b, tri=tri: e.matmul(ps[1][:, 0:64], spt[b][:], tri, start=True, stop=True),
                                 reads=[f"glsp{b}", "c64"], writes=["glp1"])
                            P.op("pe", lambda e, b=b, triC=triC: e.matmul(ps[2][0:64, 0:128], triC, spt[b][:], start=True, stop=True),
                                 reads=[f"glsp{b}", "c64"], writes=["glp2"])
                            P.op("act", lambda e, b=b: e.activation(out=Ep[b][:], in_=ps[1][:, 0:64], func=AF.Exp),
                                 reads=["glp1"], writes=[f"glEp{b}"])
                            P.op("act", lambda e, b=b: e.activation(out=Em[b][:], in_=ps[1][:, 0:64], func=AF.Exp, scale=-1.0),
                                 reads=["glp1"], writes=[f"glEm{b}"])
                            P.op("act", lambda e, b=b: e.activation(out=Ekv[b][:], in_=ps[2][0:64, 0:128], func=AF.Exp),
                                 reads=["glp2"], writes=[f"glEkv{b}"])
                            if G_STEPS < 3:
                                continue
                            P.op("pe", lambda e, kc_=kc_: e.matmul(ps[3][0:64, 0:128], kc_, idb[:], start=True, stop=True),
                                 reads=["glk", "idb"], writes=["glp3"])
                            for half in range(2):
                                vc_ = chunk_view(vb[:, half, :], ci)
                                P.op("pe", lambda e, vc_=vc_, half=half: e.matmul(ps[3][0:64, 128 + half * 128:256 + half * 128], vc_, idb[:],
                                                                                 start=True, stop=True),
                                     reads=["glv", "idb"], writes=["glp3"])
                            if G_STEPS < 3.2:
                                continue
                            P.op("dve", lambda e, b=b: e.tensor_tensor(out=kkv[b][:], in0=Ekv[b][:], in1=ps[3][0:64, 0:128], op=ALU.mult),
                                 reads=[f"glEkv{b}", "glp3"], writes=[f"glkkv{b}"])
                            if G_STEPS < 3.3:
                                continue
                            P.op("dve", lambda e, b=b: e.tensor_copy(out=vtk[b][:], in_=ps[3][0:64, 128:384]),
                                 reads=["glp3"], writes=[f"glvt{b}"])
                            if G_STEPS < 3.5:
                                continue
                            P.op("dve", lambda e, b=b, qc_=qc_: e.scalar_tensor_tensor(out=qin[b][:], in0=qc_, scalar=GSCALE, in1=Ep[b][:],
                                                                                   op0=ALU.mult, op1=ALU.mult),
                                 reads=["glq", f"glEp{b}"], writes=[f"glqi{b}"])
                            P.op("dve", lambda e, b=b, kc_=kc_: e.tensor_tensor(out=kout[b][:], in0=kc_, in1=Em[b][:], op=ALU.mult),
                                 reads=["glk", f"glEm{b}"], writes=[f"glko{b}"])
                            if G_STEPS < 4:
                                continue
                            P.op("pe", lambda e, b=b: e.matmul(ps[4][0:64, 0:64], kout[b][:], qin[b][:], start=True, stop=True),
                                 reads=[f"glko{b}", f"glqi{b}"], writes=["glp4"])
                            P.op("dve", lambda e, b=b, msk=msk: e.tensor_tensor(out=sTt[b][:], in0=msk, in1=ps[4][0:64, 0:64], op=ALU.mult),
                                 reads=["glp4", "c64"], writes=[f"glsT{b}"])
                            if G_STEPS < 5:
                                continue
                            for half in range(2):
                                P.op("pe", lambda e, b=b, half=half: e.matmul(ps[5][:, half * 64:(half + 1) * 64],
                                                                             vtk[b][:, half * 128:(half + 1) * 128], sTt[b][:],
                                                                             start=True, stop=False),
                                     reads=[f"glvt{b}", f"glsT{b}"], writes=["glp5"])
                                P.op("pe", lambda e, b=b, half=half: e.matmul(ps[5][:, half * 64:(half + 1) * 64],
                                                                             Sst[:, half * 128:(half + 1) * 128], qin[b][:],
                                                                             start=False, stop=True),
                                     reads=["glS", f"glqi{b}"], writes=["glp5"])
                            accv = chunk_view(acc[:, 0, :], ci), chunk_view(acc[:, 1, :], ci)
                            for half in range(2):
                                if d == 0:
                                    P.op("dve", lambda e, half=half, accv=accv: e.tensor_copy(out=accv[half], in_=ps[5][:, half * 64:(half + 1) * 64]),
                                         reads=["glp5"], writes=["glacc"])
                                else:
                                    P.op("dve", lambda e, half=half, accv=accv: e.tensor_tensor(out=accv[half], in0=accv[half],
                                                                                                in1=ps[5][:, half * 64:(half + 1) * 64], op=ALU.add),
                                         reads=["glp5", "glacc"], writes=["glacc"])
                            if G_STEPS < 6:
                                continue
                            P.op("pe", lambda e, b=b: e.matmul(ps[6][:, 0:256], kkv[b][:], vtk[b][:], start=True, stop=True),
                                 reads=[f"glkkv{b}", f"glvt{b}"], writes=["glp6"])
                            P.op("dve", lambda e, b=b, last=last: e.scalar_tensor_tensor(out=Sst[:], in0=Sst[:], scalar=Ep[b][:, last:last + 1],
                                                                             in1=ps[6][:, 0:256], op0=ALU.mult, op1=ALU.add),
                                 reads=["glS", f"glEp{b}", "glp6"], writes=["glS"])
                    for bi, (t0, nb, col, _, _) in enumerate(blks if G_FINISH else []):
                        sq = stg[0]
                        rr = stg[1]
                        for half in range(2):
                            P.op("act", lambda e, half=half, t0=t0, nb=nb: e.activation(out=sq[:, half * 512:half * 512 + nb],
                                                                                        in_=acc[:, half, t0:t0 + nb], func=AF.Square),
                                 reads=["glacc"], writes=["glstg0"])
                        for half in range(2):
                            P.op("pe", lambda e, half=half, nb=nb: e.matmul(ps[7][:, 0:nb], ones[:], sq[:, half * 512:half * 512 + nb],
                                                                           start=(half == 0), stop=(half == 1)),
                                 reads=["ones", "glstg0"], writes=["glp7"])
                        rsd = Ep[0]
                        P.op("act", lambda e, nb=nb: e.activation(out=sq[:, 0:nb], in_=ps[7][:, 0:nb], func=AF.Sqrt,
                                                                  bias=epst[:, 0:1], scale=1.0 / 256),
                             reads=["glp7", "epst"], writes=["glstg0"])
                        P.op("dve", lambda e, nb=nb: e.reciprocal(out=sq[:, 0:nb], in_=sq[:, 0:nb]), reads=["glstg0"], writes=["glstg0"])
                        for half in range(2):
                            r0 = 2592 + h * 256 + half * 128
                            P.dma("sp", rr[:, half * 512:half * 512 + nb], pxA[r0:r0 + 128, t0:t0 + nb], reads=["pxT"], writes=["glstg1"])
                        P.op("act", lambda e: e.activation(out=rr[:, 0:1024], in_=rr[:, 0:1024], func=AF.Silu),
                             reads=["glstg1"], writes=["glstg1"])
                        for half in range(2):
                            ob = kout[0]
                            P.op("dve", lambda e, half=half, t0=t0, nb=nb: e.scalar_tensor_tensor(
                                out=rr[:, half * 512:half * 512 + nb], in0=acc[:, half, t0:t0 + nb], scalar=ngt[:, half:half + 1],
                                in1=rr[:, half * 512:half * 512 + nb], op0=ALU.mult, op1=ALU.mult),
                                reads=["glacc", "ngt", "glstg1"], writes=["glstg1"])
                            P.op("pool", lambda e, half=half, nb=nb: e.tensor_tensor(out=vb[:, half, 0:nb], in0=rr[:, half * 512:half * 512 + nb],
                                                                                  in1=sq[:, 0:nb], op=ALU.mult),
                                 reads=["glstg1", "glstg0"], writes=["glv"])
                            P.dma("pool", yT[bi, :, 8 + 2 * h + half, 0:nb], vb[:, half, 0:nb], reads=["glv"], writes=["yT"])
                P.barrier()

        if only:
            for r0 in range(0, PSPLIT, 512):
                P.dma("pool", pxA[r0:r0 + 512], px_dbg[r0:r0 + 512], writes=["pxT"])
            P.barrier()
            if only == "gla":
                gla_stage(0)
            if only == "s5":
                s5_stage(0)
            if only == "hy":
                hyena_stage(0, True)
                fs_out = nc.dram_tensor("fs_out", [2, 4, 128, 2, 128, hyc["x"]["N2"]], F32, kind="ExternalOutput").ap()
                for o_ in range(2):
                    for cb_ in range(4):
                        P.dma("pool", fs_out[o_, cb_], hyc["x"]["fspec"][o_, cb_], reads=["fspec"], writes=["fs_out"])
            for bi in range(len(blks)):
                P.dma("pool", y_out[bi], yT[bi], reads=["yT"], writes=["y_out"])
            P.barrier()
        for l in range(0 if only else depth):
            mod_stage(l)
            castw(w_in[l], wb_in, MC_IN, KC, "wsrc")
            norm_stage(A1, 0)

            def in_extra(st):
                return [SB(st, f"ie_o{i}", [128, 4, 512], F32) for i in range(2)]

            def in_epi(ctx, bi, blk, m0, g, pbase, go=0, oj=None):
                t0, nb = blk[0], blk[1]
                if oj is None:
                    oj = (pbase // 4) % 2
                okey = f"ieo{oj}_{go}"
                for gi in range(g):
                    P.op("act", lambda e, gi=gi: e.activation(out=ctx[oj][:, go + gi, 0:nb], in_=ps[pbase + gi][:, 0:nb], func=AF.Copy),
                         reads=[f"psg{pbase}"], writes=[okey])
                dst = pxA[m0 * 128:(m0 + g) * 128] if m0 * 128 < PSPLIT else pxB[m0 * 128 - PSPLIT:(m0 + g) * 128 - PSPLIT]
                P.dma("pool", dst[:, t0:t0 + nb].rearrange("(g p) n -> p g n", p=128),
                      ctx[oj][:, go:go + g, 0:nb], reads=[okey], writes=["pxT"])

            gemm_stage(hT, "hT", KC, wb_in, MC_IN, in_epi, in_extra, pair=True)

            with contextlib.ExitStack() as st:
                if debug:
                    P.dma("pool", yT, y_dbg, writes=["yT"])
                else:
                    z = SB(st, "zfill", [128, KC, 512], BF16)
                    P.op("pool", lambda e: e.memset(z[:], 0.0), writes=["zf"])
                    for bi in range(len(blks)):
                        P.dma("pool", yT[bi], z[:], reads=["zf"], writes=["yT"])
                P.barrier()

            if not debug:
                s5_stage(l)
                hyena_stage(l, l < depth - 1 or depth < DEPTH)
                gla_stage(l)
            castw(w_br[l], wb_br, KC, KC, "wsrc")
            castw(w_out[l], wb_out, KC, KC, "wsrc2")

            with contextlib.ExitStack() as st:
                yb = [SB(st, f"mg_y{i}", [128, KC, 512], BF16) for i in range(2)]
                xb = [SB(st, f"mg_x{i}", [128, KC, 512], F32) for i in range(2)]
                mT = SB(st, "mg_m", [128, KC, 512], BF16)
                wt = [SB(st, f"mg_w{i}", [128, KC, 128], BF16) for i in range(3)]
                gt = [SB(st, f"mg_g{i}", [128, 3, 512], F32) for i in range(2)]
                t1 = [SB(st, f"mg_t{i}", [128, 512], F32) for i in range(2)]
                t2 = [SB(st, f"mg_u{i}", [128, 512], F32) for i in range(2)]
                wi = 0
                for bi, (t0, nb, col, _, _) in enumerate(blks):
                    j = bi % 2
                    P.dma("sp", yb[j][:], yT[bi], reads=["yT"], writes=[f"mgy{j}"])
                    P.dma("sp", xb[j][:, :, 0:nb], xT[:, t0:t0 + nb].rearrange("(c p) n -> p c n", p=128),
                          reads=["xT"], writes=[f"mgx{j}"])
                    for mi in range(KC):
                        wj = wi % 3
                        gj = wi % 2
                        wi += 1
                        P.dma("sp", wt[wj][:], wb_br[mi], reads=["wsrc"], writes=[f"mgw{wj}"])
                        for gi3 in range(3):
                            r0 = GATE0 + gi3 * D + mi * 128
                            srcg = pxA[r0:r0 + 128] if r0 < PSPLIT else pxB[r0 - PSPLIT:r0 - PSPLIT + 128]
                            P.dma("pool", gt[gj][:, gi3, 0:nb], srcg[:, t0:t0 + nb], reads=["pxT"], writes=[f"mgg{gj}"])
                        P.op("act", lambda e, gj=gj, nb=nb: e.activation(out=gt[gj][:, :, 0:nb], in_=gt[gj][:, :, 0:nb], func=AF.Sigmoid),
                             reads=[f"mgg{gj}"], writes=[f"mgg{gj}"])
                        for bri, (k0, k1) in enumerate(((0, 4), (4, 8), (8, 16))):
                            pi = bri
                            for kc in range(k0, k1):
                                P.op("pe", lambda e, wj=wj, j=j, kc=kc, pi=pi, nb=nb, k0=k0, k1=k1: e.matmul(
                                    ps[pi][:, 0:nb], wt[wj][:, kc, :], yb[j][:, kc, 0:nb], start=(kc == k0), stop=(kc == k1 - 1)),
                                    reads=[f"mgw{wj}", f"mgy{j}"], writes=[f"ps{pi}"])
                        P.op("dve", lambda e, gj=gj, nb=nb: e.tensor_tensor(out=t1[gj][:, 0:nb], in0=gt[gj][:, 0, 0:nb], in1=ps[0][:, 0:nb], op=ALU.mult),
                             reads=[f"mgg{gj}", "ps0"], writes=[f"mgt{gj}"])
                        P.op("dve", lambda e, gj=gj, nb=nb: e.tensor_tensor(out=t2[gj][:, 0:nb], in0=gt[gj][:, 1, 0:nb], in1=ps[1][:, 0:nb], op=ALU.mult),
                             reads=[f"mgg{gj}", "ps1"], writes=[f"mgu{gj}"])
                        P.op("pool", lambda e, gj=gj, nb=nb: e.tensor_tensor(out=t1[gj][:, 0:nb], in0=t1[gj][:, 0:nb], in1=t2[gj][:, 0:nb], op=ALU.add),
                             reads=[f"mgt{gj}", f"mgu{gj}"], writes=[f"mgt{gj}"])
                        P.op("dve", lambda e, gj=gj, nb=nb: e.tensor_tensor(out=t2[gj][:, 0:nb], in0=gt[gj][:, 2, 0:nb], in1=ps[2][:, 0:nb], op=ALU.mult),
                             reads=[f"mgg{gj}", "ps2"], writes=[f"mgu{gj}"])
                        P.op("pool", lambda e, gj=gj, nb=nb, mi=mi: e.tensor_tensor(out=mT[:, mi, 0:nb], in0=t1[gj][:, 0:nb], in1=t2[gj][:, 0:nb], op=ALU.add),
                             reads=[f"mgt{gj}", f"mgu{gj}"], writes=["mgm"])
                    for mo in range(KC):
                        wj = wi % 3
                        pi = 4 + (wi % 2)
                        wi += 1
                        P.dma("sp", wt[wj][:], wb_out[mo], reads=["wsrc2"], writes=[f"mgw{wj}"])
                        for kc in range(KC):
                            P.op("pe", lambda e, wj=wj, kc=kc, pi=pi, nb=nb: e.matmul(
                                ps[pi][:, 0:nb], wt[wj][:, kc, :], mT[:, kc, 0:nb], start=(kc == 0), stop=(kc == KC - 1)),
                                reads=[f"mgw{wj}", "mgm"], writes=[f"ps{pi}"])
                        P.op("dve", lambda e, j=j, mo=mo, pi=pi, nb=nb, col=col: e.scalar_tensor_tensor(
                            out=xb[j][:, mo, 0:nb], in0=ps[pi][:, 0:nb], scalar=modv[:, 2 * KC + mo, col:col + 1],
                            in1=xb[j][:, mo, 0:nb], op0=ALU.mult, op1=ALU.add),
                            reads=[f"ps{pi}", "modv", f"mgx{j}"], writes=[f"mgx{j}"])
                    P.dma("pool", xT[:, t0:t0 + nb].rearrange("(c p) n -> p c n", p=128), xb[j][:, :, 0:nb],
                          reads=[f"mgx{j}"], writes=["xT"])
                P.barrier()

            castw(w_up[l], wb_up, MC_UP, KC, "wsrc")
            norm_stage(A2, 3 * KC)

            def up_extra(st):
                return [SB(st, f"ue_o{i}", [128, 4, 512], F32) for i in range(2)]

            def up_epi(ctx, bi, blk, m0, g, pbase, go=0, oj=None):
                t0, nb = blk[0], blk[1]
                if oj is None:
                    oj = (pbase // 4) % 2
                okey = f"ueo{oj}_{go}"
                for gi in range(g):
                    P.op("act", lambda e, gi=gi: e.activation(out=ctx[oj][:, go + gi, 0:nb], in_=ps[pbase + gi][:, 0:nb], func=AF.Copy),
                         reads=[f"psg{pbase}"], writes=[okey])
                dst = aT[m0 * 128:(m0 + g) * 128] if m0 < KC_FF else bT[(m0 - KC_FF) * 128:(m0 - KC_FF + g) * 128]
                P.dma("pool", dst[:, t0:t0 + nb].rearrange("(g p) n -> p g n", p=128),
                      ctx[oj][:, go:go + g, 0:nb], reads=[okey], writes=["abT"])

            gemm_stage(hT, "hT", KC, wb_up, MC_UP, up_epi, up_extra, pair=True)

            with contextlib.ExitStack() as st:
                CG = 4
                aw = [SB(st, f"cv_a{i}", [128, CG, 514], F32) for i in range(2)]
                bw = [SB(st, f"cv_b{i}", [128, CG, 512], F32) for i in range(2)]
                ow = [SB(st, f"cv_o{i}", [128, CG, 512], F32) for i in range(2)]
                uw = [SB(st, f"cv_u{i}", [128, CG, 512], BF16) for i in range(2)]
                it = 0
                for bi, (t0, nb, col, le, re) in enumerate(blks):
                    for c0 in range(0, KC_FF, CG):
                        j = it % 2
                        it += 1
                        lo = t0 if le else t0 - 1
                        hi = t0 + nb if re else t0 + nb + 1
                        if le or re:
                            P.op("pool", lambda e, j=j: e.memset(aw[j][:], 0.0), writes=[f"cva{j}"])
                        P.dma("sp", aw[j][:, :, (lo - (t0 - 1)):(hi - (t0 - 1))],
                              aT[c0 * 128:(c0 + CG) * 128, lo:hi].rearrange("(g p) n -> p g n", p=128),
                              reads=["abT"], writes=[f"cva{j}"])
                        P.dma("sp", bw[j][:, :, 0:nb],
                              bT[c0 * 128:(c0 + CG) * 128, t0:t0 + nb].rearrange("(g p) n -> p g n", p=128),
                              reads=["abT"], writes=[f"cvb{j}"])
                        for gi in range(CG):
                            cc = c0 + gi
                            P.op("dve", lambda e, j=j, cc=cc, nb=nb, gi=gi: e.tensor_scalar(
                                out=ow[j][:, gi, 0:nb], in0=aw[j][:, gi, 0:nb], scalar1=cwt[:, cc, 0:1], scalar2=cbt[:, cc:cc + 1],
                                op0=ALU.mult, op1=ALU.add), reads=[f"cva{j}", "cwt", "cbt"], writes=[f"cvo{j}"])
                            P.op("dve", lambda e, j=j, cc=cc, nb=nb, gi=gi: e.scalar_tensor_tensor(
                                out=ow[j][:, gi, 0:nb], in0=aw[j][:, gi, 1:nb + 1], scalar=cwt[:, cc, 1:2], in1=ow[j][:, gi, 0:nb],
                                op0=ALU.mult, op1=ALU.add), reads=[f"cva{j}", "cwt", f"cvo{j}"], writes=[f"cvo{j}"])
                            P.op("dve", lambda e, j=j, cc=cc, nb=nb, gi=gi: e.scalar_tensor_tensor(
                                out=ow[j][:, gi, 0:nb], in0=aw[j][:, gi, 2:nb + 2], scalar=cwt[:, cc, 2:3], in1=ow[j][:, gi, 0:nb],
                                op0=ALU.mult, op1=ALU.add), reads=[f"cva{j}", "cwt", f"cvo{j}"], writes=[f"cvo{j}"])
                        P.op("act", lambda e, j=j, nb=nb: e.activation(out=ow[j][:, :, 0:nb], in_=ow[j][:, :, 0:nb], func=AF.Silu),
                             reads=[f"cvo{j}"], writes=[f"cvo{j}"])
                        P.op("pool", lambda e, j=j, nb=nb: e.tensor_tensor(out=uw[j][:, :, 0:nb], in0=ow[j][:, :, 0:nb], in1=bw[j][:, :, 0:nb], op=ALU.mult),
                             reads=[f"cvo{j}", f"cvb{j}"], writes=[f"cvu{j}"])
                        P.dma("pool", uT[bi, :, c0:c0 + CG, 0:nb], uw[j][:, :, 0:nb], reads=[f"cvu{j}"], writes=["uT"])
                P.barrier()

            castw(w_dn[l], wb_dn, KC, KC_FF, "wsrc")

            def dn_extra(st):
                return SB(st, "de_x", [128, KC, 512], F32)

            def dn_epi(ctx, bi, blk, m0, g, pbase):
                t0, nb, col = blk[0], blk[1], blk[2]
                if m0 == 0:
                    P.dma("pool", ctx[:, :, 0:nb], xT[:, t0:t0 + nb].rearrange("(c p) n -> p c n", p=128),
                          reads=["xT"], writes=["dex"])
                for gi in range(g):
                    m = m0 + gi
                    P.op("dve", lambda e, gi=gi, m=m: e.scalar_tensor_tensor(
                        out=ctx[:, m, 0:nb], in0=ps[pbase + gi][:, 0:nb], scalar=modv[:, 5 * KC + m, col:col + 1],
                        in1=ctx[:, m, 0:nb], op0=ALU.mult, op1=ALU.add),
                        reads=[f"psg{pbase}", "modv", "dex"], writes=["dex"])
                if m0 + g >= KC:
                    P.dma("pool", xT[:, t0:t0 + nb].rearrange("(c p) n -> p c n", p=128), ctx[:, :, 0:nb],
                          reads=["dex"], writes=["xT"])

            gemm_stage(uT, "uT", KC_FF, wb_dn, KC, dn_epi, dn_extra, G=2)

        if not only:
            norm_stage(None, 0, final=True)
        P.emit()
    return nc


def _tile_w(w, kc_n):
    K, M = w.shape
    return np.ascontiguousarray(w.reshape(kc_n, 128, M // 128, 128).transpose(2, 1, 0, 3))


def _vec(v):
    return np.ascontiguousarray(v.reshape(-1, 128).T)


def const_tables():
    j = np.arange(64)[:, None]
    i = np.arange(64)[None, :]
    c = -1.0 / 16.0
    t = np.stack([c * (j <= i), c * (j >= i), c * (j > i), c * (j < i), 1.0 * (j <= i), 1.0 * (j >= i)], axis=1)
    return np.ascontiguousarray(t.astype(np.float32))


def hy_tables(n):
    N = 2 * n
    N2, A = n // 64, n // 128
    tau = np.arange(N)
    pos = np.where(tau < n, tau, N - tau).astype(np.float64)
    t = pos / max(n - 1, 1)
    freqs = np.linspace(1e-4, 15, 16)
    ang = (2.0 * np.pi / n) * pos[:, None] * freqs[None]
    feats = np.concatenate([t[:, None], np.cos(ang), -np.sin(ang)], axis=1)
    rates = np.abs(np.linspace(np.log(1e-2) / 1.5, np.log(1e-2) / 0.3, 512))
    dec = np.exp(-t[:, None] * rates[None])
    dec[n] = 0.0
    dec = dec.reshape(N2, 128, 4, 128).transpose(2, 0, 3, 1)
    a = np.arange(N2)
    k1 = np.arange(128)
    th2 = 2 * np.pi * np.outer(a, a) / N2
    F2f = np.stack([np.cos(th2), -np.sin(th2)], axis=1)
    thw = 2 * np.pi * np.outer(k1, a) / N
    TW = np.stack([np.cos(thw), -np.sin(thw)], axis=1)
    thi = 2 * np.pi * np.outer(a, np.arange(A)) / N2
    F2i = np.stack([np.cos(thi) / N, -np.sin(thi) / N], axis=1)
    tht = 2 * np.pi * np.outer(a, k1) / N
    TWT = np.stack([np.cos(tht), np.sin(tht)], axis=1)
    msk = np.stack([(a < A), (a >= A)], axis=1)
    c = lambda v: np.ascontiguousarray(v.astype(np.float32))
    return dict(feats=c(feats.T), dec=c(dec), F2f=c(F2f), TW=c(TW), F2i=c(F2i), TWT=c(TWT), msk=c(msk))


def hy_f1_table():
    k = np.arange(128)
    th = 2 * np.pi * np.outer(k, k) / 128
    return np.ascontiguousarray(np.stack([np.cos(th), -np.sin(th), np.sin(th)], axis=1).astype(np.float32))


def hy_host_layout(inp, depth):
    f = lambda a: np.asarray(a, dtype=np.float32)
    out = {}
    out["hy_cw"] = np.stack([np.ascontiguousarray(f(inp["hy_conv_w"][l]).reshape(3, 12, 128).transpose(2, 1, 0)) for l in range(depth)])
    out["hy_cb"] = np.stack([np.ascontiguousarray(f(inp["hy_conv_b"][l]).reshape(12, 128).T) for l in range(depth)])
    out["hy_w1"] = np.ascontiguousarray(f(inp["hy_f_w1"])[:depth])
    out["hy_w2"] = np.ascontiguousarray(f(inp["hy_f_w2"])[:depth])
    w3 = np.concatenate([f(inp["hy_f_w3"])[:depth], f(inp["hy_f_b3"])[:depth, None, :]], axis=1)
    w3 = w3.reshape(depth, 65, 2, 2, 4, 128).transpose(0, 1, 2, 4, 3, 5).reshape(depth, 65, 2, 4, 256)
    out["hy_w3"] = np.ascontiguousarray(w3)
    fb = np.zeros((depth, 64, 6), np.float32)
    fb[:, :, 0] = f(inp["hy_f_freq1"])[:depth]
    fb[:, :, 1] = f(inp["hy_f_b1"])[:depth]
    fb[:, :, 2] = f(inp["hy_f_freq2"])[:depth]
    fb[:, :, 3] = f(inp["hy_f_b2"])[:depth]
    out["hy_fb"] = fb
    out["hy_bias"] = np.stack([np.ascontiguousarray(f(inp["hy_bias"][l]).reshape(2, 4, 128).transpose(2, 0, 1)) for l in range(depth)])
    return out


def s5_const_tables():
    a = np.arange(128)
    tri_f = (a[:, None] <= a[None, :]).astype(np.float32)
    tri_b = (a[:, None] >= a[None, :]).astype(np.float32)
    krow_f = np.broadcast_to(a[None, :], (128, 128)).astype(np.float32)
    krow_b = np.broadcast_to((127 - a)[None, :], (128, 128)).astype(np.float32)
    c128 = np.ascontiguousarray(np.stack([tri_f, tri_b, krow_f, krow_b], axis=1))
    ck = np.ascontiguousarray(np.stack([a, 127 - a, -a, -(127 - a)], axis=1).astype(np.float32))
    return c128, ck


def s5_host_layout(inp, depth):
    f = lambda a: np.asarray(a, dtype=np.float32)
    out = {}
    sp = lambda a: np.ascontiguousarray(a.reshape(depth, 2, 16, 2, 64).transpose(0, 3, 4, 1, 2).reshape(depth, 128, 2, 16))
    out["s5_are"] = sp(f(inp["s5_a_re"])[:depth])
    out["s5_aim"] = sp(f(inp["s5_a_im"])[:depth])
    ls = np.broadcast_to(f(inp["s5_log_step"])[:depth, :, :, None], (depth, 2, 32, 64))
    out["s5_lst"] = sp(np.ascontiguousarray(ls))
    def bblk(bm):
        o = np.zeros((depth, 2, 4, 128, 512), np.float32)
        b5 = bm.reshape(depth, 2, 4, 8, 64, 16)
        for gl in range(8):
            o[:, :, :, gl * 16:(gl + 1) * 16, gl * 64:(gl + 1) * 64] = b5[:, :, :, gl].transpose(0, 1, 2, 4, 3)
        return o
    out["s5_bre"] = bblk(f(inp["s5_b_re"])[:depth])
    out["s5_bim"] = bblk(f(inp["s5_b_im"])[:depth])
    def cblk(cm):
        o = np.zeros((depth, 2, 4, 128, 4, 128), np.float32)
        c6 = cm.reshape(depth, 2, 4, 4, 2, 16, 64)
        for s_ in range(4):
            for gi in range(2):
                ch0 = (2 * s_ + gi) * 16
                o[:, :, :, gi * 64:(gi + 1) * 64, s_, ch0:ch0 + 16] = c6[:, :, :, s_, gi].transpose(0, 1, 2, 4, 3)
        return o
    out["s5_cre"] = cblk(f(inp["s5_c_re"])[:depth])
    out["s5_cim"] = cblk(f(inp["s5_c_im"])[:depth])
    out["s5_dd"] = np.stack([np.ascontiguousarray(f(inp["s5_d"][l]).reshape(4, 128).T) for l in range(depth)])
    out["s5_gw"] = np.stack([np.ascontiguousarray(f(inp["s5_glu_w"][l]).reshape(4, 128, 4, 128).transpose(1, 2, 0, 3)) for l in range(depth)])
    out["s5_gb"] = np.stack([np.ascontiguousarray(f(inp["s5_glu_b"][l]).reshape(4, 128).T) for l in range(depth)])
    return out


def prep_inputs(inp, depth=DEPTH, nx=SEQ, nctx=CTX):
    f = lambda a: np.asarray(a, dtype=np.float32)
    KC = D // 128
    shared = {}
    shared["w_mod"] = np.stack([_tile_w(f(inp["w_mod"][l]), KC) for l in range(depth)])
    shared["b_mod"] = np.stack([_vec(f(inp["b_mod"][l])) for l in range(depth)])
    shared["n1g"] = np.stack([_vec(f(inp["norm1_g"][l])) for l in range(depth)])
    shared["n2g"] = np.stack([_vec(f(inp["norm2_g"][l])) for l in range(depth)])
    shared["fng"] = _vec(f(inp["final_norm_g"]))
    win = np.zeros((depth, D, IN_WP), np.float32)
    win[:, :, :C_MG] = f(inp["w_in"])[:depth, :, :C_MG]
    win[:, :, GATE0:] = f(inp["w_in"])[:depth, :, C_MG:]
    shared["w_in"] = np.stack([_tile_w(win[l], KC) for l in range(depth)])
    shared["w_br"] = np.stack([_tile_w(f(inp["w_branch"][l]), KC) for l in range(depth)])
    shared["w_out"] = np.stack([_tile_w(f(inp["w_out"][l]), KC) for l in range(depth)])
    shared["w_up"] = np.stack([_tile_w(f(inp["ff_w_up"][l]), KC) for l in range(depth)])
    shared["w_dn"] = np.stack([_tile_w(f(inp["ff_w_down"][l]), FF // 128) for l in range(depth)])
    shared["fcw"] = np.stack([np.ascontiguousarray(f(inp["ff_conv_w"][l]).reshape(3, FF // 128, 128).transpose(2, 1, 0))
                              for l in range(depth)])
    shared["fcb"] = np.stack([_vec(f(inp["ff_conv_b"][l])) for l in range(depth)])
    shared.update(s5_host_layout(inp, depth))
    shared.update(hy_host_layout(inp, depth))
    shared["hy_F1"] = hy_f1_table()
    for nm, n_ in (("x", nx), ("c", nctx)):
        for k_, v_ in hy_tables(n_).items():
            shared[f"hy_{k_}_{nm}"] = v_
    shared["cst128"], shared["cstk"] = s5_const_tables()
    wgaug = np.concatenate([f(inp["gla_wg"])[:depth], f(inp["gla_bg"])[:depth, :, None, :]], axis=2)
    shared["gla_wg"] = np.ascontiguousarray(wgaug)
    shared["gla_ng"] = np.stack([np.ascontiguousarray(f(inp["gla_norm_g"][l]).reshape(2, 128).T) for l in range(depth)])
    shared["cst64"] = const_tables()
    shared["identb"] = np.eye(128, dtype=np.float32).astype(ml_dtypes.bfloat16)
    maps = []
    for b in range(2):
        m = dict(shared)
        m["xT"] = np.ascontiguousarray(np.concatenate([f(inp["x"][b, :nx]).T, f(inp["ctx"][b, :nctx]).T], axis=1))
        cc = np.stack([f(inp["c"][b]), f(inp["c_ctx"])], axis=1)
        m["cT"] = np.ascontiguousarray(cc.reshape(KC, 128, 2).transpose(1, 0, 2))
        maps.append(m)
    return maps


def kernel(**inputs):
    nc = build()
    maps = prep_inputs(inputs)
    res = run_bass_kernel_spmd(nc, maps, core_ids=[0, 1])
    out = np.stack([np.ascontiguousarray(res.results[b]["outT"].T) for b in range(2)], axis=0)
    return out.astype(np.float32)
```

```python
import contextlib
import numpy as np
import ml_dtypes
import concourse.bass as bass
import concourse.mybir as mybir
from concourse.bass_utils import run_bass_kernel_spmd

F32 = mybir.dt.float32
BF16 = mybir.dt.bfloat16
AF = mybir.ActivationFunctionType
ALU = mybir.AluOpType

D = 2048
DEPTH = 4
SEQ = 8192
CTX = 256
IN_W = 11296
IN_WP = 11392
FF = 5632
C_MG = IN_W - 3 * D
GATE0 = 5248
PSPLIT = 5632
EPS = 1e-6
ENGS = ("pe", "act", "dve", "pool", "sp")


class Prog:
    def __init__(self, nc, n_dma_slots=6):
        self.nc = nc
        self.ops = {e: [] for e in ENGS}
        self.count = {e: 0 for e in ENGS}
        self.waited = {e: {} for e in ENGS}
        self.last_writer = {}
        self.readers = {}
        self.n_dma_slots = n_dma_slots
        self.dma_uses = {e: [0] * n_dma_slots for e in ENGS}
        self.dma_next = {e: 0 for e in ENGS}

    def _deps(self, reads, writes):
        deps = {}
        raw = {}

        def add(tok, is_raw):
            if tok:
                for s, v in tok.items():
                    if deps.get(s, 0) < v:
                        deps[s] = v
                    if is_raw and raw.get(s, 0) < v:
                        raw[s] = v

        for r in reads:
            add(self.last_writer.get(r), True)
        for w in writes:
            add(self.last_writer.get(w), False)
            add(self.readers.get(w), False)
        self._raw = raw
        return deps

    def _record(self, tok, reads, writes):
        for w in writes:
            self.last_writer[w] = dict(tok)
            self.readers[w] = {}
        for r in reads:
            d = self.readers.setdefault(r, {})
            for s, v in tok.items():
                if d.get(s, 0) < v:
                    d[s] = v

    def _waits(self, eng, deps):
        out = []
        wd = self.waited[eng]
        raw = getattr(self, "_raw", deps)
        for s, v in deps.items():
            if s == ("e", "pe") and eng == "pe":
                continue
            if s == ("e", eng):
                v = raw.get(s, 0)
                if v == 0:
                    continue
            if wd.get(s, 0) < v:
                wd[s] = v
                out.append((s, v))
        return out

    def op(self, eng, fn, reads=(), writes=()):
        deps = self._deps(reads, writes)
        waits = self._waits(eng, deps)
        self.count[eng] += 1
        tok = {("e", eng): self.count[eng]}
        self.ops[eng].append((waits, fn, (("e", eng), 1)))
        self._record(tok, reads, writes)

    def dma(self, eng, out, in_, reads=(), writes=(), **kw):
        deps = self._deps(reads, writes)
        slot = self.dma_next[eng]
        self.dma_next[eng] = (slot + 1) % self.n_dma_slots
        s = ("d", eng, slot)
        prev = self.dma_uses[eng][slot]
        if prev > 0 and deps.get(s, 0) < 16 * prev:
            deps[s] = 16 * prev
        waits = self._waits(eng, deps)
        self.dma_uses[eng][slot] = prev + 1
        tok = {s: 16 * (prev + 1)}
        self.ops[eng].append((waits, lambda e: e.dma_start(out=out, in_=in_, **kw), (s, 16)))
        self._record(tok, reads, writes)

    def barrier(self):
        deps = {}
        for e in ENGS:
            if self.count[e] > 0:
                deps[("e", e)] = self.count[e]
            for slot, u in enumerate(self.dma_uses[e]):
                if u > 0:
                    deps[("d", e, slot)] = 16 * u
        for e in ENGS:
            waits = []
            wd = self.waited[e]
            for s, v in deps.items():
                if wd.get(s, 0) < v:
                    wd[s] = v
                    waits.append((s, v))
            self.ops[e].append((waits, None, None))
        self.last_writer = {}
        self.readers = {}

    def emit(self):
        nc = self.nc
        names = set()
        for e in ENGS:
            for waits, fn, inc in self.ops[e]:
                for s, v in waits:
                    names.add(s)
                if inc:
                    names.add(inc[0])
        names = sorted(names, key=str)
        with contextlib.ExitStack() as st:
            sem = {}
            for n in names:
                sem[n] = st.enter_context(nc.semaphore("s_" + "_".join(str(x) for x in n)))
            block = st.enter_context(nc.Block())

            def run(e):
                def f(eng):
                    for waits, fn, inc in self.ops[e]:
                        for s, v in waits:
                            eng.wait_ge(sem[s], v)
                        if fn is not None:
                            fn(eng).then_inc(sem[inc[0]], inc[1])
                return f

            block.tensor(run("pe"))
            block.scalar(run("act"))
            block.vector(run("dve"))
            block.gpsimd(run("pool"))
            block.sync(run("sp"))


def token_blocks(nx, nctx):
    blks = []
    for t0 in range(0, nx, 512):
        blks.append((t0, min(512, nx - t0), 0, t0 == 0, t0 + 512 >= nx))
    for t0 in range(0, nctx, 512):
        blks.append((nx + t0, min(512, nctx - t0), 1, t0 == 0, t0 + 512 >= nctx))
    return blks


def build(depth=DEPTH, nx=SEQ, nctx=CTX, debug=False, only=None):
    T = nx + nctx
    KC = D // 128
    MC_IN = IN_WP // 128
    MC_UP = 2 * FF // 128
    KC_FF = FF // 128
    blks = token_blocks(nx, nctx)
    nc = bass.Bass("TRN2", target_bir_lowering=False)
    dt_in = lambda name, shape, dt=F32: nc.dram_tensor(name, shape, dt, kind="ExternalInput").ap()
    xT_in = dt_in("xT", [D, T])
    cT = dt_in("cT", [128, KC, 2])
    w_mod = dt_in("w_mod", [depth, 6 * KC, 128, KC, 128])
    b_mod = dt_in("b_mod", [depth, 128, 6 * KC])
    n1g = dt_in("n1g", [depth, 128, KC])
    n2g = dt_in("n2g", [depth, 128, KC])
    fng = dt_in("fng", [128, KC])
    w_in = dt_in("w_in", [depth, MC_IN, 128, KC, 128])
    w_br = dt_in("w_br", [depth, KC, 128, KC, 128])
    w_out = dt_in("w_out", [depth, KC, 128, KC, 128])
    w_up = dt_in("w_up", [depth, MC_UP, 128, KC, 128])
    w_dn = dt_in("w_dn", [depth, KC, 128, KC_FF, 128])
    fcw = dt_in("fcw", [depth, 128, KC_FF, 3])
    fcb = dt_in("fcb", [depth, 128, KC_FF])
    s5_are = dt_in("s5_are", [depth, 128, 2, 16])
    s5_aim = dt_in("s5_aim", [depth, 128, 2, 16])
    s5_lst = dt_in("s5_lst", [depth, 128, 2, 16])
    s5_bre = dt_in("s5_bre", [depth, 2, 4, 128, 512])
    s5_bim = dt_in("s5_bim", [depth, 2, 4, 128, 512])
    s5_cre = dt_in("s5_cre", [depth, 2, 4, 128, 4, 128])
    s5_cim = dt_in("s5_cim", [depth, 2, 4, 128, 4, 128])
    s5_dd = dt_in("s5_dd", [depth, 128, 4])
    s5_gw = dt_in("s5_gw", [depth, 128, 4, 4, 128])
    s5_gb = dt_in("s5_gb", [depth, 128, 4])
    cst128 = dt_in("cst128", [128, 4, 128])
    cstk = dt_in("cstk", [128, 4])
    hy_cw = dt_in("hy_cw", [depth, 128, 12, 3])
    hy_cb = dt_in("hy_cb", [depth, 128, 12])
    hy_w1 = dt_in("hy_w1", [depth, 33, 64])
    hy_w2 = dt_in("hy_w2", [depth, 64, 64])
    hy_w3 = dt_in("hy_w3", [depth, 65, 2, 4, 256])
    hy_fb = dt_in("hy_fb", [depth, 64, 6])
    hy_bias = dt_in("hy_bias", [depth, 128, 2, 4])
    HSEQ = [("x", 0, nx), ("c", nx, nctx)]
    hyc = {}
    for nm, _, n_ in HSEQ:
        N2_ = n_ // 64
        A_ = n_ // 128
        hyc[nm] = dict(N2=N2_, A=A_,
                       feats=dt_in(f"hy_feats_{nm}", [33, 2 * n_]),
                       dec=dt_in(f"hy_dec_{nm}", [4, N2_, 128, 128]),
                       F2f=dt_in(f"hy_F2f_{nm}", [N2_, 2, N2_]),
                       TW=dt_in(f"hy_TW_{nm}", [128, 2, N2_]),
                       F2i=dt_in(f"hy_F2i_{nm}", [N2_, 2, A_]),
                       TWT=dt_in(f"hy_TWT_{nm}", [N2_, 2, 128]),
                       msk=dt_in(f"hy_msk_{nm}", [N2_, 2]),
                       fspec=nc.dram_tensor(f"hy_fspec_{nm}", [2, 4, 128, 2, 128, N2_], F32).ap())
    hy_F1 = dt_in("hy_F1", [128, 3, 128])
    gla_wg = dt_in("gla_wg", [depth, 2, 17, 512])
    gla_ng = dt_in("gla_ng", [depth, 128, 2])
    cst64 = dt_in("cst64", [64, 6, 64])
    identb_in = dt_in("identb", [128, 128], BF16)
    outT = nc.dram_tensor("outT", [D, nx], F32, kind="ExternalOutput").ap()
    if only:
        dbg_out = nc.dram_tensor("dbg_out", [nx // 64, 4, 128], F32, kind="ExternalOutput").ap()
        px_dbg = dt_in("px_dbg", [PSPLIT, T])
        y_out = nc.dram_tensor("y_out", [len(blks), 128, KC, 512], BF16, kind="ExternalOutput").ap()
    if debug:
        y_dbg = dt_in("y_dbg", [len(blks), 128, KC, 512], BF16)
    xT = nc.dram_tensor("xTs", [D, T], F32).ap()
    hT = nc.dram_tensor("hTs", [len(blks), 128, KC, 512], BF16).ap()
    pxA = nc.dram_tensor("pxAs", [PSPLIT, T], F32).ap()
    pxB = nc.dram_tensor("pxBs", [IN_WP - PSPLIT, T], F32).ap()
    yT = nc.dram_tensor("yTs", [len(blks), 128, KC, 512], BF16).ap()
    aT = nc.dram_tensor("aTs", [FF, T], F32).ap()
    bT = nc.dram_tensor("bTs", [FF, T], F32).ap()
    uT = nc.dram_tensor("uTs", [len(blks), 128, KC_FF, 512], BF16).ap()
    wb_in = nc.dram_tensor("wb_in", [MC_IN, 128, KC, 128], BF16).ap()
    wb_br = nc.dram_tensor("wb_br", [KC, 128, KC, 128], BF16).ap()
    wb_out = nc.dram_tensor("wb_out", [KC, 128, KC, 128], BF16).ap()
    wb_up = nc.dram_tensor("wb_up", [MC_UP, 128, KC, 128], BF16).ap()
    wb_dn = nc.dram_tensor("wb_dn", [KC, 128, KC_FF, 128], BF16).ap()
    s5rows = nc.dram_tensor("s5rows", [2, 4, 2048], F32).ap()
    ygT = nc.dram_tensor("ygT", [len(blks), 128, 4, 512], BF16).ap()
    hvT = nc.dram_tensor("hvT", [3, 512, T], F32).ap()
    hc1T = nc.dram_tensor("hc1T", [512, T], F32).ap()
    hy1T = nc.dram_tensor("hy1T", [512, T], F32).ap()

    P = Prog(nc)
    root = contextlib.ExitStack()
    with root:
        uid = [0]

        def SB(st, name, shape, dt):
            uid[0] += 1
            return st.enter_context(nc.sbuf_tensor(f"{name}_{uid[0]}", shape, dt))
        PS = lambda st, name: st.enter_context(nc.psum_tensor(name, [128, 512], F32))
        ones = SB(root, "ones", [128, 128], F32)
        sc = SB(root, "sc", [128, KC, 2], F32)
        modv = SB(root, "modv", [128, 6 * KC, 2], F32)
        A1 = SB(root, "A1", [128, KC, 2], F32)
        A2 = SB(root, "A2", [128, KC, 2], F32)
        bm = SB(root, "bm", [128, 6 * KC], F32)
        g1t = SB(root, "g1t", [128, KC], F32)
        g2t = SB(root, "g2t", [128, KC], F32)
        gft = SB(root, "gft", [128, KC], F32)
        cwt = SB(root, "cwt", [128, KC_FF, 3], F32)
        cbt = SB(root, "cbt", [128, KC_FF], F32)
        ps = [PS(root, f"ps{i}") for i in range(8)]

        P.op("pool", lambda e: e.memset(ones[:], 1.0), writes=["ones"])
        epst = SB(root, "epst", [128, 1], F32)
        P.op("pool", lambda e: e.memset(epst[:], EPS), writes=["epst"])
        P.dma("sp", sc[:], cT, writes=["sc"])
        P.op("act", lambda e: e.activation(out=sc[:], in_=sc[:], func=AF.Silu), reads=["sc"], writes=["sc"])
        P.dma("sp", gft[:], fng, writes=["gft"])
        for r0 in range(0, D, 256):
            P.dma("pool", xT[r0:r0 + 256], xT_in[r0:r0 + 256], writes=["xT"])
        P.barrier()

        def castw(src, dst, n, kc, tag):
            with contextlib.ExitStack() as st:
                f = [SB(st, f"cw_f{i}", [128, kc, 128], F32) for i in range(3)]
                b = [SB(st, f"cw_b{i}", [128, kc, 128], BF16) for i in range(3)]
                for i in range(n):
                    j = i % 3
                    P.dma("sp", f[j][:], src[i], writes=[f"cwf{j}"])
                    eng = ("dve", "pool", "act")[i % 3]
                    if eng == "act":
                        P.op("act", lambda e, j=j: e.activation(out=b[j][:], in_=f[j][:], func=AF.Copy),
                             reads=[f"cwf{j}"], writes=[f"cwb{j}"])
                    else:
                        P.op(eng, lambda e, j=j: e.tensor_copy(out=b[j][:], in_=f[j][:]),
                             reads=[f"cwf{j}"], writes=[f"cwb{j}"])
                    P.dma("pool", dst[i], b[j][:], reads=[f"cwb{j}"], writes=[tag])
                P.barrier()

        def mod_stage(l):
            with contextlib.ExitStack() as st:
                wt = [SB(st, f"md_w{i}", [128, KC, 128], F32) for i in range(3)]
                P.dma("sp", bm[:], b_mod[l], writes=["bm"])
                for mi in range(6 * KC):
                    j = mi % 3
                    P.dma("sp", wt[j][:], w_mod[l, mi], writes=[f"mdw{j}"])
                    pb = ps[mi % 2]
                    for kc in range(KC):
                        P.op("pe", lambda e, j=j, kc=kc, pb=pb: e.matmul(pb[:, 0:2], wt[j][:, kc, :], sc[:, kc, :],
                                                                         start=(kc == 0), stop=(kc == KC - 1)),
                             reads=[f"mdw{j}", "sc"], writes=[f"ps{mi % 2}"])
                    P.op("act", lambda e, mi=mi, pb=pb: e.activation(out=modv[:, mi, :], in_=pb[:, 0:2], func=AF.Identity,
                                                                     bias=bm[:, mi:mi + 1]),
                         reads=[f"ps{mi % 2}", "bm"], writes=["modv"])
                P.dma("sp", g1t[:], n1g[l], writes=["g1t"])
                P.dma("sp", g2t[:], n2g[l], writes=["g2t"])
                P.dma("sp", cwt[:], fcw[l], writes=["cwt"])
                P.dma("sp", cbt[:], fcb[l], writes=["cbt"])
                for col in range(2):
                    P.op("dve", lambda e, col=col: e.scalar_tensor_tensor(
                        out=A1[:, :, col], in0=modv[:, KC:2 * KC, col], scalar=1.0, in1=g1t[:], op0=ALU.add, op1=ALU.mult),
                        reads=["modv", "g1t"], writes=["A1"])
                    P.op("dve", lambda e, col=col: e.scalar_tensor_tensor(
                        out=A2[:, :, col], in0=modv[:, 4 * KC:5 * KC, col], scalar=1.0, in1=g2t[:], op0=ALU.add, op1=ALU.mult),
                        reads=["modv", "g2t"], writes=["A2"])
                P.barrier()

        def norm_stage(Asc, shift_base, final=False):
            with contextlib.ExitStack() as st:
                xb = [SB(st, f"nm_x{i}", [128, KC, 512], F32) for i in range(2)]
                sq = SB(st, "nm_sq", [128, KC, 512], F32)
                rs = SB(st, "nm_rs", [128, 512], F32)
                hb = [SB(st, f"nm_h{i}", [128, KC, 512], F32 if final else BF16) for i in range(2)]
                for bi, (t0, nb, col, _, _) in enumerate(blks):
                    if final and col == 1:
                        continue
                    j = bi % 2
                    P.dma("sp", xb[j][:, :, 0:nb], xT[:, t0:t0 + nb].rearrange("(c p) n -> p c n", p=128),
                          reads=["xT"], writes=[f"nmx{j}"])
                    P.op("act", lambda e, j=j, nb=nb: e.activation(out=sq[:, :, 0:nb], in_=xb[j][:, :, 0:nb], func=AF.Square),
                         reads=[f"nmx{j}"], writes=["nmsq"])
                    for kc in range(KC):
                        P.op("pe", lambda e, kc=kc, nb=nb: e.matmul(ps[0][:, 0:nb], ones[:], sq[:, kc, 0:nb],
                                                                    start=(kc == 0), stop=(kc == KC - 1)),
                             reads=["ones", "nmsq"], writes=["ps0"])
                    P.op("act", lambda e, nb=nb: e.activation(out=rs[:, 0:nb], in_=ps[0][:, 0:nb], func=AF.Sqrt,
                                                              bias=epst[:, 0:1], scale=1.0 / D),
                         reads=["ps0", "epst"], writes=["nmrs"])
                    P.op("dve", lambda e, nb=nb: e.reciprocal(out=rs[:, 0:nb], in_=rs[:, 0:nb]),
                         reads=["nmrs"], writes=["nmrs"])
                    for kc in range(KC):
                        eng = "dve" if kc % 2 == 0 else "pool"
                        P.op(eng, lambda e, j=j, kc=kc, nb=nb: e.tensor_tensor(out=xb[j][:, kc, 0:nb], in0=xb[j][:, kc, 0:nb],
                                                                              in1=rs[:, 0:nb], op=ALU.mult),
                             reads=[f"nmx{j}", "nmrs"], writes=[f"nmx{j}"])
                    for kc in range(KC):
                        if final:
                            P.op("act", lambda e, j=j, kc=kc, nb=nb: e.activation(
                                out=hb[j][:, kc, 0:nb], in_=xb[j][:, kc, 0:nb], func=AF.Copy, scale=gft[:, kc:kc + 1]),
                                reads=[f"nmx{j}", "gft"], writes=[f"nmh{j}"])
                        else:
                            P.op("act", lambda e, j=j, kc=kc, nb=nb, col=col: e.activation(
                                out=hb[j][:, kc, 0:nb], in_=xb[j][:, kc, 0:nb], func=AF.Identity,
                                scale=Asc[:, kc, col:col + 1], bias=modv[:, shift_base + kc, col:col + 1]),
                                reads=[f"nmx{j}", "A1", "A2", "modv"], writes=[f"nmh{j}"])
                    if final:
                        P.dma("pool", outT[:, t0:t0 + nb].rearrange("(c p) n -> p c n", p=128), hb[j][:, :, 0:nb],
                              reads=[f"nmh{j}"], writes=["outT"])
                    else:
                        P.dma("pool", hT[bi], hb[j][:], reads=[f"nmh{j}"], writes=["hT"])
                P.barrier()

        def gemm_stage(src_blocks, src_key, kc_n, wsrc, mc_n, epilogue, extra=None, G=4, nib=2, pair=False):
            with contextlib.ExitStack() as st:
                if pair:
                    G, nib = 2, 4
                ib = [SB(st, f"gm_i{i}", [128, kc_n, 512], BF16) for i in range(nib)]
                wt = [SB(st, f"gm_w{i}", [128, G, kc_n, 128], BF16) for i in range(2)]
                ctx = extra(st) if extra else None
                wi = 0
                step = 2 if pair else 1
                for b0 in range(0, len(blks), step):
                    grp = list(range(b0, min(b0 + step, len(blks))))
                    for bi in grp:
                        P.dma("sp", ib[bi % nib][:], src_blocks[bi], reads=[src_key], writes=[f"gmi{bi % nib}"])
                    for m0 in range(0, mc_n, G):
                        g = min(G, mc_n - m0)
                        wj = wi % 2
                        if pair:
                            pgrp = 4 * (wi % 2)
                        else:
                            pgrp = 4 * (wi % 2) if G == 4 else 2 * (wi % 4) if G == 2 else (wi % 8)
                        wi += 1
                        P.dma("sp", wt[wj][:, 0:g], wsrc[m0:m0 + g].rearrange("g p k m -> p g k m"),
                              reads=["wsrc"], writes=[f"gmw{wj}"])
                        for k_, bi in enumerate(grp):
                            nb = blks[bi][1]
                            j = bi % nib
                            pbase = pgrp + 2 * k_ if pair else pgrp
                            for gi in range(g):
                                pi = pbase + gi
                                for kc in range(kc_n):
                                    P.op("pe", lambda e, wj=wj, j=j, kc=kc, pi=pi, nb=nb, gi=gi: e.matmul(
                                        ps[pi][:, 0:nb], wt[wj][:, gi, kc, :], ib[j][:, kc, 0:nb], start=(kc == 0), stop=(kc == kc_n - 1)),
                                        reads=[f"gmw{wj}", f"gmi{j}"], writes=[f"psg{pbase}"])
                        for k_, bi in enumerate(grp):
                            pbase = pgrp + 2 * k_ if pair else pgrp
                            if pair:
                                epilogue(ctx, bi, blks[bi], m0, g, pbase, 2 * k_, (pgrp // 4) % 2)
                            else:
                                epilogue(ctx, bi, blks[bi], m0, g, pbase)
                P.barrier()

        PI = float(np.pi)

        def s5_stage(l):
            with contextlib.ExitStack() as st:
                c128 = SB(st, "s5_c128", [128, 4, 128], F32)
                ck = SB(st, "s5_ck", [128, 4], F32)
                negpi = SB(st, "s5_negpi", [128, 1], F32)
                are = SB(st, "s5_are", [128, 2, 16], F32)
                aim = SB(st, "s5_aim", [128, 2, 16], F32)
                stp = SB(st, "s5_stp", [128, 2, 16], F32)
                lre = SB(st, "s5_lre", [128, 2, 16], F32)
                lim = SB(st, "s5_lim", [128, 2, 16], F32)
                abr = SB(st, "s5_abr", [128, 2, 16], F32)
                abi = SB(st, "s5_abi", [128, 2, 16], F32)
                cfr = SB(st, "s5_cfr", [128, 2, 16], F32)
                cfi = SB(st, "s5_cfi", [128, 2, 16], F32)
                w1 = SB(st, "s5_w1", [128, 2, 16], F32)
                w2 = SB(st, "s5_w2", [128, 2, 16], F32)
                w3 = SB(st, "s5_w3", [128, 2, 16], F32)
                ddt = SB(st, "s5_dd", [128, 4], F32)
                uT = SB(st, "s5_u", [128, T], F32)
                acc = SB(st, "s5_acc", [128, T], F32)
                rows = SB(st, "s5_rows", [128, 4, 512], F32)
                Pm = SB(st, "s5_Pm", [128, 2, 512], F32)
                Pp = SB(st, "s5_Pp", [128, 2, 4, 128], F32)
                Bb = SB(st, "s5_Bb", [128, 2, 512], F32)
                Braw = SB(st, "s5_Braw", [128, 2, 512], F32)
                Cb = SB(st, "s5_Cb", [128, 2, 4, 128], F32)
                ph = SB(st, "s5_ph", [128, 512], F32)
                ph2 = SB(st, "s5_ph2", [128, 512], F32)
                mg = SB(st, "s5_mg", [128, 512], F32)
                tt = [SB(st, f"s5_t{i}", [128, 512], F32) for i in range(4)]
                Wt = [SB(st, f"s5_W{i}", [128, 2, 512], F32) for i in range(2)]
                tmp = SB(st, "s5_tmp", [128, 2, 4, 128], F32)
                Xt = SB(st, "s5_X", [128, 2, 4, 128], F32)
                xx = [SB(st, f"s5_x{i}", [128, 4, 128], F32) for i in range(4)]
                aS = SB(st, "s5_aS", [128, 2, 4], F32)
                sw = [SB(st, f"s5_sw{i}", [128, 4], F32) for i in range(4)]
                P.dma("sp", c128[:], cst128, writes=["c128"])
                P.dma("sp", ck[:], cstk, writes=["ck"])
                P.op("pool", lambda e: e.memset(negpi[:], -PI), writes=["negpi"])
                P.dma("sp", are[:], s5_are[l], writes=["are"])
                P.dma("sp", aim[:], s5_aim[l], writes=["aim"])
                P.dma("sp", stp[:], s5_lst[l], writes=["stp"])
                P.dma("sp", ddt[:], s5_dd[l], writes=["ddt"])
                V = lambda fn, r, w: P.op("dve", fn, reads=r, writes=w)

                I32 = mybir.dt.int32
                sc_i = SB(st, "s5_sci", [128, 512], I32)
                sc_r = SB(st, "s5_scr", [128, 512], F32)
                sc_f = SB(st, "s5_scf", [128, 512], F32)
                sc_m = SB(st, "s5_scm", [128, 512], F32)

                def sincos(theta_ap, sin_out, cos_out, scratch, rkeys, skey, wkeys):
                    shp = list(theta_ap.shape)
                    n = 1
                    for v in shp[1:]:
                        n *= v
                    flat = lambda ap: ap if len(shp) == 2 else ap.rearrange("p a b -> p (a b)")
                    th = flat(theta_ap)
                    ri_, rr_, rf_, rm_ = sc_i[:, 0:n], sc_r[:, 0:n], sc_f[:, 0:n], sc_m[:, 0:n]
                    for out_ap, off, wk in ((sin_out, 0.0, wkeys[0]), (cos_out, 0.25, wkeys[1])):
                        if out_ap is None:
                            continue
                        V(lambda e, off=off: e.tensor_scalar(out=rr_, in0=th, scalar1=1.0 / (2.0 * PI), scalar2=off, op0=ALU.mult, op1=ALU.add),
                          rkeys, ["scr"])
                        V(lambda e: e.tensor_copy(out=ri_, in_=rr_), ["scr"], ["sci"])
                        V(lambda e: e.tensor_copy(out=rf_, in_=ri_), ["sci"], ["scf"])
                        V(lambda e: e.tensor_tensor(out=rr_, in0=rr_, in1=rf_, op=ALU.subtract), ["scr", "scf"], ["scr"])
                        V(lambda e: e.tensor_scalar(out=rm_, in0=rr_, scalar1=0.5, scalar2=None, op0=ALU.is_gt), ["scr"], ["scm"])
                        V(lambda e: e.tensor_tensor(out=rr_, in0=rr_, in1=rm_, op=ALU.subtract), ["scr", "scm"], ["scr"])
                        V(lambda e: e.tensor_scalar(out=rm_, in0=rr_, scalar1=-0.5, scalar2=None, op0=ALU.is_lt), ["scr"], ["scm"])
                        V(lambda e: e.tensor_tensor(out=rr_, in0=rr_, in1=rm_, op=ALU.add), ["scr", "scm"], ["scr"])
                        P.op("act", lambda e, out_ap=out_ap: e.activation(out=flat(out_ap), in_=rr_, func=AF.Sin, scale=2.0 * PI),
                             reads=["scr"], writes=[wk])

                P.op("act", lambda e: e.activation(out=stp[:], in_=stp[:], func=AF.Exp), reads=["stp"], writes=["stp"])
                V(lambda e: e.tensor_tensor(out=lre[:], in0=are[:], in1=stp[:], op=ALU.mult), ["are", "stp"], ["lre"])
                V(lambda e: e.tensor_tensor(out=lim[:], in0=aim[:], in1=stp[:], op=ALU.mult), ["aim", "stp"], ["lim"])
                P.op("act", lambda e: e.activation(out=w1[:], in_=lre[:], func=AF.Exp), reads=["lre"], writes=["w1"])
                sincos(lim[:], w2[:], w3[:], cfr[:], ["lim"], "cfr", ["w2", "w3"])
                V(lambda e: e.tensor_tensor(out=abr[:], in0=w1[:], in1=w3[:], op=ALU.mult), ["w1", "w3"], ["abr"])
                V(lambda e: e.tensor_tensor(out=abi[:], in0=w1[:], in1=w2[:], op=ALU.mult), ["w1", "w2"], ["abi"])
                V(lambda e: e.tensor_scalar(out=w1[:], in0=abr[:], scalar1=-1.0, scalar2=None, op0=ALU.add), ["abr"], ["w1"])
                V(lambda e: e.tensor_tensor(out=w2[:], in0=w1[:], in1=are[:], op=ALU.mult), ["w1", "are"], ["w2"])
                V(lambda e: e.tensor_tensor(out=w3[:], in0=abi[:], in1=aim[:], op=ALU.mult), ["abi", "aim"], ["w3"])
                V(lambda e: e.tensor_tensor(out=cfr[:], in0=w2[:], in1=w3[:], op=ALU.add), ["w2", "w3"], ["cfr"])
                V(lambda e: e.tensor_tensor(out=w2[:], in0=abi[:], in1=are[:], op=ALU.mult), ["abi", "are"], ["w2"])
                V(lambda e: e.tensor_tensor(out=w3[:], in0=w1[:], in1=aim[:], op=ALU.mult), ["w1", "aim"], ["w3"])
                V(lambda e: e.tensor_tensor(out=cfi[:], in0=w2[:], in1=w3[:], op=ALU.subtract), ["w2", "w3"], ["cfi"])
                V(lambda e: e.tensor_tensor(out=w2[:], in0=are[:], in1=are[:], op=ALU.mult), ["are"], ["w2"])
                V(lambda e: e.tensor_tensor(out=w3[:], in0=aim[:], in1=aim[:], op=ALU.mult), ["aim"], ["w3"])
                V(lambda e: e.tensor_tensor(out=w2[:], in0=w2[:], in1=w3[:], op=ALU.add), ["w2", "w3"], ["w2"])
                V(lambda e: e.reciprocal(out=w2[:], in_=w2[:]), ["w2"], ["w2"])
                V(lambda e: e.tensor_tensor(out=cfr[:], in0=cfr[:], in1=w2[:], op=ALU.mult), ["cfr", "w2"], ["cfr"])
                V(lambda e: e.tensor_tensor(out=cfi[:], in0=cfi[:], in1=w2[:], op=ALU.mult), ["cfi", "w2"], ["cfi"])
                for d in range(2):
                    for ri, (tl, key) in enumerate(((lre, "lre"), (lim, "lim"), (cfr, "cfr"), (cfi, "cfi"))):
                        P.dma("pool", s5rows[d, ri].rearrange("(s p) -> p s", p=128), tl[:, d, :], reads=[key], writes=["s5rows"],
                              allow_slow_non_contiguous=True)

                for bk in range(4):
                    for t0 in range(0, T, 2112):
                        n = min(2112, T - t0)
                        P.dma("sp", uT[:, t0:t0 + n], pxA[bk * 128:(bk + 1) * 128, t0:t0 + n], reads=["pxT"], writes=["s5u"])
                    for d in range(2):
                        for ri in range(4):
                            P.dma("sp", rows[:, ri, :], s5rows[d, ri:ri + 1, bk * 512:(bk + 1) * 512].broadcast_to([128, 512]),
                                  reads=["s5rows"], writes=["rows"])
                        P.dma("sp", Braw[:, 0, :], s5_bre[l, d, bk], writes=["Braw"])
                        P.dma("sp", Braw[:, 1, :], s5_bim[l, d, bk], writes=["Braw"])
                        P.dma("sp", Cb[:, 0], s5_cre[l, d, bk], writes=["Cb"])
                        P.dma("sp", Cb[:, 1], s5_cim[l, d, bk], writes=["Cb"])
                        V(lambda e: e.tensor_scalar(out=Cb[:, 1], in0=Cb[:, 1], scalar1=-1.0, scalar2=None, op0=ALU.mult), ["Cb"], ["Cb"])
                        V(lambda e: e.tensor_tensor(out=tt[0][:], in0=Braw[:, 0, :], in1=rows[:, 2, :], op=ALU.mult), ["Braw", "rows"], ["t0"])
                        V(lambda e: e.tensor_tensor(out=tt[1][:], in0=Braw[:, 1, :], in1=rows[:, 3, :], op=ALU.mult), ["Braw", "rows"], ["t1"])
                        V(lambda e: e.tensor_tensor(out=Bb[:, 0, :], in0=tt[0][:], in1=tt[1][:], op=ALU.subtract), ["t0", "t1"], ["Bb"])
                        V(lambda e: e.tensor_tensor(out=tt[0][:], in0=Braw[:, 0, :], in1=rows[:, 3, :], op=ALU.mult), ["Braw", "rows"], ["t0"])
                        V(lambda e: e.tensor_tensor(out=tt[1][:], in0=Braw[:, 1, :], in1=rows[:, 2, :], op=ALU.mult), ["Braw", "rows"], ["t1"])
                        V(lambda e: e.tensor_tensor(out=Bb[:, 1, :], in0=tt[0][:], in1=tt[1][:], op=ALU.add), ["t0", "t1"], ["Bb"])
                        V(lambda e, d=d: e.tensor_scalar(out=ph[:], in0=rows[:, 1, :], scalar1=ck[:, d:d + 1], scalar2=None, op0=ALU.mult),
                          ["rows", "ck"], ["ph"])
                        P.op("act", lambda e, d=d: e.activation(out=mg[:], in_=rows[:, 0, :], func=AF.Exp, scale=ck[:, 2 + d:3 + d]),
                             reads=["rows", "ck"], writes=["mg"])
                        sincos(ph[:], tt[0][:], tt[1][:], ph2[:], ["ph"], "ph2", ["t0", "t1"])
                        V(lambda e: e.tensor_tensor(out=Pm[:, 0, :], in0=mg[:], in1=tt[1][:], op=ALU.mult), ["mg", "t1"], ["Pm"])
                        V(lambda e: e.scalar_tensor_tensor(out=Pm[:, 1, :], in0=mg[:], scalar=-1.0, in1=tt[0][:], op0=ALU.mult, op1=ALU.mult),
                          ["mg", "t0"], ["Pm"])
                        for s_ in range(4):
                            sl = bk * 4 + s_
                            V(lambda e, d=d, sl=sl, s_=s_: e.tensor_scalar(out=ph[:, s_ * 128:(s_ + 1) * 128], in0=c128[:, 2 + d, :],
                                                                         scalar1=lim[:, d, sl:sl + 1], scalar2=None, op0=ALU.mult),
                              ["c128", "lim"], ["ph"])
                            P.op("act", lambda e, d=d, sl=sl, s_=s_: e.activation(out=mg[:, s_ * 128:(s_ + 1) * 128], in_=c128[:, 2 + d, :],
                                                                                  func=AF.Exp, scale=lre[:, d, sl:sl + 1]),
                                 reads=["c128", "lre"], writes=["mg"])
                        sincos(ph[:], tt[0][:], tt[1][:], ph2[:], ["ph"], "ph2", ["t0", "t1"])
                        V(lambda e: e.tensor_tensor(out=Pp[:, 0].rearrange("p s t -> p (s t)"), in0=mg[:], in1=tt[1][:], op=ALU.mult),
                          ["mg", "t1"], ["Pp"])
                        V(lambda e: e.tensor_tensor(out=Pp[:, 1].rearrange("p s t -> p (s t)"), in0=mg[:], in1=tt[0][:], op=ALU.mult),
                          ["mg", "t0"], ["Pp"])
                        P.op("pool", lambda e: e.memset(aS[:], 0.0), writes=["aS"])
                        tri = c128[:, d, :]
                        nq = nctx // 128
                        chunks = [nx + 128 * q for q in range(nq)] + [128 * q for q in range(nx // 128)]
                        if d == 1:
                            chunks = [nx + 128 * q for q in range(nq)][::-1] + [128 * q for q in range(nx // 128)][::-1]
                        last = 127 if d == 0 else 0
                        for it, t0 in enumerate(chunks):
                            b = it % 2
                            pz = 2 + 2 * b
                            for ri in range(2):
                                P.op("pe", lambda e, ri=ri, t0=t0: e.matmul(ps[ri][:, 0:512], uT[:, t0:t0 + 128], Bb[:, ri, :], start=True, stop=True),
                                     reads=["s5u", "Bb"], writes=[f"s5p{ri}"])
                            V(lambda e: e.tensor_tensor(out=tt[0][:], in0=Pm[:, 0, :], in1=ps[0][:, 0:512], op=ALU.mult), ["Pm", "s5p0"], ["t0"])
                            V(lambda e: e.tensor_tensor(out=tt[1][:], in0=Pm[:, 1, :], in1=ps[1][:, 0:512], op=ALU.mult), ["Pm", "s5p1"], ["t1"])
                            V(lambda e: e.tensor_tensor(out=tt[2][:], in0=Pm[:, 0, :], in1=ps[1][:, 0:512], op=ALU.mult), ["Pm", "s5p1"], ["t2"])
                            V(lambda e: e.tensor_tensor(out=tt[3][:], in0=Pm[:, 1, :], in1=ps[0][:, 0:512], op=ALU.mult), ["Pm", "s5p0"], ["t3"])
                            P.op("pool", lambda e, b=b: e.tensor_tensor(out=Wt[b][:, 0, :], in0=tt[0][:], in1=tt[1][:], op=ALU.subtract),
                                 reads=["t0", "t1"], writes=[f"W{b}"])
                            P.op("pool", lambda e, b=b: e.tensor_tensor(out=Wt[b][:, 1, :], in0=tt[2][:], in1=tt[3][:], op=ALU.add),
                                 reads=["t2", "t3"], writes=[f"W{b}"])
                            for ri in range(2):
                                for s_ in range(4):
                                    P.op("pe", lambda e, b=b, ri=ri, s_=s_, pz=pz, tri=tri: e.matmul(
                                        ps[pz + ri][:, s_ * 128:(s_ + 1) * 128], Wt[b][:, ri, s_ * 128:(s_ + 1) * 128], tri, start=True, stop=True),
                                        reads=[f"W{b}", "c128"], writes=[f"s5p{pz + ri}"])
                            for ri in range(2):
                                V(lambda e, ri=ri, pz=pz: e.tensor_tensor(out=tmp[:, ri], in0=ps[pz + ri][:, 0:512].rearrange("p (s t) -> p s t", s=4),
                                                                        in1=aS[:, ri, :].unsqueeze(2).to_broadcast([128, 4, 128]), op=ALU.add),
                                  [f"s5p{pz + ri}", "aS"], ["tmp"])
                            V(lambda e: e.tensor_tensor(out=xx[0][:], in0=Pp[:, 0], in1=tmp[:, 0], op=ALU.mult), ["Pp", "tmp"], ["x0"])
                            V(lambda e: e.tensor_tensor(out=xx[1][:], in0=Pp[:, 1], in1=tmp[:, 1], op=ALU.mult), ["Pp", "tmp"], ["x1"])
                            P.op("pool", lambda e: e.tensor_tensor(out=xx[2][:], in0=Pp[:, 0], in1=tmp[:, 1], op=ALU.mult), reads=["Pp", "tmp"], writes=["x2"])
                            P.op("pool", lambda e: e.tensor_tensor(out=xx[3][:], in0=Pp[:, 1], in1=tmp[:, 0], op=ALU.mult), reads=["Pp", "tmp"], writes=["x3"])
                            V(lambda e: e.tensor_tensor(out=Xt[:, 0], in0=xx[0][:], in1=xx[1][:], op=ALU.subtract), ["x0", "x1"], ["X"])
                            P.op("pool", lambda e: e.tensor_tensor(out=Xt[:, 1], in0=xx[2][:], in1=xx[3][:], op=ALU.add), reads=["x2", "x3"], writes=["X"])
                            ar_, ai_ = abr[:, d, bk * 4:bk * 4 + 4], abi[:, d, bk * 4:bk * 4 + 4]
                            xr_, xi_ = Xt[:, 0, :, last], Xt[:, 1, :, last]
                            V(lambda e, ar_=ar_, xr_=xr_: e.tensor_tensor(out=sw[0][:], in0=ar_, in1=xr_, op=ALU.mult), ["abr", "X"], ["sw0"])
                            V(lambda e, ai_=ai_, xi_=xi_: e.tensor_tensor(out=sw[1][:], in0=ai_, in1=xi_, op=ALU.mult), ["abi", "X"], ["sw1"])
                            V(lambda e, ar_=ar_, xi_=xi_: e.tensor_tensor(out=sw[2][:], in0=ar_, in1=xi_, op=ALU.mult), ["abr", "X"], ["sw2"])
                            V(lambda e, ai_=ai_, xr_=xr_: e.tensor_tensor(out=sw[3][:], in0=ai_, in1=xr_, op=ALU.mult), ["abi", "X"], ["sw3"])
                            V(lambda e: e.tensor_tensor(out=aS[:, 0, :], in0=sw[0][:], in1=sw[1][:], op=ALU.subtract), ["sw0", "sw1"], ["aS"])
                            V(lambda e: e.tensor_tensor(out=aS[:, 1, :], in0=sw[2][:], in1=sw[3][:], op=ALU.add), ["sw2", "sw3"], ["aS"])
                            for ri in range(2):
                                for s_ in range(4):
                                    P.op("pe", lambda e, ri=ri, s_=s_: e.matmul(ps[6][:, 0:128], Cb[:, ri, s_, :], Xt[:, ri, s_, :],
                                                                                start=(ri == 0 and s_ == 0), stop=(ri == 1 and s_ == 3)),
                                         reads=["Cb", "X"], writes=["s5p6"])
                            if d == 0:
                                V(lambda e, t0=t0: e.tensor_copy(out=acc[:, t0:t0 + 128], in_=ps[6][:, 0:128]), ["s5p6"], ["s5acc"])
                            else:
                                V(lambda e, t0=t0: e.tensor_tensor(out=acc[:, t0:t0 + 128], in0=acc[:, t0:t0 + 128], in1=ps[6][:, 0:128], op=ALU.add),
                                  ["s5p6", "s5acc"], ["s5acc"])
                    for bi, (t0, nb, col, _, _) in enumerate(blks):
                        yv = acc[:, t0:t0 + nb]
                        V(lambda e, yv=yv, t0=t0, nb=nb, bk=bk: e.scalar_tensor_tensor(out=yv, in0=uT[:, t0:t0 + nb], scalar=ddt[:, bk:bk + 1], in1=yv,
                                                                                     op0=ALU.mult, op1=ALU.add), ["s5u", "ddt", "s5acc"], ["s5acc"])
                        P.op("pool", lambda e, yv=yv, nb=nb: e.tensor_tensor(out=tt[0][:, 0:nb], in0=yv, in1=yv, op=ALU.mult), reads=["s5acc"], writes=["t0"])
                        V(lambda e, nb=nb: e.tensor_scalar(out=tt[0][:, 0:nb], in0=tt[0][:, 0:nb], scalar1=0.044715, scalar2=1.0, op0=ALU.mult, op1=ALU.add),
                          ["t0"], ["t0"])
                        P.op("pool", lambda e, yv=yv, nb=nb: e.tensor_tensor(out=tt[0][:, 0:nb], in0=tt[0][:, 0:nb], in1=yv, op=ALU.mult),
                             reads=["t0", "s5acc"], writes=["t0"])
                        P.op("act", lambda e, nb=nb: e.activation(out=tt[0][:, 0:nb], in_=tt[0][:, 0:nb], func=AF.Sigmoid, scale=1.5957691216057308),
                             reads=["t0"], writes=["t0"])
                        ob = Wt[bi % 2][:, 0, :].bitcast(BF16)
                        V(lambda e, yv=yv, nb=nb, ob=ob: e.tensor_tensor(out=ob[:, 0:nb], in0=tt[0][:, 0:nb], in1=yv, op=ALU.mult),
                          ["t0", "s5acc"], [f"W{bi % 2}"])
                        P.dma("pool", ygT[bi, :, bk, 0:nb], ob[:, 0:nb], reads=[f"W{bi % 2}"], writes=["ygT"])
                P.barrier()
            with contextlib.ExitStack() as st:
                gwf = SB(st, "s5_gwf", [128, 4, 4, 128], F32)
                gwb = SB(st, "s5_gwb", [128, 4, 4, 128], BF16)
                gbt = SB(st, "s5_gbt", [128, 4], F32)
                yb_ = [SB(st, f"s5_yb{i}", [128, 4, 512], BF16) for i in range(2)]
                sg = [SB(st, f"s5_sg{i}", [128, 512], F32) for i in range(2)]
                ot = [SB(st, f"s5_ot{i}", [128, 512], BF16) for i in range(2)]
                P.dma("sp", gwf[:], s5_gw[l], writes=["gwf"])
                P.dma("sp", gbt[:], s5_gb[l], writes=["gbt"])
                P.op("dve", lambda e: e.tensor_copy(out=gwb[:], in_=gwf[:]), reads=["gwf"], writes=["gwb"])
                it = 0
                for bi, (t0, nb, col, _, _) in enumerate(blks):
                    j = bi % 2
                    P.dma("sp", yb_[j][:], ygT[bi], reads=["ygT"], writes=[f"yb{j}"])
                    for m in range(4):
                        k_ = it % 2
                        it += 1
                        for kc in range(4):
                            P.op("pe", lambda e, j=j, m=m, kc=kc, k_=k_, nb=nb: e.matmul(ps[k_][:, 0:nb], gwb[:, m, kc, :], yb_[j][:, kc, 0:nb],
                                                                                      start=(kc == 0), stop=(kc == 3)),
                                 reads=["gwb", f"yb{j}"], writes=[f"glu{k_}"])
                        P.op("act", lambda e, k_=k_, m=m, nb=nb: e.activation(out=sg[k_][:, 0:nb], in_=ps[k_][:, 0:nb], func=AF.Sigmoid,
                                                                             bias=gbt[:, m:m + 1]),
                             reads=[f"glu{k_}", "gbt"], writes=[f"sg{k_}"])
                        P.op("dve", lambda e, k_=k_, j=j, m=m, nb=nb: e.tensor_tensor(out=ot[k_][:, 0:nb], in0=yb_[j][:, m, 0:nb], in1=sg[k_][:, 0:nb],
                                                                                     op=ALU.mult),
                             reads=[f"yb{j}", f"sg{k_}"], writes=[f"ot{k_}"])
                        P.dma("pool", yT[bi, :, m, 0:nb], ot[k_][:, 0:nb], reads=[f"ot{k_}"], writes=["yT"])
                P.barrier()


        def hyena_stage(l, with_ctx):
            seqs = [q for q in HSEQ if (q[0] == "x" or with_ctx)]
            import os
            if os.environ.get("HY_REV"):
                seqs = seqs[::-1]
            def h0_pass(nm, toff, n_):
                hc = hyc[nm]
                N2, A = hc["N2"], hc["A"]
                NT = 2 * n_
                with contextlib.ExitStack() as st:
                    I32 = mybir.dt.int32
                    w1t = SB(st, "hy_w1", [33, 64], F32)
                    w2t = SB(st, "hy_w2", [64, 64], F32)
                    w3t = SB(st, "hy_w3", [65, 2, 4, 256], F32)
                    fbt = SB(st, "hy_fbt", [64, 6], F32)
                    fb1 = SB(st, "hy_fb1", [64, 2], F32)
                    h2a = SB(st, "hy_h2a", [65, NT], BF16)
                    w3b = SB(st, "hy_w3b", [65, 2, 4, 256], BF16)
                    ft = [SB(st, f"hy_ft{i}", [33, 512], F32) for i in range(2)]
                    th = SB(st, "hy_th", [64, 512], F32)
                    h1 = SB(st, "hy_h1", [64, 512], F32)
                    sci = SB(st, "hy_sci", [128, 512], I32)
                    scr = SB(st, "hy_scr", [128, 512], F32)
                    scf = SB(st, "hy_scf", [128, 512], F32)
                    scm = SB(st, "hy_scm", [128, 512], F32)
                    mskt = SB(st, "hy_msk", [N2, 2], F32)
                    filt = SB(st, "hy_filt", [N2, 128, 128], F32)
                    dct = [SB(st, f"hy_dct{i}", [N2, 16, 128], F32) for i in range(2)]
                    red = SB(st, "hy_red", [N2, 128], F32)
                    inv = SB(st, "hy_inv", [N2, 128], F32)
                    F2f = SB(st, "hy_F2f", [N2, 2, N2], F32)
                    TW = SB(st, "hy_TW", [128, 2, N2], F32)
                    F1 = SB(st, "hy_F1t", [128, 3, 128], F32)
                    CS = 16
                    Y1t = SB(st, "hy_Y1t", [128, CS, 2, N2], F32)
                    xs = [SB(st, f"hy_xs{i}", [128, 2, 512], F32) for i in range(2)]
                    tq = [SB(st, f"hy_tq{i}", [128, 512], F32) for i in range(4)]
                    V = lambda fn, r, w: P.op("dve", fn, reads=r, writes=w)

                    def sin_red(theta_ap, np_, ncol, out_ap, rkeys, wkey):
                        rr_, ri_, rf_, rm_ = scr[0:np_, 0:ncol], sci[0:np_, 0:ncol], scf[0:np_, 0:ncol], scm[0:np_, 0:ncol]
                        V(lambda e: e.tensor_scalar(out=rr_, in0=theta_ap, scalar1=1.0 / (2.0 * PI), scalar2=None, op0=ALU.mult), rkeys, ["hscr"])
                        V(lambda e: e.tensor_copy(out=ri_, in_=rr_), ["hscr"], ["hsci"])
                        V(lambda e: e.tensor_copy(out=rf_, in_=ri_), ["hsci"], ["hscf"])
                        V(lambda e: e.tensor_tensor(out=rr_, in0=rr_, in1=rf_, op=ALU.subtract), ["hscr", "hscf"], ["hscr"])
                        V(lambda e: e.tensor_scalar(out=rm_, in0=rr_, scalar1=0.5, scalar2=None, op0=ALU.is_gt), ["hscr"], ["hscm"])
                        V(lambda e: e.tensor_tensor(out=rr_, in0=rr_, in1=rm_, op=ALU.subtract), ["hscr", "hscm"], ["hscr"])
                        V(lambda e: e.tensor_scalar(out=rm_, in0=rr_, scalar1=-0.5, scalar2=None, op0=ALU.is_lt), ["hscr"], ["hscm"])
                        V(lambda e: e.tensor_tensor(out=rr_, in0=rr_, in1=rm_, op=ALU.add), ["hscr", "hscm"], ["hscr"])
                        P.op("act", lambda e: e.activation(out=out_ap, in_=rr_, func=AF.Sin, scale=2.0 * PI), reads=["hscr"], writes=[wkey])

                    P.dma("sp", w1t[:], hy_w1[l], writes=["hw1"])
                    P.dma("sp", w2t[:], hy_w2[l], writes=["hw2"])
                    P.dma("sp", w3t[:], hy_w3[l], writes=["hw3f"])
                    P.op("pool", lambda e: e.tensor_copy(out=w3b[:], in_=w3t[:]), reads=["hw3f"], writes=["hw3"])
                    P.dma("sp", fbt[:], hy_fb[l], writes=["hfb"])
                    P.dma("sp", mskt[:], hc["msk"], writes=["hmsk"])
                    P.dma("sp", F2f[:], hc["F2f"], writes=["hF2f"])
                    P.dma("sp", TW[:], hc["TW"], writes=["hTW"])
                    P.dma("sp", F1[:], hy_F1, writes=["hF1"])
                    V(lambda e: e.tensor_tensor(out=fb1[:, 0:1], in0=fbt[:, 0:1], in1=fbt[:, 1:2], op=ALU.mult), ["hfb"], ["hfb1"])
                    V(lambda e: e.tensor_tensor(out=fb1[:, 1:2], in0=fbt[:, 2:3], in1=fbt[:, 3:4], op=ALU.mult), ["hfb"], ["hfb1"])
                    P.op("pool", lambda e: e.memset(h2a[:], 1.0), writes=["hh2a"])
                    P.barrier()
                    for ci, c0 in enumerate(range(0, NT, 512)):
                        j = ci % 2
                        P.dma("sp", ft[j][:], hc["feats"][:, c0:c0 + 512], writes=[f"hft{j}"])
                        P.op("pe", lambda e, j=j: e.matmul(ps[0][0:64, 0:512], w1t[:], ft[j][:], start=True, stop=True),
                             reads=["hw1", f"hft{j}"], writes=["hp0"])
                        V(lambda e: e.tensor_scalar(out=th[:], in0=ps[0][0:64, 0:512], scalar1=fbt[:, 0:1], scalar2=fb1[:, 0:1], op0=ALU.mult, op1=ALU.add),
                          ["hp0", "hfb", "hfb1"], ["hth"])
                        sin_red(th[:], 64, 512, h1[:], ["hth"], "hh1")
                        P.op("pe", lambda e: e.matmul(ps[1][0:64, 0:512], w2t[:], h1[:], start=True, stop=True), reads=["hw2", "hh1"], writes=["hp1"])
                        V(lambda e: e.tensor_scalar(out=th[:], in0=ps[1][0:64, 0:512], scalar1=fbt[:, 2:3], scalar2=fb1[:, 1:2], op0=ALU.mult, op1=ALU.add),
                          ["hp1", "hfb", "hfb1"], ["hth"])
                        sin_red(th[:], 64, 512, h2a[0:64, c0:c0 + 512], ["hth"], "hh2a")
                    if only == "hy" and nm == "x":
                        V(lambda e: e.tensor_copy(out=h1[:, 0:128], in_=h2a[0:64, 0:128]), ["hh2a"], ["hh1"])
                        V(lambda e: e.tensor_copy(out=h1[:, 128:256], in_=h2a[0:64, NT - 128:NT]), ["hh2a"], ["hh1"])
                        P.dma("pool", dbg_out[:, 2, :], h1[:, 0:128], reads=["hh1"], writes=["dbg"])
                        P.dma("pool", dbg_out[:, 0, :], h1[:, 128:256], reads=["hh1"], writes=["dbg"])
                    P.barrier()
                    h2v = h2a[:].rearrange("k (a b) -> k a b", b=128)
                    for o in range(2):
                        for cb in range(4):
                            for b_ in range(128):
                                pb = 2 + (b_ % 2)
                                P.op("pe", lambda e, b_=b_, pb=pb, o=o, cb=cb: e.matmul(ps[pb][0:N2, 0:256], h2v[:, :, b_], w3b[:, o, cb, :], start=True, stop=True),
                                     reads=["hh2a", "hw3"], writes=[f"hp{pb}"])
                                V(lambda e, b_=b_, pb=pb: e.tensor_scalar(out=filt[:, :, b_], in0=ps[pb][0:N2, 0:128], scalar1=mskt[:, 0:1], scalar2=None, op0=ALU.mult),
                                  [f"hp{pb}", "hmsk"], ["hfilt"])
                                V(lambda e, b_=b_, pb=pb: e.scalar_tensor_tensor(out=filt[:, :, b_], in0=ps[pb][0:N2, 128:256], scalar=mskt[:, 1:2], in1=filt[:, :, b_],
                                                                                 op0=ALU.mult, op1=ALU.add), [f"hp{pb}", "hmsk", "hfilt"], ["hfilt"])
                            P.barrier()
                            for pc_ in range(8):
                                j = pc_ % 2
                                P.dma("sp", dct[j][:], hc["dec"][cb, :, pc_ * 16:(pc_ + 1) * 16, :], writes=[f"hdct{j}"])
                                eng = "dve" if pc_ % 2 == 0 else "pool"
                                P.op(eng, lambda e, j=j, pc_=pc_: e.tensor_tensor(out=filt[:, pc_ * 16:(pc_ + 1) * 16, :], in0=filt[:, pc_ * 16:(pc_ + 1) * 16, :],
                                                                                 in1=dct[j][:], op=ALU.mult), reads=["hfilt", f"hdct{j}"], writes=["hfilt"])
                            P.barrier()
                            for c_ in range(128):
                                P.op("act", lambda e, c_=c_: e.activation(out=scr[0:N2, 0:128], in_=filt[:, c_, :], func=AF.Abs, accum_out=red[:, c_:c_ + 1]),
                                     reads=["hfilt"], writes=["hred", "hscr"])
                            P.op("act", lambda e: e.activation(out=scr[0:N2, 0:128], in_=filt[:, 0, :], func=AF.Abs), reads=["hfilt", "hred"], writes=["hred", "hscr"])
                            P.op("pe", lambda e: e.matmul(ps[4][0:N2, 0:128], ones[0:N2, 0:N2], red[:], start=True, stop=True), reads=["ones", "hred"], writes=["hp4"])
                            V(lambda e: e.tensor_scalar(out=inv[:], in0=ps[4][0:N2, 0:128], scalar1=EPS, scalar2=None, op0=ALU.add), ["hp4"], ["hinv"])
                            if only == "hy" and nm == "x" and o == 0 and cb == 0:
                                P.dma("pool", dbg_out[:, 3, :], filt[:, 5, :], reads=["hfilt"], writes=["dbg"])
                            V(lambda e: e.reciprocal(out=inv[:], in_=inv[:]), ["hinv"], ["hinv"])
                            for pc_ in range(8):
                                cs_ = slice(pc_ * 16, (pc_ + 1) * 16)
                                eng = "dve" if pc_ % 2 == 0 else "pool"
                                P.op(eng, lambda e, cs_=cs_: e.tensor_tensor(out=filt[:, cs_, :], in0=filt[:, cs_, :],
                                                                          in1=inv[:, cs_].unsqueeze(2).to_broadcast([N2, 16, 128]), op=ALU.mult),
                                     reads=["hfilt", "hinv"], writes=["hfilt"])
                            P.barrier()
                            if only == "hy" and nm == "x" and o == 0 and cb == 0:
                                P.dma("pool", dbg_out[:, 1, :], filt[:, 5, :], reads=["hfilt"], writes=["dbg"])
                            fwd_transform(filt, N2, N2, hc, F2f, TW, F1, Y1t, CS, xs, tq,
                                          lambda c0, ncg, xre, xim, o=o, cb=cb, hc=hc: store_spec(hc, o, cb, c0, ncg, xre, xim), V)
                    P.barrier()

            for nm_, toff_, n__ in seqs:
                h0_pass(nm_, toff_, n__)

            with contextlib.ExitStack() as st:
                cwt_ = SB(st, "hy_cw", [128, 12, 3], F32)
                cbt_ = SB(st, "hy_cb", [128, 12], F32)
                aw = [SB(st, f"hy_a{i}", [128, 514], F32) for i in range(2)]
                ow = [SB(st, f"hy_o{i}", [128, 512], F32) for i in range(2)]
                P.dma("sp", cwt_[:], hy_cw[l], writes=["hcw"])
                P.dma("sp", cbt_[:], hy_cb[l], writes=["hcb"])
                it = 0
                for bi, (t0, nb, col, le, re) in enumerate(blks):
                    for jc in range(12):
                        j = it % 2
                        it += 1
                        r0 = 3616 + jc * 128
                        lo = t0 if le else t0 - 1
                        hi = t0 + nb if re else t0 + nb + 1
                        if le or re:
                            P.op("pool", lambda e, j=j: e.memset(aw[j][:], 0.0), writes=[f"hya{j}"])
                        P.dma("sp", aw[j][:, (lo - (t0 - 1)):(hi - (t0 - 1))], pxA[r0:r0 + 128, lo:hi], reads=["pxT"], writes=[f"hya{j}"])
                        P.op("dve", lambda e, j=j, jc=jc, nb=nb: e.tensor_scalar(out=ow[j][:, 0:nb], in0=aw[j][:, 0:nb], scalar1=cwt_[:, jc, 0:1], scalar2=cbt_[:, jc:jc + 1],
                                                                               op0=ALU.mult, op1=ALU.add), reads=[f"hya{j}", "hcw", "hcb"], writes=[f"hyo{j}"])
                        P.op("dve", lambda e, j=j, jc=jc, nb=nb: e.scalar_tensor_tensor(out=ow[j][:, 0:nb], in0=aw[j][:, 1:nb + 1], scalar=cwt_[:, jc, 1:2], in1=ow[j][:, 0:nb],
                                                                                      op0=ALU.mult, op1=ALU.add), reads=[f"hya{j}", "hcw", f"hyo{j}"], writes=[f"hyo{j}"])
                        P.op("dve", lambda e, j=j, jc=jc, nb=nb: e.scalar_tensor_tensor(out=ow[j][:, 0:nb], in0=aw[j][:, 2:nb + 2], scalar=cwt_[:, jc, 2:3], in1=ow[j][:, 0:nb],
                                                                                      op0=ALU.mult, op1=ALU.add), reads=[f"hya{j}", "hcw", f"hyo{j}"], writes=[f"hyo{j}"])
                        P.dma("pool", hvT[jc // 4, (jc % 4) * 128:(jc % 4 + 1) * 128, t0:t0 + nb], ow[j][:, 0:nb], reads=[f"hyo{j}"], writes=["hvT"])
                P.barrier()

            for o in range(2):
                src = hvT[0] if o == 0 else hy1T
                for nm, toff, n_ in seqs:
                    conv_transform(src, nm, toff, n_, o)
                hy_gate(l, o, src, with_ctx)

        def hy_gate(l, o, src, with_ctx):
            if True:
                with contextlib.ExitStack() as st:
                    bt_ = SB(st, "hy_bias", [128, 2, 4], F32)
                    cv = [SB(st, f"hy_cv{i}", [128, 512], F32) for i in range(2)]
                    yv = [SB(st, f"hy_yv{i}", [128, 512], F32) for i in range(2)]
                    gv = [SB(st, f"hy_gv{i}", [128, 512], F32) for i in range(2)]
                    ob = [SB(st, f"hy_ob{i}", [128, 512], BF16) for i in range(2)]
                    P.dma("sp", bt_[:], hy_bias[l], writes=["hbias"])
                    it = 0
                    for bi, (t0, nb, col, le, re) in enumerate(blks):
                        if col == 1 and not with_ctx:
                            continue
                        for cb in range(4):
                            j = it % 2
                            it += 1
                            rs_ = slice(cb * 128, (cb + 1) * 128)
                            P.dma("sp", cv[j][:, 0:nb], hc1T[rs_, t0:t0 + nb], reads=["hc1T"], writes=[f"hcv{j}"])
                            P.dma("sp", yv[j][:, 0:nb], src[rs_, t0:t0 + nb], reads=["hvT", "hy1T"], writes=[f"hyv{j}"])
                            P.dma("sp", gv[j][:, 0:nb], hvT[1 + o, rs_, t0:t0 + nb], reads=["hvT"], writes=[f"hgv{j}"])
                            P.op("dve", lambda e, j=j, nb=nb, o=o, cb=cb: e.scalar_tensor_tensor(out=cv[j][:, 0:nb], in0=yv[j][:, 0:nb], scalar=bt_[:, o, cb:cb + 1],
                                                                                               in1=cv[j][:, 0:nb], op0=ALU.mult, op1=ALU.add),
                                 reads=[f"hyv{j}", "hbias", f"hcv{j}"], writes=[f"hcv{j}"])
                            if o == 0:
                                P.op("pool", lambda e, j=j, nb=nb: e.tensor_tensor(out=cv[j][:, 0:nb], in0=cv[j][:, 0:nb], in1=gv[j][:, 0:nb], op=ALU.mult),
                                     reads=[f"hcv{j}", f"hgv{j}"], writes=[f"hcv{j}"])
                                P.dma("pool", hy1T[rs_, t0:t0 + nb], cv[j][:, 0:nb], reads=[f"hcv{j}"], writes=["hy1T"])
                            else:
                                P.op("pool", lambda e, j=j, nb=nb: e.tensor_tensor(out=ob[j][:, 0:nb], in0=cv[j][:, 0:nb], in1=gv[j][:, 0:nb], op=ALU.mult),
                                     reads=[f"hcv{j}", f"hgv{j}"], writes=[f"hob{j}"])
                                P.dma("pool", yT[bi, :, 4 + cb, 0:nb], ob[j][:, 0:nb], reads=[f"hob{j}"], writes=["yT"])
                    P.barrier()

        def store_spec(hc, o, cb, c0, ncg, xre, xim):
            N2 = hc["N2"]
            P.dma("pool", hc["fspec"][o, cb, :, 0, c0:c0 + ncg, :], xre.rearrange("p (c k) -> p c k", k=N2), reads=["hxs"], writes=["fspec"])
            P.dma("pool", hc["fspec"][o, cb, :, 1, c0:c0 + ncg, :], xim.rearrange("p (c k) -> p c k", k=N2), reads=["hxs"], writes=["fspec"])

        def fwd_transform(Yin, K_rows, N2, hc, F2f, TW, F1, Y1t, CS, xs, tq, consume, V):
            cpb = max(1, min(CS, 512 // (2 * N2)))
            cgB = max(1, min(CS, 512 // N2))
            xi = [0]
            for cs0 in range(0, 128, CS):
                for g0 in range(0, CS, cpb):
                    pb = 5 + ((cs0 // CS * (CS // cpb) + g0 // cpb) % 2)
                    for ci in range(cpb):
                        c = cs0 + g0 + ci
                        P.op("pe", lambda e, c=c, ci=ci, pb=pb: e.matmul(ps[pb][:, ci * 2 * N2:(ci + 1) * 2 * N2], Yin[0:K_rows, c, :],
                                                                          F2f[0:K_rows].rearrange("k r n -> k (r n)"), start=True, stop=True),
                             reads=["hfilt", "hF2f"], writes=[f"hp{pb}"])
                    pv = ps[pb][:, 0:cpb * 2 * N2].rearrange("p (c r k) -> p c r k", c=cpb, r=2)
                    twr = TW[:, 0, :].unsqueeze(1).to_broadcast([128, cpb, N2])
                    twi = TW[:, 1, :].unsqueeze(1).to_broadcast([128, cpb, N2])
                    t0v, t1v, t2v, t3v = [tq[i][:, 0:cpb * N2].rearrange("p (c k) -> p c k", k=N2) for i in range(4)]
                    V(lambda e, pv=pv, twr=twr, t0v=t0v: e.tensor_tensor(out=t0v, in0=pv[:, :, 0, :], in1=twr, op=ALU.mult), [f"hp{pb}", "hTW"], ["htq0"])
                    V(lambda e, pv=pv, twi=twi, t1v=t1v: e.tensor_tensor(out=t1v, in0=pv[:, :, 1, :], in1=twi, op=ALU.mult), [f"hp{pb}", "hTW"], ["htq1"])
                    V(lambda e, pv=pv, twi=twi, t2v=t2v: e.tensor_tensor(out=t2v, in0=pv[:, :, 0, :], in1=twi, op=ALU.mult), [f"hp{pb}", "hTW"], ["htq2"])
                    V(lambda e, pv=pv, twr=twr, t3v=t3v: e.tensor_tensor(out=t3v, in0=pv[:, :, 1, :], in1=twr, op=ALU.mult), [f"hp{pb}", "hTW"], ["htq3"])
                    P.op("pool", lambda e, g0=g0, t0v=t0v, t1v=t1v: e.tensor_tensor(out=Y1t[:, g0:g0 + cpb, 0, :], in0=t0v, in1=t1v, op=ALU.subtract),
                         reads=["htq0", "htq1"], writes=["hY1t"])
                    P.op("pool", lambda e, g0=g0, t2v=t2v, t3v=t3v: e.tensor_tensor(out=Y1t[:, g0:g0 + cpb, 1, :], in0=t2v, in1=t3v, op=ALU.add),
                         reads=["htq2", "htq3"], writes=["hY1t"])
                for g0 in range(0, CS, cgB):
                    j = xi[0] % 2
                    xi[0] += 1
                    yre = Y1t[:, g0:g0 + cgB, 0, :]
                    yim = Y1t[:, g0:g0 + cgB, 1, :]
                    ncol = cgB * N2
                    P.op("pe", lambda e, yre=yre, ncol=ncol: e.matmul(ps[7][:, 0:ncol], F1[:, 0, :], yre, start=True, stop=False), reads=["hF1", "hY1t"], writes=["hp7"])
                    P.op("pe", lambda e, yim=yim, ncol=ncol: e.matmul(ps[7][:, 0:ncol], F1[:, 2, :], yim, start=False, stop=True), reads=["hF1", "hY1t"], writes=["hp7"])
                    V(lambda e, j=j, ncol=ncol: e.tensor_copy(out=xs[j][:, 0, 0:ncol], in_=ps[7][:, 0:ncol]), ["hp7"], ["hxs"])
                    P.op("pe", lambda e, yim=yim, ncol=ncol: e.matmul(ps[7][:, 0:ncol], F1[:, 0, :], yim, start=True, stop=False), reads=["hF1", "hY1t"], writes=["hp7"])
                    P.op("pe", lambda e, yre=yre, ncol=ncol: e.matmul(ps[7][:, 0:ncol], F1[:, 1, :], yre, start=False, stop=True), reads=["hF1", "hY1t"], writes=["hp7"])
                    V(lambda e, j=j, ncol=ncol: e.tensor_copy(out=xs[j][:, 1, 0:ncol], in_=ps[7][:, 0:ncol]), ["hp7"], ["hxs"])
                    consume(cs0 + g0, cgB, xs[j][:, 0, 0:ncol], xs[j][:, 1, 0:ncol])

        def conv_transform(src, nm, toff, n_, o):
            hc = hyc[nm]
            N2, A = hc["N2"], hc["A"]
            with contextlib.ExitStack() as st:
                F2f = SB(st, "hc_F2f", [N2, 2, N2], F32)
                TW = SB(st, "hc_TW", [128, 2, N2], F32)
                F1 = SB(st, "hc_F1", [128, 3, 128], F32)
                F2i = SB(st, "hc_F2i", [N2, 2, A], F32)
                TWT = SB(st, "hc_TWT", [N2, 2, 128], F32)
                CS = 16
                Yin = SB(st, "hc_Yin", [A, 128, 128], F32)
                Y1t = SB(st, "hc_Y1t", [128, CS, 2, N2], F32)
                xs = [SB(st, f"hc_xs{i}", [128, 2, 512], F32) for i in range(2)]
                tq = [SB(st, f"hc_tq{i}", [128, 512], F32) for i in range(4)]
                hf = [SB(st, f"hc_hf{i}", [128, 2, 512], F32) for i in range(2)]
                Zt = SB(st, "hc_Zt", [128, 2, CS, N2], F32)
                Gt = SB(st, "hc_Gt", [N2, CS, 2, 128], F32)
                ot = [SB(st, f"hc_ot{i}", [A, 512], F32) for i in range(2)]
                V = lambda fn, r, w: P.op("dve", fn, reads=r, writes=w)
                P.dma("sp", F2f[:], hc["F2f"], writes=["hF2f"])
                P.dma("sp", TW[:], hc["TW"], writes=["hTW"])
                P.dma("sp", F1[:], hy_F1, writes=["hF1"])
                P.dma("sp", F2i[:], hc["F2i"], writes=["hF2i"])
                P.dma("sp", TWT[:], hc["TWT"], writes=["hTWT"])
                cgB = max(1, min(CS, 512 // N2))
                cpA = max(1, min(CS, 512 // 256))
                state = {"hi": 0, "oi": 0}
                F1a = SB(st, "hc_F1a", [128, 2, 128], F32)
                F1m = SB(st, "hc_F1m", [128, 2, 128], F32)
                V(lambda e: e.tensor_copy(out=F1a[:, 0, :], in_=F1[:, 0, :]), ["hF1"], ["hF1m"])
                V(lambda e: e.tensor_copy(out=F1a[:, 1, :], in_=F1[:, 2, :]), ["hF1"], ["hF1m"])
                V(lambda e: e.tensor_copy(out=F1m[:, 0, :], in_=F1[:, 1, :]), ["hF1"], ["hF1m"])
                V(lambda e: e.tensor_copy(out=F1m[:, 1, :], in_=F1[:, 0, :]), ["hF1"], ["hF1m"])
                for cb in range(4):
                    for c4 in range(0, 128, 32):
                        P.dma("sp", Yin[:, c4:c4 + 32, :],
                              src[cb * 128 + c4:cb * 128 + c4 + 32, toff:toff + n_].rearrange("c (a b) -> a c b", b=128),
                              reads=["hvT", "hy1T"], writes=["hfilt"])

                    def consume(c0, ncg, xre, xim, cb=cb):
                        j = state["hi"] % 2
                        state["hi"] += 1
                        ncol = ncg * N2
                        P.dma("sp", hf[j][:, 0, 0:ncol].rearrange("p (c k) -> p c k", k=N2), hc["fspec"][o, cb, :, 0, c0:c0 + ncg, :], reads=["fspec"], writes=[f"hhf{j}"])
                        P.dma("sp", hf[j][:, 1, 0:ncol].rearrange("p (c k) -> p c k", k=N2), hc["fspec"][o, cb, :, 1, c0:c0 + ncg, :], reads=["fspec"], writes=[f"hhf{j}"])
                        cl = c0 % CS
                        V(lambda e, j=j, ncol=ncol, xre=xre: e.tensor_tensor(out=tq[0][:, 0:ncol], in0=xre, in1=hf[j][:, 0, 0:ncol], op=ALU.mult), ["hxs", f"hhf{j}"], ["htq0"])
                        V(lambda e, j=j, ncol=ncol, xim=xim: e.tensor_tensor(out=tq[1][:, 0:ncol], in0=xim, in1=hf[j][:, 1, 0:ncol], op=ALU.mult), ["hxs", f"hhf{j}"], ["htq1"])
                        V(lambda e, j=j, ncol=ncol, xre=xre: e.tensor_tensor(out=tq[2][:, 0:ncol], in0=xre, in1=hf[j][:, 1, 0:ncol], op=ALU.mult), ["hxs", f"hhf{j}"], ["htq2"])
                        V(lambda e, j=j, ncol=ncol, xim=xim: e.tensor_tensor(out=tq[3][:, 0:ncol], in0=xim, in1=hf[j][:, 0, 0:ncol], op=ALU.mult), ["hxs", f"hhf{j}"], ["htq3"])
                        P.op("pool", lambda e, cl=cl, ncg=ncg, ncol=ncol: e.tensor_tensor(out=Zt[:, 0, cl:cl + ncg, :], in0=tq[0][:, 0:ncol].rearrange("p (c k) -> p c k", k=N2),
                                                                                       in1=tq[1][:, 0:ncol].rearrange("p (c k) -> p c k", k=N2), op=ALU.subtract),
                             reads=["htq0", "htq1"], writes=["hZt"])
                        P.op("pool", lambda e, cl=cl, ncg=ncg, ncol=ncol: e.tensor_tensor(out=Zt[:, 1, cl:cl + ncg, :], in0=tq[2][:, 0:ncol].rearrange("p (c k) -> p c k", k=N2),
                                                                                       in1=tq[3][:, 0:ncol].rearrange("p (c k) -> p c k", k=N2), op=ALU.add),
                             reads=["htq2", "htq3"], writes=["hZt"])
                        if cl + ncg < CS:
                            return
                        cs0 = c0 + ncg - CS
                        for g0 in range(0, CS, cpA):
                            pb = 3 + (g0 // cpA) % 2
                            for ci in range(cpA):
                                cc = g0 + ci
                                P.op("pe", lambda e, cc=cc, ci=ci, pb=pb: e.matmul(ps[pb][0:N2, ci * 256:(ci + 1) * 256], Zt[:, 0, cc, :],
                                                                                   F1a[:].rearrange("k r n -> k (r n)"), start=True, stop=False),
                                     reads=["hZt", "hF1m"], writes=[f"hq{pb}"])
                                P.op("pe", lambda e, cc=cc, ci=ci, pb=pb: e.matmul(ps[pb][0:N2, ci * 256:(ci + 1) * 256], Zt[:, 1, cc, :],
                                                                                   F1m[:].rearrange("k r n -> k (r n)"), start=False, stop=True),
                                     reads=["hZt", "hF1m"], writes=[f"hq{pb}"])
                            pv = ps[pb][0:N2, 0:cpA * 256].rearrange("p (c r b) -> p c r b", c=cpA, r=2)
                            twr = TWT[:, 0, :].unsqueeze(1).to_broadcast([N2, cpA, 128])
                            twi = TWT[:, 1, :].unsqueeze(1).to_broadcast([N2, cpA, 128])
                            u0, u1, u2, u3 = [tq[i][0:N2, 0:cpA * 128].rearrange("p (c b) -> p c b", b=128) for i in range(4)]
                            V(lambda e, pv=pv, twr=twr, u0=u0: e.tensor_tensor(out=u0, in0=pv[:, :, 0, :], in1=twr, op=ALU.mult), [f"hq{pb}", "hTWT"], ["htq0"])
                            V(lambda e, pv=pv, twi=twi, u1=u1: e.tensor_tensor(out=u1, in0=pv[:, :, 1, :], in1=twi, op=ALU.mult), [f"hq{pb}", "hTWT"], ["htq1"])
                            V(lambda e, pv=pv, twi=twi, u2=u2: e.tensor_tensor(out=u2, in0=pv[:, :, 0, :], in1=twi, op=ALU.mult), [f"hq{pb}", "hTWT"], ["htq2"])
                            V(lambda e, pv=pv, twr=twr, u3=u3: e.tensor_tensor(out=u3, in0=pv[:, :, 1, :], in1=twr, op=ALU.mult), [f"hq{pb}", "hTWT"], ["htq3"])
                            P.op("pool", lambda e, g0=g0, u0=u0, u1=u1: e.tensor_tensor(out=Gt[:, g0:g0 + cpA, 0, :], in0=u0, in1=u1, op=ALU.subtract),
                                 reads=["htq0", "htq1"], writes=["hGt"])
                            P.op("pool", lambda e, g0=g0, u2=u2, u3=u3: e.tensor_tensor(out=Gt[:, g0:g0 + cpA, 1, :], in0=u2, in1=u3, op=ALU.add),
                                 reads=["htq2", "htq3"], writes=["hGt"])
                        for g0 in range(0, CS, 4):
                            k_ = state["oi"] % 2
                            state["oi"] += 1
                            P.op("pe", lambda e, g0=g0, k_=k_: e.matmul(ps[1 + k_][0:A, 0:512], F2i[:, 0, :], Gt[:, g0:g0 + 4, 0, :], start=True, stop=False),
                                 reads=["hF2i", "hGt"], writes=[f"hq{1 + k_}"])
                            P.op("pe", lambda e, g0=g0, k_=k_: e.matmul(ps[1 + k_][0:A, 0:512], F2i[:, 1, :], Gt[:, g0:g0 + 4, 1, :], start=False, stop=True),
                                 reads=["hF2i", "hGt"], writes=[f"hq{1 + k_}"])
                            V(lambda e, k_=k_: e.tensor_copy(out=ot[k_][:], in_=ps[1 + k_][0:A, 0:512]), [f"hq{1 + k_}"], [f"hot{k_}"])
                            r0 = cb * 128 + cs0 + g0
                            P.dma("pool", hc1T[r0:r0 + 4, toff:toff + n_].rearrange("c (a b) -> a c b", b=128),
                                  ot[k_][:].rearrange("a (c b) -> a c b", b=128), reads=[f"hot{k_}"], writes=["hc1T"])

                    fwd_transform(Yin, A, N2, hc, F2f, TW, F1, Y1t, CS, xs, tq, consume, V)
                P.barrier()

        HH = nx // 4096
        GSCALE = 128 ** -0.5

        def chunk_view(ap2, ci):
            if ci[0] == "c":
                return ap2[:, nx + 64 * ci[1]: nx + 64 * ci[1] + 64]
            return ap2[:, 0:nx].rearrange("p (h r c) -> p h r c", h=HH, r=64, c=64)[:, ci[2], :, ci[1]]

        def gla_order(d):
            nq = nctx // 64
            cs = [("c", q) for q in range(nq)]
            xs = [("x", c, hh) for c in range(64) for hh in range(HH)]
            return (cs + xs) if d == 0 else (cs[::-1] + xs[::-1])

        def gla_stage(l):
            import os
            G_HEADS = int(os.environ.get("GLA_HEADS", 4)); G_DIRS = int(os.environ.get("GLA_DIRS", 2))
            G_CHUNKS = int(os.environ.get("GLA_CHUNKS", 100000)); G_STEPS = float(os.environ.get("GLA_STEPS", 6))
            G_FINISH = int(os.environ.get("GLA_FINISH", 1))
            with contextlib.ExitStack() as st:
                c64 = SB(st, "gl_c64", [64, 6, 64], F32)
                idb = SB(st, "gl_idb", [128, 128], BF16)
                ngt = SB(st, "gl_ng", [128, 2], F32)
                kb = SB(st, "gl_k", [128, T], BF16)
                qb = SB(st, "gl_q", [128, T], BF16)
                vb = SB(st, "gl_v", [128, 2, T], BF16)
                lrb = SB(st, "gl_lr", [17, T], BF16)
                acc = SB(st, "gl_acc", [128, 2, T], F32)
                stg = [SB(st, f"gl_stg{i}", [128, 1056], F32) for i in range(2)]
                wgf = SB(st, "gl_wgf", [17, 128], F32)
                wgb = SB(st, "gl_wgb", [17, 128], BF16)
                Sst = SB(st, "gl_S", [128, 256], F32)
                NBUF = 2
                et = [SB(st, f"gl_e{i}", [64, 128], F32) for i in range(NBUF)]
                spt = [SB(st, f"gl_sp{i}", [64, 128], F32) for i in range(NBUF)]
                Ep = [SB(st, f"gl_Ep{i}", [128, 64], F32) for i in range(NBUF)]
                Em = [SB(st, f"gl_Em{i}", [128, 64], F32) for i in range(NBUF)]
                Ekv = [SB(st, f"gl_Ekv{i}", [64, 128], F32) for i in range(NBUF)]
                kkv = [SB(st, f"gl_kkv{i}", [64, 128], F32) for i in range(NBUF)]
                vtk = [SB(st, f"gl_vt{i}", [64, 256], F32) for i in range(NBUF)]
                qin = [SB(st, f"gl_qi{i}", [128, 64], F32) for i in range(NBUF)]
                kout = [SB(st, f"gl_ko{i}", [128, 64], F32) for i in range(NBUF)]
                sTt = [SB(st, f"gl_sT{i}", [64, 64], F32) for i in range(NBUF)]
                P.dma("sp", c64[:], cst64, writes=["c64"])
                P.dma("sp", idb[:], identb_in, writes=["idb"])
                P.dma("sp", ngt[:], gla_ng[l], writes=["ngt"])
                si = [0]

                def load_rows(dst_ap_fn, row0, nrows, key):
                    for t0 in range(0, T, 1056):
                        n = min(1056, T - t0)
                        j = si[0] % 2
                        si[0] += 1
                        P.dma("sp", stg[j][0:nrows, 0:n], pxA[row0:row0 + nrows, t0:t0 + n], reads=["pxT"], writes=[f"glstg{j}"])
                        eng = ("act", "pool")[si[0] % 2]
                        if eng == "act":
                            P.op("act", lambda e, j=j, t0=t0, n=n: e.activation(out=dst_ap_fn(t0, n), in_=stg[j][0:nrows, 0:n], func=AF.Copy),
                                 reads=[f"glstg{j}"], writes=[key])
                        else:
                            P.op("pool", lambda e, j=j, t0=t0, n=n: e.tensor_copy(out=dst_ap_fn(t0, n), in_=stg[j][0:nrows, 0:n]),
                                 reads=[f"glstg{j}"], writes=[key])

                for h in range(G_HEADS):
                    load_rows(lambda t0, n: kb[:, t0:t0 + n], 512 + h * 128, 128, "glk")
                    load_rows(lambda t0, n: qb[:, t0:t0 + n], 2080 + h * 128, 128, "glq")
                    for half in range(2):
                        load_rows(lambda t0, n, half=half: vb[:, half, t0:t0 + n], 1024 + h * 256 + half * 128, 128, "glv")
                    for d in range(G_DIRS):
                        P.op("pool", lambda e: e.memset(lrb[:], 1.0), writes=["gllr"])
                        load_rows(lambda t0, n: lrb[0:16, t0:t0 + n], 2048 + d * 16, 16, "gllr")
                        P.dma("sp", wgf[:], gla_wg[l, d, :, h * 128:(h + 1) * 128], writes=["glwgf"])
                        P.op("dve", lambda e: e.tensor_copy(out=wgb[:], in_=wgf[:]), reads=["glwgf"], writes=["glwgb"])
                        P.op("pool", lambda e: e.memset(Sst[:], 0.0), writes=["glS"])
                        tri, triC, msk = c64[:, d, :], c64[:, 2 + d, :], c64[:, 4 + d, :]
                        last = 63 if d == 0 else 0
                        order = gla_order(d)[:G_CHUNKS]
                        for it, ci in enumerate(order):
                            if ci[0] == "x" and it == nctx // 64:
                                pass
                            b = it % NBUF
                            kbk = f"glb{b}"
                            lrc, kc_, qc_ = chunk_view(lrb[:], ci), chunk_view(kb[:], ci), chunk_view(qb[:], ci)
                            P.op("pe", lambda e, lrc=lrc: e.matmul(ps[0][0:64, 0:128], lrc, wgb[:], start=True, stop=True),
                                 reads=["gllr", "glwgb"], writes=["glp0"])
                            P.op("act", lambda e, b=b: e.activation(out=et[b][:], in_=ps[0][0:64, 0:128], func=AF.Exp, scale=-1.0),
                                 reads=["glp0"], writes=[f"gle{b}"])
                            P.op("act", lambda e, b=b: e.activation(out=spt[b][:], in_=et[b][:], func=AF.Ln, bias=ones[0:64, 0:1]),
                                 reads=[f"gle{b}", "ones"], writes=[f"glsp{b}"])
                            if G_STEPS < 2:
                                continue
                            P.op("pe", lambda e, b=b, tri=tri: e.matmul(ps[1][:, 0:64], spt[b][:], tri, start=True, stop=True),
                                 reads=[f"glsp{b}", "c64"], writes=["glp1"])
                            P.op("pe", lambda e, b=b, triC=triC: e.matmul(ps[2][0:64, 0:128], triC, spt[b][:], start=True, stop=True),
                                 reads=[f"glsp{b}", "c64"], writes=["glp2"])
                            P.op("act", lambda e, b=b: e.activation(out=Ep[b][:], in_=ps[1][:, 0:64], func=AF.Exp),
                                 reads=["glp1"], writes=[f"glEp{b}"])
                            P.op("act", lambda e, b=b: e.activation(out=Em[b][:], in_=ps[1][:, 0:64], func=AF.Exp, scale=-1.0),
                                 reads=["glp1"], writes=[f"glEm{b}"])
                            P.op("act", lambda e, b=b: e.activation(out=Ekv[b][:], in_=ps[2][0:64, 0:128], func=AF.Exp),
                                 reads=["glp2"], writes=[f"glEkv{b}"])
                            if G_STEPS < 3:
                                continue
                            P.op("pe", lambda e, kc_=kc_: e.matmul(ps[3][0:64, 0:128], kc_, idb[:], start=True, stop=True),
                                 reads=["glk", "idb"], writes=["glp3"])
                            for half in range(2):
                                vc_ = chunk_view(vb[:, half, :], ci)
                                P.op("pe", lambda e, vc_=vc_, half=half: e.matmul(ps[3][0:64, 128 + half * 128:256 + half * 128], vc_, idb[:],
                                                                                 start=True, stop=True),
                                     reads=["glv", "idb"], writes=["glp3"])
                            if G_STEPS < 3.2:
                                continue
                            P.op("dve", lambda e, b=b: e.tensor_tensor(out=kkv[b][:], in0=Ekv[b][:], in1=ps[3][0:64, 0:128], op=ALU.mult),
                                 reads=[f"glEkv{b}", "glp3"], writes=[f"glkkv{b}"])
                            if G_STEPS < 3.3:
                                continue
                            P.op("dve", lambda e, b=b: e.tensor_copy(out=vtk[b][:], in_=ps[3][0:64, 128:384]),
                                 reads=["glp3"], writes=[f"glvt{b}"])
                            if G_STEPS < 3.5:
                                continue
                            P.op("dve", lambda e, b=b, qc_=qc_: e.scalar_tensor_tensor(out=qin[b][:], in0=qc_, scalar=GSCALE, in1=Ep[b][:],
                                                                                   op0=ALU.mult, op1=ALU.mult),
                                 reads=["glq", f"glEp{b}"], writes=[f"glqi{b}"])
                            P.op("dve", lambda e, b=b, kc_=kc_: e.tensor_tensor(out=kout[b][:], in0=kc_, in1=Em[b][:], op=ALU.mult),
                                 reads=["glk", f"glEm{b}"], writes=[f"glko{b}"])
                            if G_STEPS < 4:
                                continue
                            P.op("pe", lambda e, b=b: e.matmul(ps[4][0:64, 0:64], kout[b][:], qin[b][:], start=True, stop=True),
                                 reads=[f"glko{b}", f"glqi{b}"], writes=["glp4"])
                            P.op("dve", lambda e, b=b, msk=msk: e.tensor_tensor(out=sTt[b][:], in0=msk, in1=ps[4][0:64, 0:64], op=ALU.mult),
                                 reads=["glp4", "c64"], writes=[f"glsT{b}"])
                            if G_STEPS < 5:
                                continue
                            for half in range(2):
                                P.op("pe", lambda e, b=b, half=half: e.matmul(ps[5][:, half * 64:(half + 1) * 64],
                                                                             vtk[b][:, half * 128:(half + 1) * 128], sTt[b][:],
                                                                             start=True, stop=False),
                                     reads=[f"glvt{b}", f"glsT{b}"], writes=["glp5"])
                                P.op("pe", lambda e, b=b, half=half: e.matmul(ps[5][:, half * 64:(half + 1) * 64],
                                                                             Sst[:, half * 128:(half + 1) * 128], qin[b][:],
                                                                             start=False, stop=True),
                                     reads=["glS", f"glqi{b}"], writes=["glp5"])
                            accv = chunk_view(acc[:, 0, :], ci), chunk_view(acc[:, 1, :], ci)
                            for half in range(2):
                                if d == 0:
                                    P.op("dve", lambda e, half=half, accv=accv: e.tensor_copy(out=accv[half], in_=ps[5][:, half * 64:(half + 1) * 64]),
                                         reads=["glp5"], writes=["glacc"])
                                else:
                                    P.op("dve", lambda e, half=half, accv=accv: e.tensor_tensor(out=accv[half], in0=accv[half],
                                                                                                in1=ps[5][:, half * 64:(half + 1) * 64], op=ALU.add),
                                         reads=["glp5", "glacc"], writes=["glacc"])
                            if G_STEPS < 6:
                                continue
                            P.op("pe", lambda e, b=b: e.matmul(ps[6][:, 0:256], kkv[b][:], vtk[b][:], start=True, stop=True),
                                 reads=[f"glkkv{b}", f"glvt{b}"], writes=["glp6"])
                            P.op("dve", lambda e, b=b, last=last: e.scalar_tensor_tensor(out=Sst[:], in0=Sst[:], scalar=Ep[b][:, last:last + 1],
                                                                             in1=ps[6][:, 0:256], op0=ALU.mult, op1=ALU.add),
                                 reads=["glS", f"glEp{b}", "glp6"], writes=["glS"])
                    for bi, (t0, nb, col, _, _) in enumerate(blks if G_FINISH else []):
                        sq = stg[0]
                        rr = stg[1]
                        for half in range(2):
                            P.op("act", lambda e, half=half, t0=t0, nb=nb: e.activation(out=sq[:, half * 512:half * 512 + nb],
                                                                                        in_=acc[:, half, t0:t0 + nb], func=AF.Square),
                                 reads=["glacc"], writes=["glstg0"])
                        for half in range(2):
                            P.op("pe", lambda e, half=half, nb=nb: e.matmul(ps[7][:, 0:nb], ones[:], sq[:, half * 512:half * 512 + nb],
                                                                           start=(half == 0), stop=(half == 1)),
                                 reads=["ones", "glstg0"], writes=["glp7"])
                        rsd = Ep[0]
                        P.op("act", lambda e, nb=nb: e.activation(out=sq[:, 0:nb], in_=ps[7][:, 0:nb], func=AF.Sqrt,
                                                                  bias=epst[:, 0:1], scale=1.0 / 256),
                             reads=["glp7", "epst"], writes=["glstg0"])
                        P.op("dve", lambda e, nb=nb: e.reciprocal(out=sq[:, 0:nb], in_=sq[:, 0:nb]), reads=["glstg0"], writes=["glstg0"])
                        for half in range(2):
                            r0 = 2592 + h * 256 + half * 128
                            P.dma("sp", rr[:, half * 512:half * 512 + nb], pxA[r0:r0 + 128, t0:t0 + nb], reads=["pxT"], writes=["glstg1"])
                        P.op("act", lambda e: e.activation(out=rr[:, 0:1024], in_=rr[:, 0:1024], func=AF.Silu),
                             reads=["glstg1"], writes=["glstg1"])
                        for half in range(2):
                            ob = kout[0]
                            P.op("dve", lambda e, half=half, t0=t0, nb=nb: e.scalar_tensor_tensor(
                                out=rr[:, half * 512:half * 512 + nb], in0=acc[:, half, t0:t0 + nb], scalar=ngt[:, half:half + 1],
                                in1=rr[:, half * 512:half * 512 + nb], op0=ALU.mult, op1=ALU.mult),
                                reads=["glacc", "ngt", "glstg1"], writes=["glstg1"])
                            P.op("pool", lambda e, half=half, nb=nb: e.tensor_tensor(out=vb[:, half, 0:nb], in0=rr[:, half * 512:half * 512 + nb],
                                                                                  in1=sq[:, 0:nb], op=ALU.mult),
                                 reads=["glstg1", "glstg0"], writes=["glv"])
                            P.dma("pool", yT[bi, :, 8 + 2 * h + half, 0:nb], vb[:, half, 0:nb], reads=["glv"], writes=["yT"])
                P.barrier()

        if only:
            for r0 in range(0, PSPLIT, 512):
                P.dma("pool", pxA[r0:r0 + 512], px_dbg[r0:r0 + 512], writes=["pxT"])
            P.barrier()
            if only == "gla":
                gla_stage(0)
            if only == "s5":
                s5_stage(0)
            if only == "hy":
                hyena_stage(0, True)
                fs_out = nc.dram_tensor("fs_out", [2, 4, 128, 2, 128, hyc["x"]["N2"]], F32, kind="ExternalOutput").ap()
                for o_ in range(2):
                    for cb_ in range(4):
                        P.dma("pool", fs_out[o_, cb_], hyc["x"]["fspec"][o_, cb_], reads=["fspec"], writes=["fs_out"])
            for bi in range(len(blks)):
                P.dma("pool", y_out[bi], yT[bi], reads=["yT"], writes=["y_out"])
            P.barrier()
        for l in range(0 if only else depth):
            mod_stage(l)
            castw(w_in[l], wb_in, MC_IN, KC, "wsrc")
            norm_stage(A1, 0)

            def in_extra(st):
                return [SB(st, f"ie_o{i}", [128, 4, 512], F32) for i in range(2)]

            def in_epi(ctx, bi, blk, m0, g, pbase, go=0, oj=None):
                t0, nb = blk[0], blk[1]
                if oj is None:
                    oj = (pbase // 4) % 2
                okey = f"ieo{oj}_{go}"
                for gi in range(g):
                    P.op("act", lambda e, gi=gi: e.activation(out=ctx[oj][:, go + gi, 0:nb], in_=ps[pbase + gi][:, 0:nb], func=AF.Copy),
                         reads=[f"psg{pbase}"], writes=[okey])
                dst = pxA[m0 * 128:(m0 + g) * 128] if m0 * 128 < PSPLIT else pxB[m0 * 128 - PSPLIT:(m0 + g) * 128 - PSPLIT]
                P.dma("pool", dst[:, t0:t0 + nb].rearrange("(g p) n -> p g n", p=128),
                      ctx[oj][:, go:go + g, 0:nb], reads=[okey], writes=["pxT"])

            gemm_stage(hT, "hT", KC, wb_in, MC_IN, in_epi, in_extra, pair=True)

            with contextlib.ExitStack() as st:
                if debug:
                    P.dma("pool", yT, y_dbg, writes=["yT"])
                else:
                    z = SB(st, "zfill", [128, KC, 512], BF16)
                    P.op("pool", lambda e: e.memset(z[:], 0.0), writes=["zf"])
                    for bi in range(len(blks)):
                        P.dma("pool", yT[bi], z[:], reads=["zf"], writes=["yT"])
                P.barrier()

            if not debug:
                s5_stage(l)
                hyena_stage(l, l < depth - 1 or depth < DEPTH)
                gla_stage(l)
            castw(w_br[l], wb_br, KC, KC, "wsrc")
            castw(w_out[l], wb_out, KC, KC, "wsrc2")

            with contextlib.ExitStack() as st:
                yb = [SB(st, f"mg_y{i}", [128, KC, 512], BF16) for i in range(2)]
                xb = [SB(st, f"mg_x{i}", [128, KC, 512], F32) for i in range(2)]
                mT = SB(st, "mg_m", [128, KC, 512], BF16)
                wt = [SB(st, f"mg_w{i}", [128, KC, 128], BF16) for i in range(3)]
                gt = [SB(st, f"mg_g{i}", [128, 3, 512], F32) for i in range(2)]
                t1 = [SB(st, f"mg_t{i}", [128, 512], F32) for i in range(2)]
                t2 = [SB(st, f"mg_u{i}", [128, 512], F32) for i in range(2)]
                wi = 0
                for bi, (t0, nb, col, _, _) in enumerate(blks):
                    j = bi % 2
                    P.dma("sp", yb[j][:], yT[bi], reads=["yT"], writes=[f"mgy{j}"])
                    P.dma("sp", xb[j][:, :, 0:nb], xT[:, t0:t0 + nb].rearrange("(c p) n -> p c n", p=128),
                          reads=["xT"], writes=[f"mgx{j}"])
                    for mi in range(KC):
                        wj = wi % 3
                        gj = wi % 2
                        wi += 1
                        P.dma("sp", wt[wj][:], wb_br[mi], reads=["wsrc"], writes=[f"mgw{wj}"])
                        for gi3 in range(3):
                            r0 = GATE0 + gi3 * D + mi * 128
                            srcg = pxA[r0:r0 + 128] if r0 < PSPLIT else pxB[r0 - PSPLIT:r0 - PSPLIT + 128]
                            P.dma("pool", gt[gj][:, gi3, 0:nb], srcg[:, t0:t0 + nb], reads=["pxT"], writes=[f"mgg{gj}"])
                        P.op("act", lambda e, gj=gj, nb=nb: e.activation(out=gt[gj][:, :, 0:nb], in_=gt[gj][:, :, 0:nb], func=AF.Sigmoid),
                             reads=[f"mgg{gj}"], writes=[f"mgg{gj}"])
                        for bri, (k0, k1) in enumerate(((0, 4), (4, 8), (8, 16))):
                            pi = bri
                            for kc in range(k0, k1):
                                P.op("pe", lambda e, wj=wj, j=j, kc=kc, pi=pi, nb=nb, k0=k0, k1=k1: e.matmul(
                                    ps[pi][:, 0:nb], wt[wj][:, kc, :], yb[j][:, kc, 0:nb], start=(kc == k0), stop=(kc == k1 - 1)),
                                    reads=[f"mgw{wj}", f"mgy{j}"], writes=[f"ps{pi}"])
                        P.op("dve", lambda e, gj=gj, nb=nb: e.tensor_tensor(out=t1[gj][:, 0:nb], in0=gt[gj][:, 0, 0:nb], in1=ps[0][:, 0:nb], op=ALU.mult),
                             reads=[f"mgg{gj}", "ps0"], writes=[f"mgt{gj}"])
                        P.op("dve", lambda e, gj=gj, nb=nb: e.tensor_tensor(out=t2[gj][:, 0:nb], in0=gt[gj][:, 1, 0:nb], in1=ps[1][:, 0:nb], op=ALU.mult),
                             reads=[f"mgg{gj}", "ps1"], writes=[f"mgu{gj}"])
                        P.op("pool", lambda e, gj=gj, nb=nb: e.tensor_tensor(out=t1[gj][:, 0:nb], in0=t1[gj][:, 0:nb], in1=t2[gj][:, 0:nb], op=ALU.add),
                             reads=[f"mgt{gj}", f"mgu{gj}"], writes=[f"mgt{gj}"])
                        P.op("dve", lambda e, gj=gj, nb=nb: e.tensor_tensor(out=t2[gj][:, 0:nb], in0=gt[gj][:, 2, 0:nb], in1=ps[2][:, 0:nb], op=ALU.mult),
                             reads=[f"mgg{gj}", "ps2"], writes=[f"mgu{gj}"])
                        P.op("pool", lambda e, gj=gj, nb=nb, mi=mi: e.tensor_tensor(out=mT[:, mi, 0:nb], in0=t1[gj][:, 0:nb], in1=t2[gj][:, 0:nb], op=ALU.add),
                             reads=[f"mgt{gj}", f"mgu{gj}"], writes=["mgm"])
                    for mo in range(KC):
                        wj = wi % 3
                        pi = 4 + (wi % 2)
                        wi += 1
                        P.dma("sp", wt[wj][:], wb_out[mo], reads=["wsrc2"], writes=[f"mgw{wj}"])
                        for kc in range(KC):
                            P.op("pe", lambda e, wj=wj, kc=kc, pi=pi, nb=nb: e.matmul(
                                ps[pi][:, 0:nb], wt[wj][:, kc, :], mT[:, kc, 0:nb], start=(kc == 0), stop=(kc == KC - 1)),
                                reads=[f"mgw{wj}", "mgm"], writes=[f"ps{pi}"])
                        P.op("dve", lambda e, j=j, mo=mo, pi=pi, nb=nb, col=col: e.scalar_tensor_tensor(
                            out=xb[j][:, mo, 0:nb], in0=ps[pi][:, 0:nb], scalar=modv[:, 2 * KC + mo, col:col + 1],
                            in1=xb[j][:, mo, 0:nb], op0=ALU.mult, op1=ALU.add),
                            reads=[f"ps{pi}", "modv", f"mgx{j}"], writes=[f"mgx{j}"])
                    P.dma("pool", xT[:, t0:t0 + nb].rearrange("(c p) n -> p c n", p=128), xb[j][:, :, 0:nb],
                          reads=[f"mgx{j}"], writes=["xT"])
                P.barrier()

            castw(w_up[l], wb_up, MC_UP, KC, "wsrc")
            norm_stage(A2, 3 * KC)

            def up_extra(st):
                return [SB(st, f"ue_o{i}", [128, 4, 512], F32) for i in range(2)]

            def up_epi(ctx, bi, blk, m0, g, pbase, go=0, oj=None):
                t0, nb = blk[0], blk[1]
                if oj is None:
                    oj = (pbase // 4) % 2
                okey = f"ueo{oj}_{go}"
                for gi in range(g):
                    P.op("act", lambda e, gi=gi: e.activation(out=ctx[oj][:, go + gi, 0:nb], in_=ps[pbase + gi][:, 0:nb], func=AF.Copy),
                         reads=[f"psg{pbase}"], writes=[okey])
                dst = aT[m0 * 128:(m0 + g) * 128] if m0 < KC_FF else bT[(m0 - KC_FF) * 128:(m0 - KC_FF + g) * 128]
                P.dma("pool", dst[:, t0:t0 + nb].rearrange("(g p) n -> p g n", p=128),
                      ctx[oj][:, go:go + g, 0:nb], reads=[okey], writes=["abT"])

            gemm_stage(hT, "hT", KC, wb_up, MC_UP, up_epi, up_extra, pair=True)

            with contextlib.ExitStack() as st:
                CG = 4
                aw = [SB(st, f"cv_a{i}", [128, CG, 514], F32) for i in range(2)]
                bw = [SB(st, f"cv_b{i}", [128, CG, 512], F32) for i in range(2)]
                ow = [SB(st, f"cv_o{i}", [128, CG, 512], F32) for i in range(2)]
                uw = [SB(st, f"cv_u{i}", [128, CG, 512], BF16) for i in range(2)]
                it = 0
                for bi, (t0, nb, col, le, re) in enumerate(blks):
                    for c0 in range(0, KC_FF, CG):
                        j = it % 2
                        it += 1
                        lo = t0 if le else t0 - 1
                        hi = t0 + nb if re else t0 + nb + 1
                        if le or re:
                            P.op("pool", lambda e, j=j: e.memset(aw[j][:], 0.0), writes=[f"cva{j}"])
                        P.dma("sp", aw[j][:, :, (lo - (t0 - 1)):(hi - (t0 - 1))],
                              aT[c0 * 128:(c0 + CG) * 128, lo:hi].rearrange("(g p) n -> p g n", p=128),
                              reads=["abT"], writes=[f"cva{j}"])
                        P.dma("sp", bw[j][:, :, 0:nb],
                              bT[c0 * 128:(c0 + CG) * 128, t0:t0 + nb].rearrange("(g p) n -> p g n", p=128),
                              reads=["abT"], writes=[f"cvb{j}"])
                        for gi in range(CG):
                            cc = c0 + gi
                            P.op("dve", lambda e, j=j, cc=cc, nb=nb, gi=gi: e.tensor_scalar(
                                out=ow[j][:, gi, 0:nb], in0=aw[j][:, gi, 0:nb], scalar1=cwt[:, cc, 0:1], scalar2=cbt[:, cc:cc + 1],
                                op0=ALU.mult, op1=ALU.add), reads=[f"cva{j}", "cwt", "cbt"], writes=[f"cvo{j}"])
                            P.op("dve", lambda e, j=j, cc=cc, nb=nb, gi=gi: e.scalar_tensor_tensor(
                                out=ow[j][:, gi, 0:nb], in0=aw[j][:, gi, 1:nb + 1], scalar=cwt[:, cc, 1:2], in1=ow[j][:, gi, 0:nb],
                                op0=ALU.mult, op1=ALU.add), reads=[f"cva{j}", "cwt", f"cvo{j}"], writes=[f"cvo{j}"])
                            P.op("dve", lambda e, j=j, cc=cc, nb=nb, gi=gi: e.scalar_tensor_tensor(
                                out=ow[j][:, gi, 0:nb], in0=aw[j][:, gi, 2:nb + 2], scalar=cwt[:, cc, 2:3], in1=ow[j][:, gi, 0:nb],
                                op0=ALU.mult, op1=ALU.add), reads=[f"cva{j}", "cwt", f"cvo{j}"], writes=[f"cvo{j}"])
                        P.op("act", lambda e, j=j, nb=nb: e.activation(out=ow[j][:, :, 0:nb], in_=ow[j][:, :, 0:nb], func=AF.Silu),
                             reads=[f"cvo{j}"], writes=[f"cvo{j}"])
                        P.op("pool", lambda e, j=j, nb=nb: e.tensor_tensor(out=uw[j][:, :, 0:nb], in0=ow[j][:, :, 0:nb], in1=bw[j][:, :, 0:nb], op=ALU.mult),
                             reads=[f"cvo{j}", f"cvb{j}"], writes=[f"cvu{j}"])
                        P.dma("pool", uT[bi, :, c0:c0 + CG, 0:nb], uw[j][:, :, 0:nb], reads=[f"cvu{j}"], writes=["uT"])
                P.barrier()

            castw(w_dn[l], wb_dn, KC, KC_FF, "wsrc")

            def dn_extra(st):
                return SB(st, "de_x", [128, KC, 512], F32)

            def dn_epi(ctx, bi, blk, m0, g, pbase):
                t0, nb, col = blk[0], blk[1], blk[2]
                if m0 == 0:
                    P.dma("pool", ctx[:, :, 0:nb], xT[:, t0:t0 + nb].rearrange("(c p) n -> p c n", p=128),
                          reads=["xT"], writes=["dex"])
                for gi in range(g):
                    m = m0 + gi
                    P.op("dve", lambda e, gi=gi, m=m: e.scalar_tensor_tensor(
                        out=ctx[:, m, 0:nb], in0=ps[pbase + gi][:, 0:nb], scalar=modv[:, 5 * KC + m, col:col + 1],
                        in1=ctx[:, m, 0:nb], op0=ALU.mult, op1=ALU.add),
                        reads=[f"psg{pbase}", "modv", "dex"], writes=["dex"])
                if m0 + g >= KC:
                    P.dma("pool", xT[:, t0:t0 + nb].rearrange("(c p) n -> p c n", p=128), ctx[:, :, 0:nb],
                          reads=["dex"], writes=["xT"])

            gemm_stage(uT, "uT", KC_FF, wb_dn, KC, dn_epi, dn_extra, G=2)

        if not only:
            norm_stage(None, 0, final=True)
        P.emit()
    return nc


def _tile_w(w, kc_n):
    K, M = w.shape
    return np.ascontiguousarray(w.reshape(kc_n, 128, M // 128, 128).transpose(2, 1, 0, 3))


def _vec(v):
    return np.ascontiguousarray(v.reshape(-1, 128).T)


def const_tables():
    j = np.arange(64)[:, None]
    i = np.arange(64)[None, :]
    c = -1.0 / 16.0
    t = np.stack([c * (j <= i), c * (j >= i), c * (j > i), c * (j < i), 1.0 * (j <= i), 1.0 * (j >= i)], axis=1)
    return np.ascontiguousarray(t.astype(np.float32))


def hy_tables(n):
    N = 2 * n
    N2, A = n // 64, n // 128
    tau = np.arange(N)
    pos = np.where(tau < n, tau, N - tau).astype(np.float64)
    t = pos / max(n - 1, 1)
    freqs = np.linspace(1e-4, 15, 16)
    ang = (2.0 * np.pi / n) * pos[:, None] * freqs[None]
    feats = np.concatenate([t[:, None], np.cos(ang), -np.sin(ang)], axis=1)
    rates = np.abs(np.linspace(np.log(1e-2) / 1.5, np.log(1e-2) / 0.3, 512))
    dec = np.exp(-t[:, None] * rates[None])
    dec[n] = 0.0
    dec = dec.reshape(N2, 128, 4, 128).transpose(2, 0, 3, 1)
    a = np.arange(N2)
    k1 = np.arange(128)
    th2 = 2 * np.pi * np.outer(a, a) / N2
    F2f = np.stack([np.cos(th2), -np.sin(th2)], axis=1)
    thw = 2 * np.pi * np.outer(k1, a) / N
    TW = np.stack([np.cos(thw), -np.sin(thw)], axis=1)
    thi = 2 * np.pi * np.outer(a, np.arange(A)) / N2
    F2i = np.stack([np.cos(thi) / N, -np.sin(thi) / N], axis=1)
    tht = 2 * np.pi * np.outer(a, k1) / N
    TWT = np.stack([np.cos(tht), np.sin(tht)], axis=1)
    msk = np.stack([(a < A), (a >= A)], axis=1)
    c = lambda v: np.ascontiguousarray(v.astype(np.float32))
    return dict(feats=c(feats.T), dec=c(dec), F2f=c(F2f), TW=c(TW), F2i=c(F2i), TWT=c(TWT), msk=c(msk))


def hy_f1_table():
    k = np.arange(128)
    th = 2 * np.pi * np.outer(k, k) / 128
    return np.ascontiguousarray(np.stack([np.cos(th), -np.sin(th), np.sin(th)], axis=1).astype(np.float32))


def hy_host_layout(inp, depth):
    f = lambda a: np.asarray(a, dtype=np.float32)
    out = {}
    out["hy_cw"] = np.stack([np.ascontiguousarray(f(inp["hy_conv_w"][l]).reshape(3, 12, 128).transpose(2, 1, 0)) for l in range(depth)])
    out["hy_cb"] = np.stack([np.ascontiguousarray(f(inp["hy_conv_b"][l]).reshape(12, 128).T) for l in range(depth)])
    out["hy_w1"] = np.ascontiguousarray(f(inp["hy_f_w1"])[:depth])
    out["hy_w2"] = np.ascontiguousarray(f(inp["hy_f_w2"])[:depth])
    w3 = np.concatenate([f(inp["hy_f_w3"])[:depth], f(inp["hy_f_b3"])[:depth, None, :]], axis=1)
    w3 = w3.reshape(depth, 65, 2, 2, 4, 128).transpose(0, 1, 2, 4, 3, 5).reshape(depth, 65, 2, 4, 256)
    out["hy_w3"] = np.ascontiguousarray(w3)
    fb = np.zeros((depth, 64, 6), np.float32)
    fb[:, :, 0] = f(inp["hy_f_freq1"])[:depth]
    fb[:, :, 1] = f(inp["hy_f_b1"])[:depth]
    fb[:, :, 2] = f(inp["hy_f_freq2"])[:depth]
    fb[:, :, 3] = f(inp["hy_f_b2"])[:depth]
    out["hy_fb"] = fb
    out["hy_bias"] = np.stack([np.ascontiguousarray(f(inp["hy_bias"][l]).reshape(2, 4, 128).transpose(2, 0, 1)) for l in range(depth)])
    return out


def s5_const_tables():
    a = np.arange(128)
    tri_f = (a[:, None] <= a[None, :]).astype(np.float32)
    tri_b = (a[:, None] >= a[None, :]).astype(np.float32)
    krow_f = np.broadcast_to(a[None, :], (128, 128)).astype(np.float32)
    krow_b = np.broadcast_to((127 - a)[None, :], (128, 128)).astype(np.float32)
    c128 = np.ascontiguousarray(np.stack([tri_f, tri_b, krow_f, krow_b], axis=1))
    ck = np.ascontiguousarray(np.stack([a, 127 - a, -a, -(127 - a)], axis=1).astype(np.float32))
    return c128, ck


def s5_host_layout(inp, depth):
    f = lambda a: np.asarray(a, dtype=np.float32)
    out = {}
    sp = lambda a: np.ascontiguousarray(a.reshape(depth, 2, 16, 2, 64).transpose(0, 3, 4, 1, 2).reshape(depth, 128, 2, 16))
    out["s5_are"] = sp(f(inp["s5_a_re"])[:depth])
    out["s5_aim"] = sp(f(inp["s5_a_im"])[:depth])
    ls = np.broadcast_to(f(inp["s5_log_step"])[:depth, :, :, None], (depth, 2, 32, 64))
    out["s5_lst"] = sp(np.ascontiguousarray(ls))
    def bblk(bm):
        o = np.zeros((depth, 2, 4, 128, 512), np.float32)
        b5 = bm.reshape(depth, 2, 4, 8, 64, 16)
        for gl in range(8):
            o[:, :, :, gl * 16:(gl + 1) * 16, gl * 64:(gl + 1) * 64] = b5[:, :, :, gl].transpose(0, 1, 2, 4, 3)
        return o
    out["s5_bre"] = bblk(f(inp["s5_b_re"])[:depth])
    out["s5_bim"] = bblk(f(inp["s5_b_im"])[:depth])
    def cblk(cm):
        o = np.zeros((depth, 2, 4, 128, 4, 128), np.float32)
        c6 = cm.reshape(depth, 2, 4, 4, 2, 16, 64)
        for s_ in range(4):
            for gi in range(2):
                ch0 = (2 * s_ + gi) * 16
                o[:, :, :, gi * 64:(gi + 1) * 64, s_, ch0:ch0 + 16] = c6[:, :, :, s_, gi].transpose(0, 1, 2, 4, 3)
        return o
    out["s5_cre"] = cblk(f(inp["s5_c_re"])[:depth])
    out["s5_cim"] = cblk(f(inp["s5_c_im"])[:depth])
    out["s5_dd"] = np.stack([np.ascontiguousarray(f(inp["s5_d"][l]).reshape(4, 128).T) for l in range(depth)])
    out["s5_gw"] = np.stack([np.ascontiguousarray(f(inp["s5_glu_w"][l]).reshape(4, 128, 4, 128).transpose(1, 2, 0, 3)) for l in range(depth)])
    out["s5_gb"] = np.stack([np.ascontiguousarray(f(inp["s5_glu_b"][l]).reshape(4, 128).T) for l in range(depth)])
    return out


def prep_inputs(inp, depth=DEPTH, nx=SEQ, nctx=CTX):
    f = lambda a: np.asarray(a, dtype=np.float32)
    KC = D // 128
    shared = {}
    shared["w_mod"] = np.stack([_tile_w(f(inp["w_mod"][l]), KC) for l in range(depth)])
    shared["b_mod"] = np.stack([_vec(f(inp["b_mod"][l])) for l in range(depth)])
    shared["n1g"] = np.stack([_vec(f(inp["norm1_g"][l])) for l in range(depth)])
    shared["n2g"] = np.stack([_vec(f(inp["norm2_g"][l])) for l in range(depth)])
    shared["fng"] = _vec(f(inp["final_norm_g"]))
    win = np.zeros((depth, D, IN_WP), np.float32)
    win[:, :, :C_MG] = f(inp["w_in"])[:depth, :, :C_MG]
    win[:, :, GATE0:] = f(inp["w_in"])[:depth, :, C_MG:]
    shared["w_in"] = np.stack([_tile_w(win[l], KC) for l in range(depth)])
    shared["w_br"] = np.stack([_tile_w(f(inp["w_branch"][l]), KC) for l in range(depth)])
    shared["w_out"] = np.stack([_tile_w(f(inp["w_out"][l]), KC) for l in range(depth)])
    shared["w_up"] = np.stack([_tile_w(f(inp["ff_w_up"][l]), KC) for l in range(depth)])
    shared["w_dn"] = np.stack([_tile_w(f(inp["ff_w_down"][l]), FF // 128) for l in range(depth)])
    shared["fcw"] = np.stack([np.ascontiguousarray(f(inp["ff_conv_w"][l]).reshape(3, FF // 128, 128).transpose(2, 1, 0))
                              for l in range(depth)])
    shared["fcb"] = np.stack([_vec(f(inp["ff_conv_b"][l])) for l in range(depth)])
    shared.update(s5_host_layout(inp, depth))
    shared.update(hy_host_layout(inp, depth))
    shared["hy_F1"] = hy_f1_table()
    for nm, n_ in (("x", nx), ("c", nctx)):
        for k_, v_ in hy_tables(n_).items():
            shared[f"hy_{k_}_{nm}"] = v_
    shared["cst128"], shared["cstk"] = s5_const_tables()
    wgaug = np.concatenate([f(inp["gla_wg"])[:depth], f(inp["gla_bg"])[:depth, :, None, :]], axis=2)
    shared["gla_wg"] = np.ascontiguousarray(wgaug)
    shared["gla_ng"] = np.stack([np.ascontiguousarray(f(inp["gla_norm_g"][l]).reshape(2, 128).T) for l in range(depth)])
    shared["cst64"] = const_tables()
    shared["identb"] = np.eye(128, dtype=np.float32).astype(ml_dtypes.bfloat16)
    maps = []
    for b in range(2):
        m = dict(shared)
        m["xT"] = np.ascontiguousarray(np.concatenate([f(inp["x"][b, :nx]).T, f(inp["ctx"][b, :nctx]).T], axis=1))
        cc = np.stack([f(inp["c"][b]), f(inp["c_ctx"])], axis=1)
        m["cT"] = np.ascontiguousarray(cc.reshape(KC, 128, 2).transpose(1, 0, 2))
        maps.append(m)
    return maps


def kernel(**inputs):
    nc = build()
    maps = prep_inputs(inputs)
    res = run_bass_kernel_spmd(nc, maps, core_ids=[0, 1])
    out = np.stack([np.ascontiguousarray(res.results[b]["outT"].T) for b in range(2)], axis=0)
    return out.astype(np.float32)
```
